# Optimizing a Trainium2 kernel written in Bass

```python
import math
import jax, jax.numpy as jnp
from jax import lax
import numpy as np

D_MODEL = 1024
BATCH = 4
SEQ = 4096
DEPTH = 2
DEC_BATCH = 32
DEC_SEQ = 8
PAST_LEN = 8192
PAGE_SIZE = 128

N_MIXERS = 2
N_ATTN_LAYERS = (DEPTH + N_MIXERS - 1) // N_MIXERS
N_SSM_LAYERS = DEPTH // N_MIXERS

ATT_HEADS = 16
ATT_HEAD_DIM = 64
ATT_WIDTH = ATT_HEADS * ATT_HEAD_DIM
DIL_GROUPS = ((128, 1), (512, 4), (2048, 16))
N_DIL = len(DIL_GROUPS)
BAND_BLOCK = 128
ATT_IN = 3 * N_DIL * ATT_WIDTH + ATT_WIDTH

SSM_D_INNER = 2 * D_MODEL
SSM_HEAD_DIM = 64
SSM_HEADS = SSM_D_INNER // SSM_HEAD_DIM
SSM_STATE = 128
SSM_GROUPS = 4
SSM_HEADS_PER_GROUP = SSM_HEADS // SSM_GROUPS
SSM_CONV = 4
SSM_CHUNK = 128
SSM_CONV_DIM = SSM_D_INNER + 2 * SSM_GROUPS * SSM_STATE
SSM_IN = 2 * SSM_D_INNER + 2 * SSM_GROUPS * SSM_STATE + SSM_HEADS

NORM_EPS = 1e-6
GATE_NORM_EPS = 1e-5

kernel_name = "dilated_swa_mamba2_hybrid_step"


def rms_norm(x, w, eps=NORM_EPS):
    xf = x.astype(jnp.float32)
    y = xf * lax.rsqrt(jnp.mean(xf * xf, axis=-1, keepdims=True) + eps)
    return (y * w.astype(jnp.float32)).astype(x.dtype)


def attn_project(x, norm_w, w_in, q_gain, k_gain):
    b, s, _ = x.shape
    h = rms_norm(x, norm_w)
    proj = h @ w_in
    qkv = proj[..., :3 * N_DIL * ATT_WIDTH].reshape(b, s, N_DIL, 3, ATT_HEADS, ATT_HEAD_DIM)
    gate = proj[..., 3 * N_DIL * ATT_WIDTH:]
    q = rms_norm(qkv[:, :, :, 0], q_gain[:, None, :]) * (ATT_HEAD_DIM ** -0.5)
    k = rms_norm(qkv[:, :, :, 1], k_gain[:, None, :])
    v = qkv[:, :, :, 2]
    return q, k, v, gate


def dilated_band_prompt(q, k, v, window, dilation):
    b, s, nh, dh = q.shape
    wn = window // dilation
    span = dilation * BAND_BLOCK
    sp = -(-s // span) * span
    nb = sp // span
    pad = ((0, 0), (0, sp - s), (0, 0), (0, 0))

    def to_blocks(t):
        t = jnp.pad(t.astype(jnp.float32), pad).reshape(b, sp // dilation, dilation, nh, dh)
        return t.transpose(0, 2, 1, 3, 4).reshape(b, dilation, nb, BAND_BLOCK, nh, dh)

    def with_prev(t):
        prev = jnp.pad(t, ((0, 0), (0, 0), (1, 0), (0, 0), (0, 0), (0, 0)))[:, :, :-1]
        return jnp.concatenate([prev, t], axis=3)

    qb, kb, vb = to_blocks(q), to_blocks(k), to_blocks(v)
    kk, vv = with_prev(kb), with_prev(vb)
    scores = jnp.einsum('brnqhe,brnkhe->brnhqk', qb, kk)
    blk = jnp.arange(nb)[:, None, None]
    qi = jnp.arange(BAND_BLOCK)[None, :, None]
    kj = jnp.arange(2 * BAND_BLOCK)[None, None, :]
    dist = BAND_BLOCK + qi - kj
    mask = (dist >= 0) & (dist <= wn) & ((blk - 1) * BAND_BLOCK + kj >= 0)
    scores = jnp.where(mask[None, None, :, None], scores, -jnp.inf)
    m = jnp.max(scores, axis=-1, keepdims=True)
    p = jnp.exp(scores - m)
    den = jnp.sum(p, axis=-1, keepdims=True)
    o = jnp.einsum('brnhqk,brnkhe->brnqhe', p / den, vv)
    lse = (m + jnp.log(den))[..., 0]
    o = o.reshape(b, dilation, sp // dilation, nh, dh).transpose(0, 2, 1, 3, 4).reshape(b, sp, nh, dh)[:, :s]
    lse = lse.transpose(0, 1, 2, 4, 3).reshape(b, dilation, sp // dilation, nh)
    lse = lse.transpose(0, 2, 1, 3).reshape(b, sp, nh)[:, :s]
    return o, lse


def dilated_gather_sample(q, k_all, v_all, buf_len, window, dilation):
    t = q.shape[1]
    wn = window // dilation
    idx = buf_len + jnp.arange(t)[:, None] - dilation * jnp.arange(wn + 1)[None, :]
    valid = idx >= 0
    idx = jnp.maximum(idx, 0)
    kg = k_all[:, idx].astype(jnp.float32)
    vg = v_all[:, idx].astype(jnp.float32)
    scores = jnp.einsum('bthe,btkhe->bthk', q.astype(jnp.float32), kg)
    scores = jnp.where(valid[None, :, None, :], scores, -jnp.inf)
    m = jnp.max(scores, axis=-1, keepdims=True)
    p = jnp.exp(scores - m)
    den = jnp.sum(p, axis=-1, keepdims=True)
    o = jnp.einsum('bthk,btkhe->bthe', p / den, vg)
    lse = (m + jnp.log(den))[..., 0]
    return o, lse


def attn_merge(outs, lses, gate, w_out):
    wts = jax.nn.softmax(lses, axis=0)
    o = jnp.einsum('gbsh,gbshe->bshe', wts, outs)
    b, s = o.shape[:2]
    o = o.reshape(b, s, ATT_WIDTH).astype(gate.dtype) * jax.nn.silu(gate)
    return o @ w_out


def attn_layer_prompt(x, norm_w, w_in, q_gain, k_gain, w_out):
    q, k, v, gate = attn_project(x, norm_w, w_in, q_gain, k_gain)
    s = x.shape[1]
    outs, lses, new_kv = [], [], []
    for g, (window, dilation) in enumerate(DIL_GROUPS):
        o, l = dilated_band_prompt(q[:, :, g], k[:, :, g], v[:, :, g], window, dilation)
        outs.append(o)
        lses.append(l)
        keep = min(window, s)
        new_kv.append(jnp.stack([k[:, s - keep:, g], v[:, s - keep:, g]], axis=2))
    y = x + attn_merge(jnp.stack(outs), jnp.stack(lses), gate, w_out).astype(x.dtype)
    return y, new_kv


def attn_layer_sample(x, kv_bufs, norm_w, w_in, q_gain, k_gain, w_out):
    q, k, v, gate = attn_project(x, norm_w, w_in, q_gain, k_gain)
    outs, lses, new_kv = [], [], []
    for g, (window, dilation) in enumerate(DIL_GROUPS):
        buf = kv_bufs[g]
        k_all = jnp.concatenate([buf[:, :, 0], k[:, :, g].astype(buf.dtype)], axis=1)
        v_all = jnp.concatenate([buf[:, :, 1], v[:, :, g].astype(buf.dtype)], axis=1)
        o, l = dilated_gather_sample(q[:, :, g], k_all, v_all, buf.shape[1], window, dilation)
        outs.append(o)
        lses.append(l)
        new_kv.append(jnp.stack([k[:, :, g], v[:, :, g]], axis=2))
    y = x + attn_merge(jnp.stack(outs), jnp.stack(lses), gate, w_out).astype(x.dtype)
    return y, new_kv


def causal_conv(xbc, conv_state, conv_w, conv_b):
    L = xbc.shape[1]
    xpad = jnp.concatenate([conv_state.astype(xbc.dtype), xbc], axis=1)
    out = conv_b + xpad[:, 0:L] * conv_w[0]
    for j in range(1, SSM_CONV):
        out = out + xpad[:, j:j + L] * conv_w[j]
    return jax.nn.silu(out), xpad[:, L:]


def ssd_chunked(x, dt, A, Bm, Cm, h0, chunk):
    f32 = jnp.float32
    b, L, nh, p = x.shape
    c = L // chunk
    G, R, N = SSM_GROUPS, SSM_HEADS_PER_GROUP, SSM_STATE
    x = x.astype(f32).reshape(b, c, chunk, G, R, p)
    dt = dt.astype(f32).reshape(b, c, chunk, G, R)
    Bm = Bm.astype(f32).reshape(b, c, chunk, G, N)
    Cm = Cm.astype(f32).reshape(b, c, chunk, G, N)
    a_cs = jnp.cumsum(dt * A.astype(f32).reshape(G, R), axis=2)
    xdt = x * dt[..., None]
    a_t = a_cs.transpose(0, 1, 3, 4, 2)
    seg = a_t[..., :, None] - a_t[..., None, :]
    causal = jnp.tril(jnp.ones((chunk, chunk), dtype=bool))
    decay = jnp.exp(jnp.where(causal, seg, -jnp.inf))
    cb = jnp.einsum('bclgn,bcsgn->bcgls', Cm, Bm)
    y_diag = jnp.einsum('bcgls,bcgrls,bcsgrp->bclgrp', cb, decay, xdt)
    decay_to_end = jnp.exp(a_cs[:, :, -1:] - a_cs)
    states = jnp.einsum('bclgn,bclgr,bclgrp->bcgrpn', Bm, decay_to_end, xdt)
    chunk_decay = jnp.exp(a_cs[:, :, -1])

    def step(h, inp):
        s_c, d_c = inp
        return h * d_c[..., None, None] + s_c, h

    h_init = h0.astype(f32).reshape(b, G, R, p, N)
    h_final, h_prev = lax.scan(step, h_init, (states.transpose(1, 0, 2, 3, 4, 5), chunk_decay.transpose(1, 0, 2, 3)))
    h_prev = h_prev.transpose(1, 0, 2, 3, 4, 5)
    y_off = jnp.einsum('bclgn,bcgrpn,bclgr->bclgrp', Cm, h_prev, jnp.exp(a_cs))
    y = (y_diag + y_off).reshape(b, L, nh, p)
    return y, h_final.reshape(b, nh, p, N)


def ssm_layer(x, conv_state, ssm_state, chunk, norm_w, w_in, conv_w, conv_b, dt_bias, A_log, D, gate_norm, w_out):
    f32 = jnp.float32
    b, L, _ = x.shape
    h = rms_norm(x, norm_w)
    proj = h @ w_in
    z = proj[..., :SSM_D_INNER]
    xbc = proj[..., SSM_D_INNER:SSM_D_INNER + SSM_CONV_DIM]
    dt = proj[..., SSM_D_INNER + SSM_CONV_DIM:]
    xbc, new_conv = causal_conv(xbc, conv_state, conv_w, conv_b)
    gn = SSM_GROUPS * SSM_STATE
    xs = xbc[..., :SSM_D_INNER].reshape(b, L, SSM_HEADS, SSM_HEAD_DIM)
    Bm = xbc[..., SSM_D_INNER:SSM_D_INNER + gn].reshape(b, L, SSM_GROUPS, SSM_STATE)
    Cm = xbc[..., SSM_D_INNER + gn:].reshape(b, L, SSM_GROUPS, SSM_STATE)
    dt = jax.nn.softplus(dt.astype(f32) + dt_bias.astype(f32))
    A = -jnp.exp(A_log.astype(f32))
    y, new_ssm = ssd_chunked(xs, dt, A, Bm, Cm, ssm_state, chunk)
    y = y + D.astype(f32)[:, None] * xs.astype(f32)
    y = y.reshape(b, L, SSM_D_INNER) * jax.nn.silu(z.astype(f32))
    yg = y.reshape(b, L, SSM_GROUPS, SSM_D_INNER // SSM_GROUPS)
    yg = yg * lax.rsqrt(jnp.mean(yg * yg, axis=-1, keepdims=True) + GATE_NORM_EPS)
    y = yg.reshape(b, L, SSM_D_INNER) * gate_norm.astype(f32)
    out = x + (y.astype(x.dtype) @ w_out).astype(x.dtype)
    return out, new_conv, new_ssm


def setup_inputs(seed: int = 0) -> dict:
    key = jax.random.key(seed)
    ks = jax.random.split(key, 24)
    f32 = jnp.float32

    def nrm(k, shape, scale):
        return scale * jax.random.normal(k, shape, f32)

    NA, NS = N_ATTN_LAYERS, N_SSM_LAYERS
    buf = [min(w, PAST_LEN) for (w, _) in DIL_GROUPS]
    x_prompt = nrm(ks[0], (BATCH, SEQ, D_MODEL), 1.0)
    x_sample = nrm(ks[1], (DEC_BATCH, DEC_SEQ, D_MODEL), 1.0)
    cache_kv_g0 = nrm(ks[2], (NA, DEC_BATCH, buf[0], 2, ATT_HEADS, ATT_HEAD_DIM), 1.0)
    cache_kv_g1 = nrm(ks[3], (NA, DEC_BATCH, buf[1], 2, ATT_HEADS, ATT_HEAD_DIM), 1.0)
    cache_kv_g2 = nrm(ks[4], (NA, DEC_BATCH, buf[2], 2, ATT_HEADS, ATT_HEAD_DIM), 1.0)
    state_conv = nrm(ks[5], (NS, DEC_BATCH, SSM_CONV - 1, SSM_CONV_DIM), 1.0)
    state_ssm = nrm(ks[6], (NS, DEC_BATCH, SSM_HEADS, SSM_HEAD_DIM, SSM_STATE), 0.1)
    attn_norm = 1.0 + nrm(ks[7], (NA, D_MODEL), 0.02)
    attn_w_in = nrm(ks[8], (NA, D_MODEL, ATT_IN), D_MODEL ** -0.5)
    attn_q_gain = 1.0 + nrm(ks[9], (NA, N_DIL, ATT_HEAD_DIM), 0.02)
    attn_k_gain = 1.0 + nrm(ks[10], (NA, N_DIL, ATT_HEAD_DIM), 0.02)
    attn_w_out = nrm(ks[11], (NA, ATT_WIDTH, D_MODEL), ATT_WIDTH ** -0.5)
    ssm_norm = 1.0 + nrm(ks[12], (NS, D_MODEL), 0.02)
    ssm_w_in = nrm(ks[13], (NS, D_MODEL, SSM_IN), D_MODEL ** -0.5)
    ssm_conv_w = nrm(ks[14], (NS, SSM_CONV, SSM_CONV_DIM), SSM_CONV ** -0.5)
    ssm_conv_b = nrm(ks[15], (NS, SSM_CONV_DIM), 0.02)
    dt0 = jnp.exp(jax.random.uniform(ks[16], (NS, SSM_HEADS), f32, math.log(1e-3), math.log(1e-1)))
    ssm_dt_bias = dt0 + jnp.log(-jnp.expm1(-dt0))
    ssm_A_log = jnp.log(jax.random.uniform(ks[17], (NS, SSM_HEADS), f32, 1.0, 16.0))
    ssm_D = 1.0 + nrm(ks[18], (NS, SSM_HEADS), 0.02)
    ssm_gate_norm = 1.0 + nrm(ks[19], (NS, SSM_D_INNER), 0.02)
    ssm_w_out = nrm(ks[20], (NS, SSM_D_INNER, D_MODEL), SSM_D_INNER ** -0.5)
    return {
        "x_prompt": x_prompt, "x_sample": x_sample,
        "cache_kv_g0": cache_kv_g0, "cache_kv_g1": cache_kv_g1, "cache_kv_g2": cache_kv_g2,
        "state_conv": state_conv, "state_ssm": state_ssm,
        "attn_norm": attn_norm, "attn_w_in": attn_w_in, "attn_q_gain": attn_q_gain,
        "attn_k_gain": attn_k_gain, "attn_w_out": attn_w_out,
        "ssm_norm": ssm_norm, "ssm_w_in": ssm_w_in, "ssm_conv_w": ssm_conv_w, "ssm_conv_b": ssm_conv_b,
        "ssm_dt_bias": ssm_dt_bias, "ssm_A_log": ssm_A_log, "ssm_D": ssm_D,
        "ssm_gate_norm": ssm_gate_norm, "ssm_w_out": ssm_w_out,
    }


def reference(x_prompt, x_sample, cache_kv_g0, cache_kv_g1, cache_kv_g2, state_conv, state_ssm,
              attn_norm, attn_w_in, attn_q_gain, attn_k_gain, attn_w_out,
              ssm_norm, ssm_w_in, ssm_conv_w, ssm_conv_b, ssm_dt_bias, ssm_A_log, ssm_D,
              ssm_gate_norm, ssm_w_out):
    kv_caches = (cache_kv_g0, cache_kv_g1, cache_kv_g2)
    xp, xs = x_prompt, x_sample
    kv_p = [[] for _ in range(N_DIL)]
    kv_s = [[] for _ in range(N_DIL)]
    conv_p, conv_s, ssm_p, ssm_s = [], [], [], []
    for i in range(DEPTH):
        j = i // N_MIXERS
        if i % N_MIXERS == 0:
            params = (attn_norm[j], attn_w_in[j], attn_q_gain[j], attn_k_gain[j], attn_w_out[j])
            xp, new_p = attn_layer_prompt(xp, *params)
            xs, new_s = attn_layer_sample(xs, [c[j] for c in kv_caches], *params)
            for g in range(N_DIL):
                kv_p[g].append(new_p[g])
                kv_s[g].append(new_s[g])
        else:
            params = (ssm_norm[j], ssm_w_in[j], ssm_conv_w[j], ssm_conv_b[j], ssm_dt_bias[j],
                      ssm_A_log[j], ssm_D[j], ssm_gate_norm[j], ssm_w_out[j])
            b = xp.shape[0]
            zero_conv = jnp.zeros((b, SSM_CONV - 1, SSM_CONV_DIM), xp.dtype)
            zero_ssm = jnp.zeros((b, SSM_HEADS, SSM_HEAD_DIM, SSM_STATE), jnp.float32)
            xp, c_p, h_p = ssm_layer(xp, zero_conv, zero_ssm, SSM_CHUNK, *params)
            xs, c_s, h_s = ssm_layer(xs, state_conv[j], state_ssm[j], xs.shape[1], *params)
            conv_p.append(c_p)
            conv_s.append(c_s)
            ssm_p.append(h_p)
            ssm_s.append(h_s)
    new_kv_g0_prompt = jnp.stack(kv_p[0])
    new_kv_g1_prompt = jnp.stack(kv_p[1])
    new_kv_g2_prompt = jnp.stack(kv_p[2])
    new_kv_g0_sample = jnp.stack(kv_s[0])
    new_kv_g1_sample = jnp.stack(kv_s[1])
    new_kv_g2_sample = jnp.stack(kv_s[2])
    new_conv_prompt = jnp.stack(conv_p)
    new_conv_sample = jnp.stack(conv_s)
    new_ssm_prompt = jnp.stack(ssm_p)
    new_ssm_sample = jnp.stack(ssm_s)
    return (xp, xs, new_kv_g0_prompt, new_kv_g1_prompt, new_kv_g2_prompt,
            new_kv_g0_sample, new_kv_g1_sample, new_kv_g2_sample,
            new_conv_prompt, new_conv_sample, new_ssm_prompt, new_ssm_sample)
```

```python
import os
import numpy as np
import ml_dtypes
from contextlib import ExitStack
import concourse.bass as bass
import concourse.mybir as mybir
from concourse.bass_utils import run_bass_kernel_spmd

F32 = mybir.dt.float32
BF16 = mybir.dt.bfloat16
ALU = mybir.AluOpType
AF = mybir.ActivationFunctionType

NTOK = 4096
NSAM = 32
NT = NTOK + NSAM
DM = 1024
GROUPS = ((128, 1), (512, 4), (2048, 16))
NDS = 24


class Buf:
    __slots__ = ("w", "r")

    def __init__(self):
        self.w = None
        self.r = {}


class Tile:
    def __init__(self, ap, buf=None, excl=False):
        self.ap = ap
        self.buf = buf if buf is not None else Buf()
        self.excl = excl


class Sched:
    def __init__(self, nc, es):
        self.nc = nc
        self.keys = ["pe", "act", "dve", "pool", "sp"]
        self.sem = {k: es.enter_context(nc.semaphore("s_" + k)) for k in ["pe", "act", "dve", "pool"]}
        self.cnt = {k: 0 for k in self.sem}
        self.dsem = [es.enter_context(nc.semaphore("d%d" % i)) for i in range(NDS)]
        self.es = es
        self.psem = []
        self.dcnt = [0] * NDS
        self.dnext = 0
        self.seen = {k: {} for k in self.keys}
        self.prog = {k: [] for k in self.keys}
        self.nwait = 0

    def _wait(self, ek, sk, val):
        if self.seen[ek].get(sk, 0) >= val:
            return
        self.seen[ek][sk] = val
        self.prog[ek].append(("w", sk, val))
        self.nwait += 1

    def op(self, ek, fn, reads=(), writes=(), dma=False):
        deps = {}
        for t in reads:
            b = t.buf
            if b.w is not None:
                deps[b.w[0]] = max(deps.get(b.w[0], 0), b.w[1])
            if t.excl:
                for sk, v in b.r.items():
                    if sk != ek:
                        deps[sk] = max(deps.get(sk, 0), v)
        for t in writes:
            b = t.buf
            if b.w is not None:
                deps[b.w[0]] = max(deps.get(b.w[0], 0), b.w[1])
            for sk, v in b.r.items():
                deps[sk] = max(deps.get(sk, 0), v)
        for sk, v in deps.items():
            if sk == ek and ek in ("pe", "pool"):
                continue
            self._wait(ek, sk, v)
        if dma and ek == "pool":
            self.psem.append(self.es.enter_context(self.nc.semaphore("p%d" % len(self.psem))))
            ticket = (("p", len(self.psem) - 1), 16)
        elif ek == "sp" or dma:
            i = self.dnext
            self.dnext = (i + 1) % NDS
            if self.dcnt[i] > 0:
                self._wait(ek, ("d", i), self.dcnt[i] * 16)
            self.dcnt[i] += 1
            ticket = (("d", i), self.dcnt[i] * 16)
        else:
            self.cnt[ek] += 1
            ticket = (ek, self.cnt[ek])
        self.prog[ek].append(("i", fn, ticket))
        for t in reads:
            r = t.buf.r
            r[ticket[0]] = max(r.get(ticket[0], 0), ticket[1])
        for t in writes:
            t.buf.w = ticket
            t.buf.r = {}
        return ticket

    def touch(self, ek, tiles):
        for t in tiles:
            b = t.buf
            if b.w is not None and not (b.w[0] == ek and ek == "pe"):
                self._wait(ek, b.w[0], b.w[1])
            for sk, v in b.r.items():
                self._wait(ek, sk, v)

    def barrier(self):
        for ek in self.keys:
            for k in ["pe", "act", "dve", "pool"]:
                if k != ek and self.cnt[k] > 0:
                    self._wait(ek, k, self.cnt[k])
            for i in range(NDS):
                if self.dcnt[i] > 0:
                    self._wait(ek, ("d", i), self.dcnt[i] * 16)
            for i in range(len(self.psem)):
                self._wait(ek, ("p", i), 16)

    def semof(self, sk):
        if isinstance(sk, str):
            return self.sem[sk]
        return self.dsem[sk[1]] if sk[0] == "d" else self.psem[sk[1]]

    def emit(self, block):
        def run(ek):
            def body(e):
                for it in self.prog[ek]:
                    if it[0] == "w":
                        e.wait_ge(self.semof(it[1]), it[2])
                    else:
                        ins = it[1](e)
                        sk, v = it[2]
                        ins.then_inc(self.semof(sk), 16 if not isinstance(sk, str) else 1)
            return body
        block.tensor(run("pe"))
        block.scalar(run("act"))
        block.vector(run("dve"))
        block.gpsimd(run("pool"))
        block.sync(run("sp"))


class Arena:
    def __init__(self, nc, es, nwords):
        self.t = es.enter_context(nc.sbuf_tensor("arena", [128, nwords], F32))
        self.n = nwords
        self.off = 0

    def alloc(self, nelem, dtype=F32):
        w = nelem if dtype == F32 else (nelem + 1) // 2
        assert self.off + w <= self.n, ("SBUF arena overflow", self.off, w, self.n)
        a = self.t[:, self.off:self.off + w]
        self.off += w
        return a if dtype == F32 else a.bitcast(BF16)

    def tile(self, nelem, dtype=F32):
        return Tile(self.alloc(nelem, dtype))


def host_consts():
    c = {}
    c["ident"] = np.eye(128, dtype=np.float32)
    k = np.arange(128)
    c["bones"] = (k[:, None] // 64 == k[None, :] // 64).astype(np.float32)
    cc = np.arange(256)
    m = np.zeros((128, 256), np.float32)
    m[:, :128] = (cc[None, :128] <= k[:, None])
    m[:, 128:] = ((cc[None, 128:] - 128) >= k[:, None])
    c["mask2"] = m
    c["ones"] = np.ones((128, 128), np.float32)
    c["tri"] = (k[:, None] <= k[None, :]).astype(np.float32)
    ms = np.zeros((128, 13, 16), np.float32)
    bi = 0
    for g, (W, d) in enumerate(GROUPS):
        nres = min(d, 8)
        for rp in range(nres):
            for t in range(8):
                ok = ((t - rp) % d == 0) & (rp + d * k >= t)
                ms[:, bi, t] = ok
                ms[:, bi, 8 + t] = ok
            bi += 1
    c["maskS"] = ms.reshape(128, 13 * 16)
    mn = np.zeros((32, 4, 3, 16), np.float32)
    for b in range(4):
        for g, (W, d) in enumerate(GROUPS):
            for u in range(8):
                for t in range(8):
                    ok = (u <= t) and ((t - u) % d == 0)
                    mn[b * 8 + u, b, g, t] = ok
                    mn[b * 8 + u, b, g, 8 + t] = ok
    mnp = np.zeros((128, 4 * 3 * 16), np.float32)
    mnp[:32] = mn.reshape(32, -1)
    c["maskN"] = mnp
    return c


CONST_ORDER = ["ident", "bones", "mask2", "ones", "maskS", "maskN", "tri"]


def pack_consts():
    c = host_consts()
    offs = {}
    cols = []
    o = 0
    for n in CONST_ORDER:
        offs[n] = (o, c[n].shape[1])
        o += c[n].shape[1]
        cols.append(c[n])
    return np.concatenate(cols, axis=1).astype(np.float32), offs


STAGE = int(os.environ.get("K_STAGE", "99"))
DEBUG = os.environ.get("K_DEBUG", "") != ""
SKIP = os.environ.get("K_SKIP", "").split(",")


def build():
    cpack, coffs = pack_consts()
    NCC = cpack.shape[1]
    nc = bass.Bass("TRN2", target_bir_lowering=False)

    def din(name, shape, dt=F32):
        return nc.dram_tensor(name, list(shape), dt, kind="ExternalInput").ap()

    def dout(name, shape, dt=F32):
        return nc.dram_tensor(name, list(shape), dt, kind="ExternalOutput").ap()

    def dscr(name, shape, dt=F32):
        return nc.dram_tensor(name, list(shape), dt, kind="Internal").ap()

    xp = din("xp", [NTOK, DM])
    xs = din("xs", [NSAM, DM])
    kvc = [din("kv%d" % g, [4, GROUPS[g][0], 2, 1024]) for g in range(3)]
    consts = din("consts", [128, NCC])
    a_normw = din("a_normw", [128, 8])
    a_win = din("a_win", [DM, 10240])
    a_gain = din("a_gain", [128, 6])
    a_wout = din("a_wout", [1024, DM])

    sconv = din("sconv", [4, 3, 3072])
    sssm = din("sssm", [4, 2048, 128])
    s_normw = din("s_normw", [128, 8])
    s_win = din("s_win", [DM, 5152])
    s_cw = din("s_cw", [128, 96])
    s_cb = din("s_cb", [128, 24])
    s_dtb = din("s_dtb", [128, 32])
    s_alog = din("s_alog", [128, 32])
    s_D = din("s_D", [128, 32])
    s_gn = din("s_gn", [128, 16])
    s_wout = din("s_wout", [2048, DM])
    x1d = dscr("x1d", [2048 + NSAM, DM])
    flag_in = din("flag", [128, 1])
    xbs = dscr("xbs", [16, 128, 20 * 128], BF16)
    dtsd = dscr("dtsd", [16, 128, 256])
    aTd = dscr("aTd", [16, 32, 128])
    cc_in = dscr("cc_in", [128, 2048 + 72])
    cc_out = dscr("cc_out", [256, 2048 + 72])
    nconv_p = dout("nconv_p", [3, 3072])
    nconv_s = dout("nconv_s", [4, 3, 3072])
    nssm_p = dout("nssm_p", [2048, 128])
    nssm_s = dout("nssm_s", [4, 2048, 128])
    y_p = dout("y_p", [2048, DM])
    y_s = dout("y_s", [NSAM, DM])
    nkv_p = [dout("nkv%d_p" % g, [GROUPS[g][0], 2, 1024]) for g in range(3)]
    nkv_s = [dout("nkv%d_s" % g, [NSAM, 2, 1024]) for g in range(3)]
    ogd = dscr("ogd", [8, 128, 2048], BF16)
    dbg = {}

    es = ExitStack()
    with es:
        S = Sched(nc, es)
        A = Arena(nc, es, int(os.environ.get('K_ARENA', '53200')))
        psum = es.enter_context(nc.psum_tensor("psum", [128, 4096], F32))
        banks = [Tile(psum[:, i * 512:(i + 1) * 512], excl=True) for i in range(8)]
        roles = {"mm": [0, 1, 2, 3], "pv": [4, 5], "aux": [6, 7]}
        rr = {k: 0 for k in roles}

        def PS(role):
            lst = roles[role]
            b = banks[lst[rr[role] % len(lst)]]
            rr[role] += 1
            return b

        def dma(out, in_, reads=(), writes=()):
            return S.op("sp", lambda e: e.dma_start(out=out, in_=in_), reads=reads, writes=writes)

        def mm(out, lhsT, rhs, start, stop, reads, writes):
            return S.op("pe", lambda e: e.matmul(out, lhsT=lhsT, rhs=rhs, start=start, stop=stop), reads=reads, writes=writes)

        def tr(out, in_, ident, reads, writes):
            return S.op("pe", lambda e: e.transpose(out=out, in_=in_, identity=ident), reads=reads, writes=writes)

        def act(out, in_, func, reads, writes, bias=None, scale=None):
            kw = {}
            if bias is not None:
                kw["bias"] = bias
            if scale is not None:
                kw["scale"] = scale
            return S.op("act", lambda e: e.activation(out=out, in_=in_, func=func, **kw), reads=reads, writes=writes)

        def tt(ek, out, in0, in1, op, reads, writes):
            return S.op(ek, lambda e: e.tensor_tensor(out=out, in0=in0, in1=in1, op=op), reads=reads, writes=writes)

        def ts(ek, out, in0, s1, op0, reads, writes, s2=None, op1=None):
            if op1 is None:
                return S.op(ek, lambda e: e.tensor_scalar(out=out, in0=in0, scalar1=s1, scalar2=None, op0=op0), reads=reads, writes=writes)
            return S.op(ek, lambda e: e.tensor_scalar(out=out, in0=in0, scalar1=s1, scalar2=s2, op0=op0, op1=op1), reads=reads, writes=writes)

        def stt(out, in0, scalar, in1, op0, op1, reads, writes):
            return S.op("dve", lambda e: e.scalar_tensor_tensor(out=out, in0=in0, scalar=scalar, in1=in1, op0=op0, op1=op1), reads=reads, writes=writes)

        def cp(ek, out, in_, reads, writes):
            return S.op(ek, lambda e: e.tensor_copy(out=out, in_=in_), reads=reads, writes=writes)

        def mset(ek, ap, val, writes):
            return S.op(ek, lambda e: e.memset(ap, val), writes=writes)

        def dump(name, ap, shape, tiles, dt=F32):
            d = dout("dbg_" + name, shape, dt)
            dbg[name] = d
            dma(d, ap, reads=tiles)

        def v3(ap, **kw):
            k = list(kw.keys())[0]
            return ap.rearrange("p (a b) -> p a b", a=kw[k])

        cst = A.tile(NCC)
        dma(cst.ap, consts, writes=[cst])

        def cview(n, rows=128):
            o, w = coffs[n]
            return cst.ap[:rows, o:o + w]

        identf = cview("ident")
        normw = A.tile(8)
        dma(normw.ap, a_normw, writes=[normw])
        gcol = A.tile(6)
        dma(gcol.ap, a_gain, writes=[gcol])
        for g in range(3):
            ts("dve", gcol.ap[:, 2 * g:2 * g + 1], gcol.ap[:, 2 * g:2 * g + 1], 0.125, ALU.mult, [gcol], [gcol])
        eps6 = A.tile(1)
        mset("dve", eps6.ap, 1e-6, [eps6])
        cb = A.tile(128 + 256 + 128 + 13 * 16 + 192 + 128, BF16)
        bones_b = cb.ap[:, 0:128]
        mask2_b = cb.ap[:, 128:384]
        ones_b = cb.ap[:, 384:512]
        maskS_b = cb.ap[:, 512:512 + 208]
        maskN_b = cb.ap[:, 720:720 + 192]
        identb = cb.ap[:, 912:912 + 128]
        for dst, nm in ((bones_b, "bones"), (mask2_b, "mask2"), (ones_b, "ones"), (maskS_b, "maskS"), (maskN_b, "maskN"), (identb, "ident")):
            cp("dve", dst, cview(nm), [cst], [cb])

        flagt = A.tile(1)
        dma(flagt.ap, flag_in, writes=[flagt])
        hTb = [Tile(None) for _ in range(9)]

        def hchunks(t0, t1):
            return [hTb[c] for c in range(t0 // 512, min((t1 - 1) // 512, 8) + 1)]

        qs_all = A.tile(8 * 3 * 32, BF16)
        ks_all = A.tile(8 * 3 * 32, BF16)
        vs_all = A.tile(8 * 3 * 128, BF16)
        sgs = A.tile(8 * 32)
        qs4 = qs_all.ap.rearrange("p (h g t) -> p h g t", h=8, g=3)
        ks4 = ks_all.ap.rearrange("p (h g t) -> p h g t", h=8, g=3)
        vs4 = vs_all.ap.rearrange("p (h g c) -> p h g c", h=8, g=3)
        sgs3 = sgs.ap.rearrange("p (h t) -> p h t", h=8)
        ogs = A.tile(8 * 32, BF16)
        ogs3 = ogs.ap.rearrange("p (h t) -> p h t", h=8)
        hmark = A.off
        hT = A.alloc(8 * NT, BF16)
        hT3 = hT.rearrange("p (k t) -> p k t", k=8)
        p0mark = A.off
        xbuf = [A.tile(1024) for _ in range(2)]
        xn_l = [A.tile(1024) for _ in range(2)]
        junk_l = [A.tile(1024) for _ in range(2)]
        small_l = [A.tile(8) for _ in range(2)]

        def ttr(out, in0, in1, accum, reads, writes):
            return S.op("dve", lambda e: e.tensor_tensor_reduce(out=out, in0=in0, in1=in1, scale=1.0, scalar=0.0,
                                                                op0=ALU.mult, op1=ALU.add, accum_out=accum), reads=reads, writes=writes)

        roles["aux"] = [0, 1, 2, 3, 4, 5, 6, 7]

        def p0_stageA(blk):
            rows = 128 if blk < 32 else 32
            src = xp[blk * 128:(blk + 1) * 128, :] if blk < 32 else xs
            xb = xbuf[blk % 2]
            xn, junk, small = xn_l[blk % 2], junk_l[blk % 2], small_l[blk % 2]
            dma(xb.ap[:rows], src, writes=[xb])
            tt("dve", junk.ap[:rows], xb.ap[:rows], xb.ap[:rows], ALU.mult, [xb], [junk])
            S.op("dve", lambda e, rows=rows, small=small, junk=junk: e.tensor_reduce(out=small.ap[:rows, 0:1], in_=junk.ap[:rows], axis=mybir.AxisListType.X, op=ALU.add),
                 reads=[junk], writes=[small])
            act(small.ap[:rows, 1:2], small.ap[:rows, 0:1], AF.Ln, [small, eps6], [small], bias=eps6.ap[:rows], scale=1.0 / 1024)
            act(small.ap[:rows, 2:3], small.ap[:rows, 1:2], AF.Exp, [small], [small], scale=-0.5)
            act(xn.ap[:rows], xb.ap[:rows], AF.Copy, [xb, small], [xn], scale=small.ap[:rows, 2:3])

        def p0_stageB(blk):
            rows = 128 if blk < 32 else 32
            xn = xn_l[blk % 2]
            tok0 = blk * 128
            for half in range(2):
                pb = PS("aux")
                for k4 in range(4):
                    kc = half * 4 + k4
                    tr(pb.ap[:, k4 * 128:k4 * 128 + rows], xn.ap[:rows, kc * 128:(kc + 1) * 128], identf[:rows, :rows], [xn, cst], [pb])
                tt("dve", hT3[:, half * 4:half * 4 + 4, tok0:tok0 + rows], v3(pb.ap, a=4)[:, :, :rows],
                   normw.ap[:, half * 4:half * 4 + 4].unsqueeze(2).broadcast_to([128, 4, rows]), ALU.mult, [pb, normw], [hTb[blk // 4]])

        p0_stageA(0)
        for blk in range(33):
            if blk + 1 < 33:
                p0_stageA(blk + 1)
            p0_stageB(blk)
        if DEBUG:
            dump("hT", hT, [128, 8 * NT], hTb, BF16)

        roles["aux"] = [6, 7]
        S.barrier()
        A.off = p0mark
        wst = [A.tile(8 * 384) for _ in range(2)]
        wbf = [A.tile(8 * 384, BF16) for _ in range(2)]
        qn = A.tile(NT, BF16)
        kn = A.tile(NT, BF16)
        Vaug = A.tile(33 * 2 * 128, BF16)
        Vaug4 = Vaug.ap.rearrange("p (u h c) -> p u h c", u=33, h=2)
        mset("pool", Vaug.ap, 1.0, [Vaug])
        mpair = A.tile(3 * 512, BF16)
        mp3 = mpair.ap.rearrange("p (v c) -> p v c", v=3)
        for v_ in range(3):
            for hlf in range(2):
                cp("dve", mp3[:, v_, hlf * 256:(hlf + 1) * 256], cview("mask2"), [cst], [mpair])
        for v_, hlf in ((1, 0), (2, 0), (2, 1)):
            ts("dve", mp3[:, v_, hlf * 256:hlf * 256 + 128], mp3[:, v_, hlf * 256:hlf * 256 + 128], flagt.ap[:, 0:1], ALU.mult, [mpair, flagt], [mpair])
        U = [A.tile(2048) for _ in range(2)]
        sq = [A.tile(512, BF16) for _ in range(2)]
        rs = [A.tile(512) for _ in range(2)]
        lnv = rs
        kf = [A.tile(512) for _ in range(2)]
        kt = [A.tile(512)] * 2
        vf = [A.tile(512) for _ in range(2)]
        Pb = [A.tile(512, BF16) for _ in range(3)]
        sg = vf
        numt = kf
        dent = kt
        ogc = [A.tile(512, BF16) for _ in range(2)]
        cnt = {"w": 0, "t": 0, "p": 0}

        def w3(t):
            return t.ap.rearrange("p (k c) -> p k c", k=8)

        def inproj_fm(wt, c0, tc, ps):
            N = 512 if tc < 8 else 32
            t0 = tc * 512
            wv = wt.ap.rearrange("p (k c) -> p k c", k=8)
            for kc in range(8):
                mm(ps.ap[:, :N], wv[:, kc, c0:c0 + 128], hT3[:, kc, t0:t0 + N], kc == 0, kc == 7, [wt, hTb[tc]], [ps])
            return N, t0

        def unit(u, d):
            sp_, r_ = (u, 0) if d == 1 else (u // d, u % d)
            return sp_, r_, sp_ * 128 * d + r_

        def sl(st, d):
            return slice(st, st + 127 * d + 1, d)

        npairs = 8 if STAGE >= 2 else (1 if STAGE >= 0 else 0)
        wbg = A.tile(8 * 128, BF16)

        def run_group(hp, g, W, d, wb, mid_hook):
            span = 128 * d
            def qk_stage1(s_, tc):
                ps = PS("mm")
                N, t0 = inproj_fm(wb, s_ * 128, tc, ps)
                i2 = cnt["t"] % 2
                cnt["t"] += 1
                act(sq[i2].ap[:, :N], ps.ap[:, :N], AF.Square, [ps], [sq[i2]])
                return (s_, tc, ps, N, t0, i2)

            def qk_stage2(st8):
                s_, tc, ps, N, t0, i2 = st8
                dst = qn if s_ == 0 else kn
                p2 = PS("aux")
                mm(p2.ap[:, :N], bones_b, sq[i2].ap[:, :N], True, True, [sq[i2], cb], [p2])
                act(lnv[i2].ap[:, :N], p2.ap[:, :N], AF.Ln, [p2, eps6], [lnv[i2]], bias=eps6.ap, scale=1.0 / 64)
                act(rs[i2].ap[:, :N], lnv[i2].ap[:, :N], AF.Exp, [lnv[i2]], [rs[i2]], scale=-0.5)
                gc = gcol.ap[:, 2 * g + s_:2 * g + s_ + 1]
                stt(dst.ap[:, t0:t0 + N], ps.ap[:, :N], gc, rs[i2].ap[:, :N], ALU.mult, ALU.mult, [ps, rs[i2], gcol], [dst])
                need = (s_ == 1) and (tc == 8 or (t0 + N > NTOK - W)) and ('kout' not in SKIP)
                if need:
                    stt(kf[i2].ap[:, :N], ps.ap[:, :N], gc, rs[i2].ap[:, :N], ALU.mult, ALU.mult, [ps, rs[i2], gcol], [kf[i2]])
                    p3 = PS("aux")
                    nb = (N + 127) // 128
                    for j in range(nb):
                        rws = min(128, N - j * 128)
                        tr(p3.ap[:rws, j * 128:(j + 1) * 128], kf[i2].ap[:, j * 128:j * 128 + rws], identf, [kf[i2], cst], [p3])
                    rmax = min(128, N)
                    act(kt[i2].ap[:rmax, :nb * 128], p3.ap[:rmax, :nb * 128], AF.Copy, [p3], [kt[i2]])
                    if tc == 8:
                        dma(nkv_s[g][:, 0, hp * 128:(hp + 1) * 128], kt[i2].ap[:32, 0:128], reads=[kt[i2]])
                    else:
                        for j in range(nb):
                            tk = t0 + j * 128
                            if tk >= NTOK - W:
                                r0 = tk - (NTOK - W)
                                dma(nkv_p[g][r0:r0 + 128, 0, hp * 128:(hp + 1) * 128], kt[i2].ap[:, j * 128:(j + 1) * 128], reads=[kt[i2]])

            pend = None
            kc_min = 0 if g == 2 else 3
            for s_ in range(2):
                for tc in (range(4, 9) if s_ == 0 else range(kc_min, 9)):
                    cur = qk_stage1(s_, tc)
                    if pend is not None:
                        qk_stage2(pend)
                    pend = cur
            qk_stage2(pend)
            cp("pool", qs4[:, hp, g, :], qn.ap[:, NTOK:NT], [qn], [qs_all])
            cp("pool", ks4[:, hp, g, :], kn.ap[:, NTOK:NT], [kn], [ks_all])
            for u0 in range(0 if g == 2 else 12, 36, 4):
                ps = PS("mm")
                us = [u for u in range(u0, u0 + 4) if u <= 32]
                for j, u in enumerate(us):
                    for kc in range(8):
                        if u < 32:
                            sp_, r_, st = unit(u, d)
                            mm(ps.ap[:, j * 128:(j + 1) * 128], hT3[:, kc, sl(st, d)], w3(wb)[:, kc, 256:384], kc == 0, kc == 7,
                               [wb] + hchunks(st, st + 127 * d + 1), [ps])
                        else:
                            mm(ps.ap[:32, j * 128:(j + 1) * 128], hT3[:, kc, NTOK:NT], w3(wb)[:, kc, 256:384], kc == 0, kc == 7,
                               [wb, hTb[8]], [ps])
                if us[0] < 32:
                    n = len(us)
                    for j in range(n):
                        cp("dve", Vaug4[:, u0 + j, :, 0:64], ps.ap[:, j * 128:(j + 1) * 128].rearrange("p (h c) -> p h c", h=2), [ps], [Vaug])
                    outs = [(j, u) for j, u in enumerate(us) if unit(u, d)[2] >= NTOK - W]
                    if outs and 'vout' not in SKIP:
                        i2 = cnt["t"] % 2
                        cnt["t"] += 1
                        cp("dve", vf[i2].ap, ps.ap, [ps], [vf[i2]])
                        for j, u in outs:
                            sp_, r_, st = unit(u, d)
                            rel = st - (NTOK - W)
                            dma(nkv_p[g][sl(rel, d), 1, hp * 128:(hp + 1) * 128], vf[i2].ap[:, j * 128:(j + 1) * 128], reads=[vf[i2]])
                else:
                    act(Vaug4[:32, 32, :, 0:64], ps.ap[:32, 0:128].rearrange("p (h c) -> p h c", h=2), AF.Copy, [ps], [Vaug])
                    i2 = cnt["t"] % 2
                    cnt["t"] += 1
                    cp("dve", vf[i2].ap[:32, 0:128], ps.ap[:32, 0:128], [ps], [vf[i2]])
                    dma(nkv_s[g][:, 1, hp * 128:(hp + 1) * 128], vf[i2].ap[:32, 0:128], reads=[vf[i2]])
                    cp("pool", vs4[:32, hp, g, :].rearrange("p (h c) -> p h c", h=2), Vaug4[:32, 32, :, 0:64], [Vaug], [vs_all])
            if DEBUG and hp == 0 and g == 1:
                dump("qn", qn.ap, [128, NT], [qn], BF16)
                dump("kn", kn.ap, [128, NT], [kn], BF16)
                dump("Vaug", Vaug.ap, [128, 33 * 256], [Vaug], BF16)
            if STAGE < 1:
                return
            def att_scores(hh, uu):
                hs = slice(64 * hh, 64 * hh + 64)
                psS = PS("mm")
                P = Pb[cnt["p"] % 3]
                cnt["p"] += 1
                for j in range(2):
                    u = uu + j
                    sp_, r_, st = unit(u, d)
                    qap = qn.ap[hs, sl(st, d)]
                    if sp_ > 0:
                        mm(psS.ap[:, j * 256:j * 256 + 128], kn.ap[hs, sl(st - span, d)], qap, True, True, [qn, kn], [psS])
                    mm(psS.ap[:, j * 256 + 128:j * 256 + 256], kn.ap[hs, sl(st, d)], qap, True, True, [qn, kn], [psS])
                act(P.ap, psS.ap, AF.Exp, [psS], [P])
                if g == 2 or (g == 1 and uu < 20):
                    mv = 2
                elif g == 0 and uu == 16:
                    mv = 1
                else:
                    mv = 0
                tt("pool", P.ap, P.ap, mp3[:, mv, :], ALU.mult, [P, mpair], [P])
                return P

            def att_pv(hh, u0, uu, P, po):
                for j in range(2):
                    u = uu + j
                    sp_, r_, st = unit(u, d)
                    jj = u - u0
                    if sp_ > 0:
                        mm(po.ap[:, jj * 128:(jj + 1) * 128], Vaug4[:, u - d, hh, :], P.ap[:, j * 256:j * 256 + 128], True, False, [Vaug, P], [po])
                    mm(po.ap[:, jj * 128:(jj + 1) * 128], Vaug4[:, u, hh, :], P.ap[:, j * 256 + 128:j * 256 + 256], sp_ == 0, True, [Vaug, P], [po])

            def att_evac(hh, u0, po):
                Uh = U[hh]
                if g == 0:
                    cp("dve", Uh.ap[:, u0 * 128 - 2048:(u0 + 4) * 128 - 2048], po.ap, [po], [Uh])
                else:
                    sp_, r0, _ = unit(u0, d)
                    uv = Uh.ap[:, sp_ * span - 2048:(sp_ + 1) * span - 2048].rearrange("p (i r) -> p r i", r=d)[:, r0:r0 + 4, :]
                    tt("dve", uv, v3(po.ap, a=4), uv, ALU.add, [po, Uh], [Uh])

            tasks = [(hh, u0, uu) for hh in range(2) for u0 in range(16, 32, 4) for uu in (u0, u0 + 2)]
            pend = None
            po = None
            for ti_, tk_ in enumerate(tasks + [None]):
                if mid_hook and ti_ % 2 == 0:
                    mid_hook.pop(0)()
                curP = att_scores(tk_[0], tk_[2]) if tk_ is not None else None
                if pend is not None:
                    ph, pu0, puu, pP = pend
                    if puu == pu0:
                        po = PS("pv")
                    att_pv(ph, pu0, puu, pP, po)
                    if puu == pu0 + 2:
                        att_evac(ph, pu0, po)
                pend = (tk_[0], tk_[1], tk_[2], curP) if tk_ is not None else None

        def finalize_steps(hp):
            return [(lambda tc=tc: finalize_chunk(hp, tc)) for tc in range(4, 9)]

        def finalize_chunk(hp, tc):
            if True:
                ps = PS("mm")
                N, t0 = inproj_fm(wbg, 0, tc, ps)
                i2 = cnt["t"] % 2
                cnt["t"] += 1
                if tc == 8:
                    act(sgs3[:, hp, :], ps.ap[:, :32], AF.Silu, [ps], [sgs])
                    return
                act(sg[i2].ap, ps.ap, AF.Silu, [ps], [sg[i2]])
                cs = slice(t0 - 2048, t0 - 2048 + 512)
                cp("pool", numt[i2].ap[0:64], U[0].ap[0:64, cs], [U[0]], [numt[i2]])
                cp("dve", numt[i2].ap[64:128], U[1].ap[0:64, cs], [U[1]], [numt[i2]])
                cp("dve", dent[i2].ap[0:64], U[0].ap[64:128, cs], [U[0]], [dent[i2]])
                cp("pool", dent[i2].ap[64:128], U[1].ap[64:128, cs], [U[1]], [dent[i2]])
                act(dent[i2].ap, dent[i2].ap, AF.Ln, [dent[i2]], [dent[i2]])
                act(dent[i2].ap, dent[i2].ap, AF.Exp, [dent[i2]], [dent[i2]], scale=-1.0)
                tt("dve", numt[i2].ap, numt[i2].ap, dent[i2].ap, ALU.mult, [numt[i2], dent[i2]], [numt[i2]])
                tt("dve", ogc[i2].ap, numt[i2].ap, sg[i2].ap, ALU.mult, [numt[i2], sg[i2]], [ogc[i2]])
                dma(ogd[hp, :, cs], ogc[i2].ap, reads=[ogc[i2]])
                if DEBUG and hp == 0 and tc == 3:
                    dump("ogc", ogc[i2].ap, [128, 512], [ogc[i2]], BF16)


        jobs = [(hp, g) for hp in range(npairs) for g in (0, 1, 2, 3)]
        wslot = {}

        def issue_load(ji):
            hp, g = jobs[ji]
            ws = wst[ji % 2]
            if g < 3:
                nload = sum(1 for (h2, g2) in jobs[:ji] if g2 < 3)
                wb = wbf[nload % 2]
                for s_ in range(3):
                    c0 = g * 3072 + s_ * 1024 + hp * 128
                    dma(w3(ws)[:, :, s_ * 128:(s_ + 1) * 128], a_win[:, c0:c0 + 128].rearrange("(k p) c -> p k c", p=128), writes=[ws])
                cp("pool", wb.ap, ws.ap, [ws], [wb])
                wslot[ji] = wb
            else:
                c0 = 9216 + hp * 128
                dma(w3(ws)[:, :, 0:128], a_win[:, c0:c0 + 128].rearrange("(k p) c -> p k c", p=128), writes=[ws])
                cp("pool", wbg.ap.rearrange("p (k c) -> p k c", k=8), w3(ws)[:, :, 0:128], [ws], [wbg])

        pending_steps = []
        loaded = set()

        def ensure_load(ji):
            if ji < len(jobs) and ji not in loaded:
                loaded.add(ji)
                issue_load(ji)

        ensure_load(0)
        for ji, (hp, g) in enumerate(jobs):
            ensure_load(ji + 1)
            if ji + 1 < len(jobs) and jobs[ji + 1][1] == 3:
                ensure_load(ji + 2)
            if g < 3:
                W, d = GROUPS[g]
                run_group(hp, g, W, d, wslot[ji], pending_steps if g == 0 else None)
            elif STAGE >= 1:
                while pending_steps:
                    pending_steps.pop(0)()
                pending_steps.extend(finalize_steps(hp))
        while pending_steps:
            pending_steps.pop(0)()

        if STAGE >= 3:
            S.barrier()
            A.off = hmark
            if STAGE >= 4:
                top0 = A.n - (8 * 5152 + 16 * 1024) // 2
                Wss = Tile(A.t[:, top0:top0 + 8 * 5152 // 2].bitcast(BF16))
                Ws3 = Wss.ap.rearrange("p (k c) -> p k c", k=8)
                wos = Tile(A.t[:, top0 + 8 * 5152 // 2:A.n].bitcast(BF16))
                wos3 = wos.ap.rearrange("p (h c) -> p h c", h=16)

                def pdma(out, in_, reads=(), writes=()):
                    return S.op("pool", lambda e: e.dma_start(out=out, in_=in_), reads=reads, writes=writes, dma=True)

                for kc in range(8):
                    pdma(Ws3[:, kc, :], s_win[kc * 128:(kc + 1) * 128, :], writes=[Wss])
            kvb = [A.tile(2048) for _ in range(2)]
            vcb = [A.tile(1024, BF16) for _ in range(2)]
            kcT = [A.tile(1024, BF16) for _ in range(2)]
            qbd = A.tile(8 * 3 * 16, BF16)
            qbd4 = qbd.ap.rearrange("p (h g t) -> p h g t", h=8, g=3)
            zer = A.tile(128, BF16)
            Psm = [A.tile(128, BF16) for _ in range(2)]
            PN = A.tile(384, BF16)
            rden = A.tile(128)
            osb = A.tile(128)
            mset("dve", qbd.ap, 0.0, [qbd])
            mset("dve", zer.ap, 0.0, [zer])
            blocks = []
            for g, (W, d) in enumerate(GROUPS):
                for rp in range(min(d, 8)):
                    blocks.append((g, d, rp))
            for b in range(4):
                cp("dve", qbd4[0:64, :, :, 0:8], qs4[0:64, :, :, b * 8:(b + 1) * 8], [qs_all], [qbd])
                cp("dve", qbd4[64:128, :, :, 8:16], qs4[64:128, :, :, b * 8:(b + 1) * 8], [qs_all], [qbd])
                psA = PS("pv")
                mm(psA.ap[:, 0:256], zer.ap, mask2_b, True, False, [zer, cb], [psA])
                psN = PS("aux")
                for g in range(3):
                    for hp in range(8):
                        mm(psN.ap[:32, g * 128 + hp * 16:g * 128 + hp * 16 + 16], ks4[:, hp, g, :], qbd4[:, hp, g, :], True, True, [ks_all, qbd], [psN])
                act(PN.ap[:32], psN.ap[:32, 0:384], AF.Exp, [psN], [PN])
                mN = maskN_b[:32, b * 48:(b + 1) * 48].rearrange("p (g t) -> p g t", g=3).unsqueeze(2).broadcast_to([32, 3, 8, 16])
                pn4 = PN.ap[:32].rearrange("p (g h t) -> p g h t", g=3, h=8)
                tt("dve", pn4, pn4, mN, ALU.mult, [PN, cb], [PN])
                for bi, (g, d, rp) in enumerate(blocks):
                    kv = kvb[bi % 2]
                    W = GROUPS[g][0]
                    dma(kv.ap, kvc[g][b, rp:rp + 127 * d + 1:d, :, :].rearrange("r s c -> r (s c)"), writes=[kv])
                    vb = vcb[bi % 2]
                    act(vb.ap, kv.ap[:, 1024:2048], AF.Copy, [kv], [vb]) if bi % 2 else cp("dve", vb.ap, kv.ap[:, 1024:2048], [kv], [vb])
                    kT = kcT[bi % 2]
                    for half in range(2):
                        pT = PS("aux")
                        for j in range(4):
                            hp = half * 4 + j
                            tr(pT.ap[:, j * 128:(j + 1) * 128], kv.ap[:, hp * 128:(hp + 1) * 128], identf, [kv, cst], [pT])
                        act(kT.ap[:, half * 512:(half + 1) * 512], pT.ap, AF.Copy, [pT], [kT])
                    psS = PS("mm")
                    for hp in range(8):
                        mm(psS.ap[:, hp * 16:(hp + 1) * 16], kT.ap[:, hp * 128:(hp + 1) * 128], qbd4[:, hp, g, :], True, True, [kT, qbd], [psS])
                    Pm = Psm[bi % 2]
                    act(Pm.ap, psS.ap[:, 0:128], AF.Exp, [psS], [Pm])
                    mS = maskS_b[:, bi * 16:(bi + 1) * 16].unsqueeze(1).broadcast_to([128, 8, 16])
                    tt("dve", v3(Pm.ap, a=8), v3(Pm.ap, a=8), mS, ALU.mult, [Pm, cb], [Pm])
                    for hp in range(8):
                        mm(psA.ap[:, hp * 16:(hp + 1) * 16], vb.ap[:, hp * 128:(hp + 1) * 128], Pm.ap[:, hp * 16:(hp + 1) * 16], False, False, [vb, Pm], [psA])
                    mm(psA.ap[:, 128:256], ones_b, Pm.ap, False, False, [Pm, cb], [psA])
                for g in range(3):
                    for hp in range(8):
                        mm(psA.ap[:, hp * 16:(hp + 1) * 16], vs4[:32, hp, g, :], PN.ap[:32, g * 128 + hp * 16:g * 128 + hp * 16 + 16], False, False, [vs_all, PN], [psA])
                    mm(psA.ap[:, 128:256], ones_b[:32, :], PN.ap[:32, g * 128:(g + 1) * 128], False, g == 2, [PN, cb], [psA])
                S.op("dve", lambda e, psA=psA: e.reciprocal(out=rden.ap, in_=psA.ap[:, 128:256]), reads=[psA], writes=[rden])
                tt("dve", osb.ap, psA.ap[:, 0:128], rden.ap, ALU.mult, [psA, rden], [osb])
                o3 = v3(osb.ap, a=8)
                tt("dve", ogs3[0:64, :, b * 8:(b + 1) * 8], o3[0:64, :, 0:8], sgs3[0:64, :, b * 8:(b + 1) * 8], ALU.mult, [osb, sgs], [ogs])
                tt("dve", ogs3[64:128, :, b * 8:(b + 1) * 8], o3[64:128, :, 8:16], sgs3[64:128, :, b * 8:(b + 1) * 8], ALU.mult, [osb, sgs], [ogs])
            if DEBUG:
                dump("ogs", ogs.ap, [128, 256], [ogs], BF16)

            S.barrier()
            A.off = hmark
            woutb = A.tile(8 * 1024, BF16)
            S.op("pool", lambda e: e.dma_start(out=woutb.ap.rearrange("p (h c) -> p h c", h=8), in_=a_wout.rearrange("(h p) c -> p h c", p=128)),
                 writes=[woutb], dma=True)
            wo3 = woutb.ap.rearrange("p (h c) -> p h c", h=8)
            ogb = [A.tile(1024, BF16) for _ in range(2)]
            xres = [A.tile(1024) for _ in range(2)]
            x1t = [A.tile(1024) for _ in range(2)]
            def op_loads(blk):
                i2 = blk % 2
                if blk < 32:
                    dma(v3(ogb[i2].ap, a=8), ogd[:, :, (blk - 16) * 128:(blk - 15) * 128].rearrange("h p t -> p h t"), writes=[ogb[i2]])
                    dma(xres[i2].ap, xp[blk * 128:(blk + 1) * 128, :], writes=[xres[i2]])
                else:
                    dma(xres[i2].ap[:32], xs, writes=[xres[i2]])

            op_loads(16)
            for blk in range(16, 33):
                rows = 128 if blk < 32 else 32
                i2 = blk % 2
                if blk + 1 < 33:
                    op_loads(blk + 1)
                if blk < 32:
                    og3 = v3(ogb[i2].ap, a=8)
                    rd_og = [ogb[i2]]
                else:
                    og3 = ogs3
                    rd_og = [ogs]
                for ncol in range(2):
                    ps = PS("mm")
                    for hp in range(8):
                        mm(ps.ap[:rows, :], og3[:, hp, :rows], wo3[:, hp, ncol * 512:(ncol + 1) * 512], hp == 0, hp == 7, rd_og + [woutb], [ps])
                    tt("dve", x1t[i2].ap[:rows, ncol * 512:(ncol + 1) * 512], ps.ap[:rows, :], xres[i2].ap[:rows, ncol * 512:(ncol + 1) * 512], ALU.add,
                       [ps, xres[i2]], [x1t[i2]])
                dma(x1d[(blk - 16) * 128:(blk - 16) * 128 + rows, :], x1t[i2].ap[:rows], reads=[x1t[i2]])
                if STAGE == 3:
                    if blk < 32:
                        dma(y_p[(blk - 16) * 128:(blk - 15) * 128, :], x1t[i2].ap, reads=[x1t[i2]])
                    else:
                        dma(y_s, x1t[i2].ap[:32], reads=[x1t[i2]])

        if STAGE >= 4:
            S.barrier()
            A.off = hmark
            AX = mybir.AxisListType.X
            roles["aux"] = [4, 5, 6, 7]
            x1b_pre = [A.tile(1024) for _ in range(2)]
            prm = A.tile(8 + 96 + 24 + 32 * 3 + 16 + 4)
            normw2 = prm.ap[:, 0:8]
            cw = prm.ap[:, 8:104]
            cbias = prm.ap[:, 104:128]
            dtb = prm.ap[:, 128:160]
            Aneg = prm.ap[:, 160:192]
            Drep = prm.ap[:, 192:224]
            gnT = prm.ap[:, 224:240]
            one1 = prm.ap[:, 240:241]
            eps5 = prm.ap[:, 241:242]
            for dst, src in ((normw2, s_normw), (cw, s_cw), (cbias, s_cb), (dtb, s_dtb), (Aneg, s_alog), (Drep, s_D), (gnT, s_gn)):
                dma(dst, src, writes=[prm])
            act(Aneg, Aneg, AF.Exp, [prm], [prm])
            ts("dve", Aneg, Aneg, -1.0, ALU.mult, [prm], [prm])
            mset("dve", one1, 1.0, [prm])
            mset("dve", eps5, 1e-5, [prm])
            triM = cview("tri")

            x1b = x1b_pre
            jk = A.tile(2048)
            xn2 = Tile(jk.ap[:, 1024:2048], jk.buf)
            yout = [Tile(jk.ap[:, 0:1024], jk.buf), Tile(jk.ap[:, 1024:2048], jk.buf)]
            sm2 = A.tile(8)
            hT2 = A.tile(8 * 128, BF16)
            hT23 = hT2.ap.rearrange("p (k t) -> p k t", k=8)
            xpre = A.tile(24 * 131)
            xpre3 = xpre.ap.rearrange("p (s t) -> p s t", s=24)
            blockA = A.alloc(3072)
            yacc = Tile(blockA[:, 0:2048])
            Tt = Tile(blockA[:, 2048:3072])
            Ee = Tt
            cacc3 = blockA.rearrange("p (s t) -> p s t", s=24)
            caccT = [yacc, Tt]
            cslab = [Tile(None) for _ in range(24)]
            xbc_w = A.alloc(1536)
            xbcT = Tile(xbc_w.bitcast(BF16))
            xbc3 = xbcT.ap.rearrange("p (s t) -> p s t", s=24)
            ynT = Tile(xbc_w[:, 0:1024].bitcast(BF16), xbcT.buf)
            ynT3 = ynT.ap.rearrange("p (s t) -> p s t", s=16)
            blockB = A.alloc(2048)
            xtok = Tile(blockB[:, 0:1024].bitcast(BF16))
            xdt = Tile(blockB[:, 1024:2048].bitcast(BF16))
            szT = [xtok, xdt]
            sz_ap = blockB
            yb = A.tile(2048, BF16)
            xdte = Tile(yb.ap, yb.buf)
            Btok = A.tile(512, BF16)
            Hf = A.tile(2048)
            Hb = A.tile(2048, BF16)
            dts = A.tile(32 * 8)
            wreg = A.t[:, top0 + 8 * 5152 // 2:A.n]
            xbcT_b = Tile(wreg[:, 0:1536].bitcast(BF16))
            dts_b = Tile(wreg[:, 1536:1792])
            XBC = [xbcT, xbcT_b]
            DTS = [dts, dts_b]
            aT = A.tile(128)
            CBm = A.tile(128)
            MT2 = [Tile(jk.ap[:, 1536:2048].bitcast(BF16)), Tile(jk.ap[:, 0:512].bitcast(BF16))]
            tmpg = Tile(jk.ap[:, 1024:1536])
            CBm2 = [CBm, Tile(jk.ap[:, 512:640])]
            ssd_t = MT2 + [tmpg, CBm2[1]]
            assert A.off <= top0, ("L1 tiles overlap resident weights", A.off, top0)
            mset("dve", xpre.ap, 0.0, [xpre])
            mset("dve", Hf.ap, 0.0, [Hf])
            mset("pool", Hb.ap, 0.0, [Hb])

            def bf_bank(t):
                return t.ap.bitcast(BF16)

            def state_out(dst):
                for q4 in range(4):
                    pT = PS("aux")
                    for j in range(4):
                        t_ = q4 * 4 + j
                        tr(pT.ap[:, j * 128:(j + 1) * 128], Hf.ap[:, t_ * 128:(t_ + 1) * 128], identf, [Hf, cst], [pT])
                    act(jk.ap[:, q4 * 512:(q4 + 1) * 512], pT.ap, AF.Copy, [pT], [jk])
                dma(dst.rearrange("(t p) n -> p t n", p=128), v3(jk.ap, a=16), reads=[jk])

            def state_in(src):
                dma(v3(jk.ap, a=16), src.rearrange("(t p) n -> p t n", p=128), writes=[jk])
                for q4 in range(4):
                    pT = PS("aux")
                    for j in range(4):
                        t_ = q4 * 4 + j
                        tr(pT.ap[:, j * 128:(j + 1) * 128], jk.ap[:, t_ * 128:(t_ + 1) * 128], identf, [jk, cst], [pT])
                    act(Hf.ap[:, q4 * 512:(q4 + 1) * 512], pT.ap, AF.Copy, [pT], [Hf])
                cp("pool", Hb.ap, Hf.ap, [Hf], [Hb])

            def nc_dma(out, in_, reads=(), writes=()):
                return S.op("sp", lambda e: e.dma_start(out=out, in_=in_, allow_slow_non_contiguous=True), reads=reads, writes=writes)

            def l1_block(tok0, L, ydst, bi, pre=False, allproj=False, save=None, reuse=None, par=0):
                xbcT = XBC[par]
                xbc3 = xbcT.ap.rearrange("p (s t) -> p s t", s=24)
                dts = DTS[par]

                def dsl(i):
                    return dts.ap[:, i * 32:(i + 1) * 32]
                NSL = 20 if pre else 24
                NPJ = 24 if (allproj or not pre) else 20
                SL0 = 20 if reuse is not None else 0
                x1 = x1b[bi % 2]
                dma(x1.ap[:L], x1d[tok0:tok0 + L, :], writes=[x1])
                tt("dve", jk.ap[:L, 0:1024], x1.ap[:L], x1.ap[:L], ALU.mult, [x1], [jk])
                S.op("dve", lambda e: e.tensor_reduce(out=sm2.ap[:L, 0:1], in_=jk.ap[:L, 0:1024], axis=AX, op=ALU.add), reads=[jk], writes=[sm2])
                act(sm2.ap[:L, 1:2], sm2.ap[:L, 0:1], AF.Ln, [sm2, eps6], [sm2], bias=eps6.ap[:L], scale=1.0 / 1024)
                act(sm2.ap[:L, 2:3], sm2.ap[:L, 1:2], AF.Exp, [sm2], [sm2], scale=-0.5)
                act(xn2.ap[:L], x1.ap[:L], AF.Copy, [x1, sm2], [xn2], scale=sm2.ap[:L, 2:3])
                for half in range(2):
                    pb = PS("aux")
                    for k4 in range(4):
                        kc = half * 4 + k4
                        tr(pb.ap[:, k4 * 128:k4 * 128 + L], xn2.ap[:L, kc * 128:(kc + 1) * 128], identf[:L, :L], [xn2, cst], [pb])
                    tt("dve", hT23[:, half * 4:half * 4 + 4, :L], v3(pb.ap, a=4)[:, :, :L],
                       normw2[:, half * 4:half * 4 + 4].unsqueeze(2).broadcast_to([128, 4, L]), ALU.mult, [pb, prm], [hT2])
                if reuse is None:
                    ps = PS("aux")
                    for kc in range(8):
                        mm(ps.ap[:L, 0:32], hT23[:, kc, :L], Ws3[:, kc, 5120:5152], kc == 0, kc == 7, [hT2, Wss], [ps])
                    tt("dve", dsl(0)[:L], ps.ap[:L, 0:32], dtb[:L], ALU.add, [ps, prm], [dts])
                for s4 in range((0 if allproj else SL0 // 4), NPJ // 4):
                    ps = PS("mm")
                    for j in range(4):
                        slab = s4 * 4 + j
                        for kc in range(8):
                            mm(ps.ap[:, j * 128:j * 128 + L], Ws3[:, kc, 2048 + slab * 128:2048 + (slab + 1) * 128], hT23[:, kc, :L],
                               kc == 0, kc == 7, [hT2, Wss], [ps])
                    act(xpre3[:, s4 * 4:s4 * 4 + 4, 3:3 + L], v3(ps.ap, a=4)[:, :, :L], AF.Copy, [ps], [xpre])
                if reuse is None:
                    act(dsl(1)[:L], dsl(0)[:L], AF.Abs, [dts], [dts])
                    act(dsl(1)[:L], dsl(1)[:L], AF.Exp, [dts], [dts], scale=-1.0)
                    act(dsl(1)[:L], dsl(1)[:L], AF.Ln, [dts, prm], [dts], bias=one1[:L], scale=1.0)
                    stt(dsl(2)[:L], dsl(0)[:L], 0.0, dsl(1)[:L], ALU.max, ALU.add, [dts], [dts])
                    tt("dve", dsl(3)[:L], dsl(2)[:L], Aneg[:L], ALU.mult, [dts, prm], [dts])
                    ps = PS("aux")
                    mm(ps.ap[:L, 0:32], triM[:L, :L], dsl(3)[:L], True, True, [cst, dts], [ps])
                    cp("dve", dsl(4)[:L], ps.ap[:L, 0:32], [ps], [dts])
                    act(dsl(6)[:L], dsl(4)[:L], AF.Exp, [dts], [dts])
                    ps = PS("aux")
                    mm(ps.ap[:, 0:32], identf[:L, L - 1:L].broadcast_to([L, 128]), dsl(4)[:L], True, True, [cst, dts], [ps])
                    cp("dve", dsl(7), ps.ap[:, 0:32], [ps], [dts])
                    tt("dve", dsl(5)[:L], dsl(7)[:L], dsl(4)[:L], ALU.subtract, [dts], [dts])
                    act(dsl(5)[:L], dsl(5)[:L], AF.Exp, [dts], [dts])
                    act(dsl(7), dsl(7), AF.Exp, [dts], [dts])
                    ps = PS("aux")
                    tr(ps.ap[:32, :L], dsl(4)[:L], identf[:L, :L], [dts, cst], [ps])
                    cp("dve", aT.ap[:32, :L], ps.ap[:32, :L], [ps], [aT])
                else:
                    dma(dts.ap, dtsd[reuse], writes=[dts])
                    dma(aT.ap[:32, :128], aTd[reuse], writes=[aT])
                if save is not None:
                    dma(dtsd[save], dts.ap, reads=[dts])
                    dma(aTd[save], aT.ap[:32, :128], reads=[aT])
                S.touch("dve", caccT)
                S.touch("act", caccT)
                if L <= 32:
                    S.touch("dve", [jk])
                    cw3 = cw.rearrange("p (j s) -> p j s", j=4)
                    accv = cacc3[:, :, :L]
                    tmpv = tmpg.ap[:, 0:24 * L].rearrange("p (s t) -> p s t", s=24)
                    allc = cslab + caccT
                    tt("dve", accv, xpre3[:, :, 0:L], cw3[:, 0, :].unsqueeze(2).broadcast_to([128, 24, L]), ALU.mult, [xpre, prm], allc)
                    tt("dve", accv, accv, cbias.unsqueeze(2).broadcast_to([128, 24, L]), ALU.add, [prm] + allc, allc)
                    for j in range(1, 4):
                        tt("dve", tmpv, xpre3[:, :, j:j + L], cw3[:, j, :].unsqueeze(2).broadcast_to([128, 24, L]), ALU.mult, [xpre, prm], [tmpg])
                        tt("dve", accv, accv, tmpv, ALU.add, [tmpg] + allc, allc)
                else:
                    for j in range(4):
                        for slab in range(SL0, NSL):
                            cs_ = cslab[slab]
                            if j == 0:
                                act(cacc3[:, slab, :L], xpre3[:, slab, 0:L], AF.Identity, [xpre, prm], [cs_], bias=cbias[:, slab:slab + 1], scale=cw[:, slab:slab + 1])
                            else:
                                stt(cacc3[:, slab, :L], xpre3[:, slab, j:j + L], cw[:, j * 24 + slab:j * 24 + slab + 1], cacc3[:, slab, :L],
                                    ALU.mult, ALU.add, [xpre, prm, cs_], [cs_])
                if reuse is not None:
                    dma(xbc3[:, 0:20, :], xbs[reuse].rearrange("p (s t) -> p s t", s=20), writes=[xbcT])
                act(xbc3[:, SL0:NSL, :L], cacc3[:, SL0:NSL, :L], AF.Silu, cslab + caccT, [xbcT])
                if save is not None:
                    dma(xbs[save].rearrange("p (s t) -> p s t", s=20), xbc3[:, 0:20, :], reads=[xbcT])
                cp("dve", xpre3[:, :, 0:3], xpre3[:, :, L:L + 3], [xpre], [xpre])
                if pre:
                    yield
                for half in range(2):
                    pb = PS("aux")
                    pv_ = bf_bank(pb)
                    for j in range(8):
                        slab = half * 8 + j
                        tr(pv_[:L, j * 128:(j + 1) * 128], xbc3[:, slab, :L], identb, [xbcT, cb], [pb])
                    act(xtok.ap[:L, half * 1024:(half + 1) * 1024], pv_[:L, :], AF.Copy, [pb], [xtok])
                pb = PS("aux")
                pv_ = bf_bank(pb)
                for gi in range(4):
                    tr(pv_[:L, gi * 128:(gi + 1) * 128], xbc3[:, 16 + gi, :L], identb, [xbcT, cb], [pb])
                act(Btok.ap[:L], pv_[:L, 0:512], AF.Copy, [pb], [Btok])
                x3 = xtok.ap[:L].rearrange("p (h c) -> p h c", h=32)
                if pre:
                    tt("dve", dsl(1)[:L], dsl(2)[:L], dsl(5)[:L], ALU.mult, [dts], [dts])
                    tt("dve", xdte.ap[:L].rearrange("p (h c) -> p h c", h=32), x3, dsl(1)[:L].unsqueeze(2).broadcast_to([L, 32, 64]), ALU.mult,
                       [xtok, dts], [xdte])
                else:
                    tt("dve", xdt.ap[:L].rearrange("p (h c) -> p h c", h=32), x3, dsl(2)[:L].unsqueeze(2).broadcast_to([L, 32, 64]), ALU.mult,
                       [xtok, dts], [xdt])
                    tt("dve", xdte.ap[:L].rearrange("p (h c) -> p h c", h=32), xdt.ap[:L].rearrange("p (h c) -> p h c", h=32),
                       dsl(5)[:L].unsqueeze(2).broadcast_to([L, 32, 64]), ALU.mult, [xdt, dts], [xdte])
                g3 = lambda ap: ap.rearrange("p (h c) -> p h c", h=8)
                S.touch("dve", [jk])
                S.touch("act", [jk])
                S.touch("pe", [jk])

                def ssd_stage1(gi):
                    BT = xbc3[:, 16 + gi, :L]
                    CT = xbc3[:, 20 + gi, :L]
                    cbm = CBm2[gi % 2]
                    mt = MT2[gi % 2]
                    ps = PS("aux")
                    mm(ps.ap[:L, :L], BT, CT, True, True, [xbcT], [ps])
                    tt("dve", cbm.ap[:L, :L], ps.ap[:L, :L], triM[:L, :L], ALU.mult, [ps, cst], [cbm])
                    for half in range(2):
                        pe_ = PS("mm")
                        for j in range(4):
                            h = gi * 8 + half * 4 + j
                            mm(pe_.ap[:L, j * 128:j * 128 + L], identf[0:32, h:h + 1].broadcast_to([32, L]), aT.ap[:32, :L], True, True, [cst, aT], [pe_])
                        h0 = gi * 8 + half * 4
                        tt("dve", v3(Tt.ap[:L, half * 512:(half + 1) * 512], a=4)[:, :, :L], v3(pe_.ap[:L, :], a=4)[:, :, :L],
                           dsl(4)[:L, h0:h0 + 4].unsqueeze(2).broadcast_to([L, 4, L]), ALU.subtract, [pe_, dts], [Tt])
                    T3 = v3(Tt.ap, a=8)[:L, :, :L]
                    M3 = v3(mt.ap, a=8)[:L, :, :L]
                    tt("dve", T3, T3, triM[:L, :L].unsqueeze(1).broadcast_to([L, 8, L]), ALU.mult, [Tt, cst], [Tt])
                    act(T3, T3, AF.Exp, [Tt], [Tt])
                    tt("dve", M3, T3, cbm.ap[:L, :L].unsqueeze(1).broadcast_to([L, 8, L]), ALU.mult, [Tt, cbm], [mt])

                def ssd_stage2(gi):
                    CT = xbc3[:, 20 + gi, :L]
                    mt = MT2[gi % 2]
                    if not pre:
                        psY = PS("mm")
                        for j in range(8):
                            h = gi * 8 + j
                            mm(psY.ap[:L, j * 64:(j + 1) * 64], v3(mt.ap, a=8)[:L, j, :L], xdt.ap[:L, h * 64:(h + 1) * 64], True, True, [mt, xdt], [psY])
                        psO = PS("mm")
                        mm(psO.ap[:L, :], CT, Hb.ap[:, gi * 512:(gi + 1) * 512], True, True, [xbcT, Hb], [psO])
                        tt("dve", g3(tmpg.ap[:L]), g3(psO.ap[:L, :]), dsl(6)[:L, gi * 8:(gi + 1) * 8].unsqueeze(2).broadcast_to([L, 8, 64]), ALU.mult,
                           [psO, dts], [tmpg])
                        yg = yacc.ap[:L, gi * 512:(gi + 1) * 512]
                        tt("dve", yg, tmpg.ap[:L], psY.ap[:L, :], ALU.add, [tmpg, psY], [yacc])
                        tt("dve", g3(tmpg.ap[:L]), g3(xtok.ap[:L, gi * 512:(gi + 1) * 512]), Drep[:L, gi * 8:(gi + 1) * 8].unsqueeze(2).broadcast_to([L, 8, 64]),
                           ALU.mult, [xtok, prm], [tmpg])
                        tt("dve", yg, yg, tmpg.ap[:L], ALU.add, [tmpg, yacc], [yacc])
                    psS = PS("mm")
                    mm(psS.ap[:, :], Btok.ap[:L, gi * 128:(gi + 1) * 128], xdte.ap[:L, gi * 512:(gi + 1) * 512], True, True, [Btok, xdte], [psS])
                    Hg = Hf.ap[:, gi * 512:(gi + 1) * 512]
                    tt("dve", g3(Hg), g3(Hg), dsl(7)[:, gi * 8:(gi + 1) * 8].unsqueeze(2).broadcast_to([128, 8, 64]), ALU.mult,
                       [Hf, dts] + ([] if pre else [psO]), [Hf])
                    tt("dve", Hg, Hg, psS.ap[:, :], ALU.add, [Hf, psS], [Hf])
                    if not pre:
                        cp("pool", Hb.ap[:, gi * 512:(gi + 1) * 512], Hg, [Hf], [Hb])

                if pre:
                    for gi in range(4):
                        ssd_stage2(gi)
                else:
                    ssd_stage1(0)
                    for gi in range(4):
                        if gi + 1 < 4:
                            ssd_stage1(gi + 1)
                        ssd_stage2(gi)
                if pre:
                    return
                for c in range(4):
                    ps = PS("mm")
                    for kc in range(8):
                        mm(ps.ap[:L, :], hT23[:, kc, :L], Ws3[:, kc, c * 512:(c + 1) * 512], kc == 0, kc == 7, [hT2, Wss], [ps])
                    act(sz_ap[:L, c * 512:(c + 1) * 512], ps.ap[:L, :], AF.Silu, [ps], szT)
                tt("dve", yacc.ap[:L], yacc.ap[:L], sz_ap[:L], ALU.mult, [yacc] + szT, [yacc])
                tt("dve", jk.ap[:L], yacc.ap[:L], yacc.ap[:L], ALU.mult, [yacc], [jk] + ssd_t)
                S.op("dve", lambda e: e.tensor_reduce(out=sm2.ap[:L, 4:8], in_=v3(jk.ap, a=4)[:L], axis=AX, op=ALU.add), reads=[jk], writes=[sm2])
                act(sm2.ap[:L, 4:8], sm2.ap[:L, 4:8], AF.Ln, [sm2, prm], [sm2], bias=eps5[:L], scale=1.0 / 512)
                act(sm2.ap[:L, 4:8], sm2.ap[:L, 4:8], AF.Exp, [sm2], [sm2], scale=-0.5)
                tt("dve", v3(yb.ap, a=4)[:L], v3(yacc.ap, a=4)[:L], sm2.ap[:L, 4:8].unsqueeze(2).broadcast_to([L, 4, 512]), ALU.mult, [yacc, sm2], [yb])
                for half in range(2):
                    pb = PS("aux")
                    pv_ = bf_bank(pb)
                    for j in range(8):
                        slab = half * 8 + j
                        tr(pv_[:, j * 128:j * 128 + L], yb.ap[:L, slab * 128:(slab + 1) * 128], identb[:L, :L], [yb, cb], [pb])
                    tt("dve", ynT3[:, half * 8:half * 8 + 8, :L], v3(pv_, a=8)[:, :, :L], gnT[:, half * 8:half * 8 + 8].unsqueeze(2).broadcast_to([128, 8, L]),
                       ALU.mult, [pb, prm], [ynT])
                yo = yout[bi % 2]
                for ncol in range(2):
                    ps = PS("mm")
                    for slab in range(16):
                        mm(ps.ap[:L, :], ynT3[:, slab, :L], wos3[:, slab, ncol * 512:(ncol + 1) * 512], slab == 0, slab == 15, [ynT, wos], [ps])
                    tt("dve", yo.ap[:L, ncol * 512:(ncol + 1) * 512], ps.ap[:L, :], x1.ap[:L, ncol * 512:(ncol + 1) * 512], ALU.add, [ps, x1], [yo])
                dma(ydst, yo.ap[:L], reads=[yo])

            nblk = 16 if STAGE >= 5 else 2
            def run(gen):
                for _ in gen:
                    pass

            gens = [l1_block(blk * 128, 128, None, blk, pre=True, allproj=(blk == nblk - 1), save=blk, par=blk % 2) for blk in range(nblk)]
            next(gens[0])
            for blk in range(nblk):
                if blk + 1 < nblk:
                    next(gens[blk + 1])
                run(gens[blk])
            for sl_ in range(0, 16, 4):
                pdma(wos3[:, sl_:sl_ + 4, :], s_wout[sl_ * 128:(sl_ + 4) * 128, :].rearrange("(h p) c -> p h c", p=128), writes=[wos, xbcT_b, dts_b])
            ccin_t = Tile(None)
            dma(cc_in[:, 0:2048], Hf.ap, reads=[Hf], writes=[ccin_t])
            dma(cc_in[:, 2048:2120].rearrange("p (s j) -> p s j", s=24), xpre3[:, :, 0:3], reads=[xpre, ccin_t], writes=[ccin_t])
            cct = Tile(None)
            S.op("pool", lambda e: e.collective_compute("AllGather", ALU.bypass, replica_groups=[[0, 1], [2, 3], [4, 5], [6, 7]],
                                                         ins=[cc_in], outs=[cc_out]), reads=[ccin_t], writes=[cct])
            dma(Hf.ap, cc_out[0:128, 0:2048], reads=[cct], writes=[Hf])
            dma(xpre3[:, :, 0:3], cc_out[0:128, 2048:2120].rearrange("p (s j) -> p s j", s=24), reads=[cct], writes=[xpre])
            ts("dve", Hf.ap, Hf.ap, flagt.ap[:, 0:1], ALU.mult, [Hf, flagt], [Hf])
            ts("dve", xpre3[:, :, 0:3], xpre3[:, :, 0:3], flagt.ap[:, 0:1], ALU.mult, [xpre, flagt], [xpre])
            cp("pool", Hb.ap, Hf.ap, [Hf], [Hb])
            for blk in range(nblk):
                run(l1_block(blk * 128, 128, y_p[blk * 128:(blk + 1) * 128, :], blk, reuse=(blk if blk >= 1 else None), allproj=(blk == nblk - 1)))
            for j in range(3):
                nc_dma(nconv_p[j].rearrange("(s p) -> p s", p=128), xpre3[:, :, j], reads=[xpre])
            state_out(nssm_p)
            for b in range(4):
                for j in range(3):
                    nc_dma(xpre3[:, :, j], sconv[b, j].rearrange("(s p) -> p s", p=128), writes=[xpre])
                state_in(sssm[b])
                run(l1_block(2048 + b * 8, 8, y_s[b * 8:(b + 1) * 8, :], b))
                for j in range(3):
                    nc_dma(nconv_s[b, j].rearrange("(s p) -> p s", p=128), xpre3[:, :, 8 + j], reads=[xpre])
                state_out(nssm_s[b])

        S.barrier()
        blockc = es.enter_context(nc.Block())
        S.emit(blockc)
    print("sched: instr counts", {k: len(v) for k, v in S.prog.items()}, "waits", S.nwait, "arena", A.off)
    return nc, cpack, dbg


def prep_inputs(inp, core):
    b = core // 2
    half = core % 2
    m = {}
    if half == 1:
        m["xp"] = np.ascontiguousarray(inp["x_prompt"][b])
    else:
        m["xp"] = np.concatenate([np.zeros((2048, DM), np.float32), inp["x_prompt"][b][:2048]], 0)
    m["flag"] = np.full((128, 1), float(half), np.float32)
    m["xs"] = np.ascontiguousarray(inp["x_sample"][4 * core:4 * core + 4].reshape(32, 1024))
    for g, key in enumerate(["cache_kv_g0", "cache_kv_g1", "cache_kv_g2"]):
        m["kv%d" % g] = np.ascontiguousarray(inp[key][0, 4 * core:4 * core + 4].reshape(4, GROUPS[g][0], 2, 1024))
    m["a_normw"] = np.ascontiguousarray(inp["attn_norm"][0].reshape(8, 128).T)
    m["a_win"] = np.ascontiguousarray(inp["attn_w_in"][0])
    gq = inp["attn_q_gain"][0]
    gk = inp["attn_k_gain"][0]
    gc = np.zeros((128, 6), np.float32)
    for g in range(3):
        gc[:, 2 * g] = np.tile(gq[g], 2)
        gc[:, 2 * g + 1] = np.tile(gk[g], 2)
    m["a_gain"] = gc
    m["a_wout"] = np.ascontiguousarray(inp["attn_w_out"][0])
    m["sconv"] = np.ascontiguousarray(inp["state_conv"][0, 4 * core:4 * core + 4])
    m["sssm"] = np.ascontiguousarray(inp["state_ssm"][0, 4 * core:4 * core + 4].reshape(4, 2048, 128))
    m["s_normw"] = np.ascontiguousarray(inp["ssm_norm"][0].reshape(8, 128).T)
    m["s_win"] = np.ascontiguousarray(inp["ssm_w_in"][0])
    m["s_cw"] = np.ascontiguousarray(inp["ssm_conv_w"][0].reshape(4, 24, 128).transpose(2, 0, 1).reshape(128, 96))
    m["s_cb"] = np.ascontiguousarray(inp["ssm_conv_b"][0].reshape(24, 128).T)
    m["s_dtb"] = np.ascontiguousarray(np.broadcast_to(inp["ssm_dt_bias"][0][None, :], (128, 32)))
    m["s_alog"] = np.ascontiguousarray(np.broadcast_to(inp["ssm_A_log"][0][None, :], (128, 32)))
    m["s_D"] = np.ascontiguousarray(np.broadcast_to(inp["ssm_D"][0][None, :], (128, 32)))
    m["s_gn"] = np.ascontiguousarray(inp["ssm_gate_norm"][0].reshape(16, 128).T)
    m["s_wout"] = np.ascontiguousarray(inp["ssm_w_out"][0])
    return m


_CACHE = {}


def kernel(**inputs):
    inp = {k: np.asarray(v) for k, v in inputs.items()}
    if "nc" not in _CACHE:
        _CACHE["nc"] = build()
    nc, cpack, dbg = _CACHE["nc"]
    in_maps = []
    ncores = int(os.environ.get('K_CORES', '8'))
    for c in range(ncores):
        m = prep_inputs(inp, c)
        m["consts"] = cpack
        in_maps.append(m)
    res = run_bass_kernel_spmd(nc, in_maps, core_ids=list(range(ncores)))
    R = res.results
    if ncores < 8 or STAGE < 5:
        return R
    f = np.float32
    y_prompt = np.stack([np.concatenate([np.asarray(R[2 * b]["y_p"], f), np.asarray(R[2 * b + 1]["y_p"], f)], 0) for b in range(4)])
    y_sample = np.concatenate([np.asarray(R[c]["y_s"], f).reshape(4, 8, DM) for c in range(8)], 0)
    outs = [y_prompt, y_sample]
    for g in range(3):
        W = GROUPS[g][0]
        outs.append(np.stack([np.asarray(R[2 * b + 1]["nkv%d_p" % g], f).reshape(W, 2, 16, 64) for b in range(4)])[None])
    for g in range(3):
        outs.append(np.concatenate([np.asarray(R[c]["nkv%d_s" % g], f).reshape(4, 8, 2, 16, 64) for c in range(8)], 0)[None])
    outs.append(np.stack([np.asarray(R[2 * b + 1]["nconv_p"], f) for b in range(4)])[None])
    outs.append(np.concatenate([np.asarray(R[c]["nconv_s"], f) for c in range(8)], 0)[None])
    outs.append(np.stack([np.asarray(R[2 * b + 1]["nssm_p"], f).reshape(32, 64, 128) for b in range(4)])[None])
    outs.append(np.concatenate([np.asarray(R[c]["nssm_s"], f).reshape(4, 32, 64, 128) for c in range(8)], 0)[None])
    return tuple(outs)
```

```python
import os
import numpy as np
import ml_dtypes
from contextlib import ExitStack
import concourse.bass as bass
import concourse.mybir as mybir
from concourse.bass_utils import run_bass_kernel_spmd

F32 = mybir.dt.float32
BF16 = mybir.dt.bfloat16
ALU = mybir.AluOpType
AF = mybir.ActivationFunctionType

NTOK = 4096
NSAM = 32
NT = NTOK + NSAM
DM = 1024
GROUPS = ((128, 1), (512, 4), (2048, 16))
NDS = 24


class Buf:
    __slots__ = ("w", "r")

    def __init__(self):
        self.w = None
        self.r = {}


class Tile:
    def __init__(self, ap, buf=None, excl=False):
        self.ap = ap
        self.buf = buf if buf is not None else Buf()
        self.excl = excl


class Sched:
    def __init__(self, nc, es):
        self.nc = nc
        self.keys = ["pe", "act", "dve", "pool", "sp"]
        self.sem = {k: es.enter_context(nc.semaphore("s_" + k)) for k in ["pe", "act", "dve", "pool"]}
        self.cnt = {k: 0 for k in self.sem}
        self.dsem = [es.enter_context(nc.semaphore("d%d" % i)) for i in range(NDS)]
        self.es = es
        self.psem = []
        self.dcnt = [0] * NDS
        self.dnext = 0
        self.seen = {k: {} for k in self.keys}
        self.prog = {k: [] for k in self.keys}
        self.nwait = 0

    def _wait(self, ek, sk, val):
        if self.seen[ek].get(sk, 0) >= val:
            return
        self.seen[ek][sk] = val
        self.prog[ek].append(("w", sk, val))
        self.nwait += 1

    def op(self, ek, fn, reads=(), writes=(), dma=False):
        deps = {}
        for t in reads:
            b = t.buf
            if b.w is not None:
                deps[b.w[0]] = max(deps.get(b.w[0], 0), b.w[1])
            if t.excl:
                for sk, v in b.r.items():
                    if sk != ek:
                        deps[sk] = max(deps.get(sk, 0), v)
        for t in writes:
            b = t.buf
            if b.w is not None:
                deps[b.w[0]] = max(deps.get(b.w[0], 0), b.w[1])
            for sk, v in b.r.items():
                deps[sk] = max(deps.get(sk, 0), v)
        for sk, v in deps.items():
            if sk == ek and ek in ("pe", "pool"):
                continue
            self._wait(ek, sk, v)
        if dma and ek == "pool":
            self.psem.append(self.es.enter_context(self.nc.semaphore("p%d" % len(self.psem))))
            ticket = (("p", len(self.psem) - 1), 16)
        elif ek == "sp" or dma:
            i = self.dnext
            self.dnext = (i + 1) % NDS
            if self.dcnt[i] > 0:
                self._wait(ek, ("d", i), self.dcnt[i] * 16)
            self.dcnt[i] += 1
            ticket = (("d", i), self.dcnt[i] * 16)
        else:
            self.cnt[ek] += 1
            ticket = (ek, self.cnt[ek])
        self.prog[ek].append(("i", fn, ticket))
        for t in reads:
            r = t.buf.r
            r[ticket[0]] = max(r.get(ticket[0], 0), ticket[1])
        for t in writes:
            t.buf.w = ticket
            t.buf.r = {}
        return ticket

    def touch(self, ek, tiles):
        for t in tiles:
            b = t.buf
            if b.w is not None and not (b.w[0] == ek and ek == "pe"):
                self._wait(ek, b.w[0], b.w[1])
            for sk, v in b.r.items():
                self._wait(ek, sk, v)

    def barrier(self):
        for ek in self.keys:
            for k in ["pe", "act", "dve", "pool"]:
                if k != ek and self.cnt[k] > 0:
                    self._wait(ek, k, self.cnt[k])
            for i in range(NDS):
                if self.dcnt[i] > 0:
                    self._wait(ek, ("d", i), self.dcnt[i] * 16)
            for i in range(len(self.psem)):
                self._wait(ek, ("p", i), 16)

    def semof(self, sk):
        if isinstance(sk, str):
            return self.sem[sk]
        return self.dsem[sk[1]] if sk[0] == "d" else self.psem[sk[1]]

    def emit(self, block):
        def run(ek):
            def body(e):
                for it in self.prog[ek]:
                    if it[0] == "w":
                        e.wait_ge(self.semof(it[1]), it[2])
                    else:
                        ins = it[1](e)
                        sk, v = it[2]
                        ins.then_inc(self.semof(sk), 16 if not isinstance(sk, str) else 1)
            return body
        block.tensor(run("pe"))
        block.scalar(run("act"))
        block.vector(run("dve"))
        block.gpsimd(run("pool"))
        block.sync(run("sp"))


class Arena:
    def __init__(self, nc, es, nwords):
        self.t = es.enter_context(nc.sbuf_tensor("arena", [128, nwords], F32))
        self.n = nwords
        self.off = 0

    def alloc(self, nelem, dtype=F32):
        w = nelem if dtype == F32 else (nelem + 1) // 2
        assert self.off + w <= self.n, ("SBUF arena overflow", self.off, w, self.n)
        a = self.t[:, self.off:self.off + w]
        self.off += w
        return a if dtype == F32 else a.bitcast(BF16)

    def tile(self, nelem, dtype=F32):
        return Tile(self.alloc(nelem, dtype))


def host_consts():
    c = {}
    c["ident"] = np.eye(128, dtype=np.float32)
    k = np.arange(128)
    c["bones"] = (k[:, None] // 64 == k[None, :] // 64).astype(np.float32)
    cc = np.arange(256)
    m = np.zeros((128, 256), np.float32)
    m[:, :128] = (cc[None, :128] <= k[:, None])
    m[:, 128:] = ((cc[None, 128:] - 128) >= k[:, None])
    c["mask2"] = m
    c["ones"] = np.ones((128, 128), np.float32)
    c["tri"] = (k[:, None] <= k[None, :]).astype(np.float32)
    ms = np.zeros((128, 13, 16), np.float32)
    bi = 0
    for g, (W, d) in enumerate(GROUPS):
        nres = min(d, 8)
        for rp in range(nres):
            for t in range(8):
                ok = ((t - rp) % d == 0) & (rp + d * k >= t)
                ms[:, bi, t] = ok
                ms[:, bi, 8 + t] = ok
            bi += 1
    c["maskS"] = ms.reshape(128, 13 * 16)
    mn = np.zeros((32, 4, 3, 16), np.float32)
    for b in range(4):
        for g, (W, d) in enumerate(GROUPS):
            for u in range(8):
                for t in range(8):
                    ok = (u <= t) and ((t - u) % d == 0)
                    mn[b * 8 + u, b, g, t] = ok
                    mn[b * 8 + u, b, g, 8 + t] = ok
    mnp = np.zeros((128, 4 * 3 * 16), np.float32)
    mnp[:32] = mn.reshape(32, -1)
    c["maskN"] = mnp
    return c


CONST_ORDER = ["ident", "bones", "mask2", "ones", "maskS", "maskN", "tri"]


def pack_consts():
    c = host_consts()
    offs = {}
    cols = []
    o = 0
    for n in CONST_ORDER:
        offs[n] = (o, c[n].shape[1])
        o += c[n].shape[1]
        cols.append(c[n])
    return np.concatenate(cols, axis=1).astype(np.float32), offs


STAGE = int(os.environ.get("K_STAGE", "99"))
DEBUG = os.environ.get("K_DEBUG", "") != ""
SKIP = os.environ.get("K_SKIP", "").split(",")


def build():
    cpack, coffs = pack_consts()
    NCC = cpack.shape[1]
    nc = bass.Bass("TRN2", target_bir_lowering=False)

    def din(name, shape, dt=F32):
        return nc.dram_tensor(name, list(shape), dt, kind="ExternalInput").ap()

    def dout(name, shape, dt=F32):
        return nc.dram_tensor(name, list(shape), dt, kind="ExternalOutput").ap()

    def dscr(name, shape, dt=F32):
        return nc.dram_tensor(name, list(shape), dt, kind="Internal").ap()

    xp = din("xp", [NTOK, DM])
    xs = din("xs", [NSAM, DM])
    kvc = [din("kv%d" % g, [4, GROUPS[g][0], 2, 1024]) for g in range(3)]
    consts = din("consts", [128, NCC])
    a_normw = din("a_normw", [128, 8])
    a_win = din("a_win", [DM, 10240])
    a_gain = din("a_gain", [128, 6])
    a_wout = din("a_wout", [1024, DM])

    sconv = din("sconv", [4, 3, 3072])
    sssm = din("sssm", [4, 2048, 128])
    s_normw = din("s_normw", [128, 8])
    s_win = din("s_win", [DM, 5152])
    s_cw = din("s_cw", [128, 96])
    s_cb = din("s_cb", [128, 24])
    s_dtb = din("s_dtb", [128, 32])
    s_alog = din("s_alog", [128, 32])
    s_D = din("s_D", [128, 32])
    s_gn = din("s_gn", [128, 16])
    s_wout = din("s_wout", [2048, DM])
    x1d = dscr("x1d", [2048 + NSAM, DM])
    flag_in = din("flag", [128, 1])
    xbs = dscr("xbs", [16, 128, 20 * 128], BF16)
    dtsd = dscr("dtsd", [16, 128, 256])
    aTd = dscr("aTd", [16, 32, 128])
    cc_in = dscr("cc_in", [128, 2048 + 72])
    cc_out = dscr("cc_out", [256, 2048 + 72])
    nconv_p = dout("nconv_p", [3, 3072])
    nconv_s = dout("nconv_s", [4, 3, 3072])
    nssm_p = dout("nssm_p", [2048, 128])
    nssm_s = dout("nssm_s", [4, 2048, 128])
    y_p = dout("y_p", [2048, DM])
    y_s = dout("y_s", [NSAM, DM])
    nkv_p = [dout("nkv%d_p" % g, [GROUPS[g][0], 2, 1024]) for g in range(3)]
    nkv_s = [dout("nkv%d_s" % g, [NSAM, 2, 1024]) for g in range(3)]
    ogd = dscr("ogd", [8, 128, 2048], BF16)
    dbg = {}

    es = ExitStack()
    with es:
        S = Sched(nc, es)
        A = Arena(nc, es, int(os.environ.get('K_ARENA', '53200')))
        psum = es.enter_context(nc.psum_tensor("psum", [128, 4096], F32))
        banks = [Tile(psum[:, i * 512:(i + 1) * 512], excl=True) for i in range(8)]
        roles = {"mm": [0, 1, 2, 3], "pv": [4, 5], "aux": [6, 7]}
        rr = {k: 0 for k in roles}

        def PS(role):
            lst = roles[role]
            b = banks[lst[rr[role] % len(lst)]]
            rr[role] += 1
            return b

        def dma(out, in_, reads=(), writes=()):
            return S.op("sp", lambda e: e.dma_start(out=out, in_=in_), reads=reads, writes=writes)

        def mm(out, lhsT, rhs, start, stop, reads, writes):
            return S.op("pe", lambda e: e.matmul(out, lhsT=lhsT, rhs=rhs, start=start, stop=stop), reads=reads, writes=writes)

        def tr(out, in_, ident, reads, writes):
            return S.op("pe", lambda e: e.transpose(out=out, in_=in_, identity=ident), reads=reads, writes=writes)

        def act(out, in_, func, reads, writes, bias=None, scale=None):
            kw = {}
            if bias is not None:
                kw["bias"] = bias
            if scale is not None:
                kw["scale"] = scale
            return S.op("act", lambda e: e.activation(out=out, in_=in_, func=func, **kw), reads=reads, writes=writes)

        def tt(ek, out, in0, in1, op, reads, writes):
            return S.op(ek, lambda e: e.tensor_tensor(out=out, in0=in0, in1=in1, op=op), reads=reads, writes=writes)

        def ts(ek, out, in0, s1, op0, reads, writes, s2=None, op1=None):
            if op1 is None:
                return S.op(ek, lambda e: e.tensor_scalar(out=out, in0=in0, scalar1=s1, scalar2=None, op0=op0), reads=reads, writes=writes)
            return S.op(ek, lambda e: e.tensor_scalar(out=out, in0=in0, scalar1=s1, scalar2=s2, op0=op0, op1=op1), reads=reads, writes=writes)

        def stt(out, in0, scalar, in1, op0, op1, reads, writes):
            return S.op("dve", lambda e: e.scalar_tensor_tensor(out=out, in0=in0, scalar=scalar, in1=in1, op0=op0, op1=op1), reads=reads, writes=writes)

        def cp(ek, out, in_, reads, writes):
            return S.op(ek, lambda e: e.tensor_copy(out=out, in_=in_), reads=reads, writes=writes)

        def mset(ek, ap, val, writes):
            return S.op(ek, lambda e: e.memset(ap, val), writes=writes)

        def dump(name, ap, shape, tiles, dt=F32):
            d = dout("dbg_" + name, shape, dt)
            dbg[name] = d
            dma(d, ap, reads=tiles)

        def v3(ap, **kw):
            k = list(kw.keys())[0]
            return ap.rearrange("p (a b) -> p a b", a=kw[k])

        cst = A.tile(NCC)
        dma(cst.ap, consts, writes=[cst])

        def cview(n, rows=128):
            o, w = coffs[n]
            return cst.ap[:rows, o:o + w]

        identf = cview("ident")
        normw = A.tile(8)
        dma(normw.ap, a_normw, writes=[normw])
        gcol = A.tile(6)
        dma(gcol.ap, a_gain, writes=[gcol])
        for g in range(3):
            ts("dve", gcol.ap[:, 2 * g:2 * g + 1], gcol.ap[:, 2 * g:2 * g + 1], 0.125, ALU.mult, [gcol], [gcol])
        eps6 = A.tile(1)
        mset("dve", eps6.ap, 1e-6, [eps6])
        cb = A.tile(128 + 256 + 128 + 13 * 16 + 192 + 128, BF16)
        bones_b = cb.ap[:, 0:128]
        mask2_b = cb.ap[:, 128:384]
        ones_b = cb.ap[:, 384:512]
        maskS_b = cb.ap[:, 512:512 + 208]
        maskN_b = cb.ap[:, 720:720 + 192]
        identb = cb.ap[:, 912:912 + 128]
        for dst, nm in ((bones_b, "bones"), (mask2_b, "mask2"), (ones_b, "ones"), (maskS_b, "maskS"), (maskN_b, "maskN"), (identb, "ident")):
            cp("dve", dst, cview(nm), [cst], [cb])

        flagt = A.tile(1)
        dma(flagt.ap, flag_in, writes=[flagt])
        hTb = [Tile(None) for _ in range(9)]

        def hchunks(t0, t1):
            return [hTb[c] for c in range(t0 // 512, min((t1 - 1) // 512, 8) + 1)]

        qs_all = A.tile(8 * 3 * 32, BF16)
        ks_all = A.tile(8 * 3 * 32, BF16)
        vs_all = A.tile(8 * 3 * 128, BF16)
        sgs = A.tile(8 * 32)
        qs4 = qs_all.ap.rearrange("p (h g t) -> p h g t", h=8, g=3)
        ks4 = ks_all.ap.rearrange("p (h g t) -> p h g t", h=8, g=3)
        vs4 = vs_all.ap.rearrange("p (h g c) -> p h g c", h=8, g=3)
        sgs3 = sgs.ap.rearrange("p (h t) -> p h t", h=8)
        ogs = A.tile(8 * 32, BF16)
        ogs3 = ogs.ap.rearrange("p (h t) -> p h t", h=8)
        hmark = A.off
        hT = A.alloc(8 * NT, BF16)
        hT3 = hT.rearrange("p (k t) -> p k t", k=8)
        p0mark = A.off
        xbuf = [A.tile(1024) for _ in range(2)]
        xn_l = [A.tile(1024) for _ in range(2)]
        junk_l = [A.tile(1024) for _ in range(2)]
        small_l = [A.tile(8) for _ in range(2)]

        def ttr(out, in0, in1, accum, reads, writes):
            return S.op("dve", lambda e: e.tensor_tensor_reduce(out=out, in0=in0, in1=in1, scale=1.0, scalar=0.0,
                                                                op0=ALU.mult, op1=ALU.add, accum_out=accum), reads=reads, writes=writes)

        roles["aux"] = [0, 1, 2, 3, 4, 5, 6, 7]

        def p0_stageA(blk):
            rows = 128 if blk < 32 else 32
            src = xp[blk * 128:(blk + 1) * 128, :] if blk < 32 else xs
            xb = xbuf[blk % 2]
            xn, junk, small = xn_l[blk % 2], junk_l[blk % 2], small_l[blk % 2]
            dma(xb.ap[:rows], src, writes=[xb])
            tt("dve", junk.ap[:rows], xb.ap[:rows], xb.ap[:rows], ALU.mult, [xb], [junk])
            S.op("dve", lambda e, rows=rows, small=small, junk=junk: e.tensor_reduce(out=small.ap[:rows, 0:1], in_=junk.ap[:rows], axis=mybir.AxisListType.X, op=ALU.add),
                 reads=[junk], writes=[small])
            act(small.ap[:rows, 1:2], small.ap[:rows, 0:1], AF.Ln, [small, eps6], [small], bias=eps6.ap[:rows], scale=1.0 / 1024)
            act(small.ap[:rows, 2:3], small.ap[:rows, 1:2], AF.Exp, [small], [small], scale=-0.5)
            act(xn.ap[:rows], xb.ap[:rows], AF.Copy, [xb, small], [xn], scale=small.ap[:rows, 2:3])

        def p0_stageB(blk):
            rows = 128 if blk < 32 else 32
            xn = xn_l[blk % 2]
            tok0 = blk * 128
            for half in range(2):
                pb = PS("aux")
                for k4 in range(4):
                    kc = half * 4 + k4
                    tr(pb.ap[:, k4 * 128:k4 * 128 + rows], xn.ap[:rows, kc * 128:(kc + 1) * 128], identf[:rows, :rows], [xn, cst], [pb])
                tt("dve", hT3[:, half * 4:half * 4 + 4, tok0:tok0 + rows], v3(pb.ap, a=4)[:, :, :rows],
                   normw.ap[:, half * 4:half * 4 + 4].unsqueeze(2).broadcast_to([128, 4, rows]), ALU.mult, [pb, normw], [hTb[blk // 4]])

        p0_stageA(0)
        for blk in range(33):
            if blk + 1 < 33:
                p0_stageA(blk + 1)
            p0_stageB(blk)
        if DEBUG:
            dump("hT", hT, [128, 8 * NT], hTb, BF16)

        roles["aux"] = [6, 7]
        S.barrier()
        A.off = p0mark
        wst = [A.tile(8 * 384) for _ in range(2)]
        wbf = [A.tile(8 * 384, BF16) for _ in range(2)]
        qn = A.tile(NT, BF16)
        kn = A.tile(NT, BF16)
        Vaug = A.tile(33 * 2 * 128, BF16)
        Vaug4 = Vaug.ap.rearrange("p (u h c) -> p u h c", u=33, h=2)
        mset("pool", Vaug.ap, 1.0, [Vaug])
        mpair = A.tile(3 * 512, BF16)
        mp3 = mpair.ap.rearrange("p (v c) -> p v c", v=3)
        for v_ in range(3):
            for hlf in range(2):
                cp("dve", mp3[:, v_, hlf * 256:(hlf + 1) * 256], cview("mask2"), [cst], [mpair])
        for v_, hlf in ((1, 0), (2, 0), (2, 1)):
            ts("dve", mp3[:, v_, hlf * 256:hlf * 256 + 128], mp3[:, v_, hlf * 256:hlf * 256 + 128], flagt.ap[:, 0:1], ALU.mult, [mpair, flagt], [mpair])
        U = [A.tile(2048) for _ in range(2)]
        sq = [A.tile(512, BF16) for _ in range(2)]
        rs = [A.tile(512) for _ in range(2)]
        lnv = rs
        kf = [A.tile(512) for _ in range(2)]
        kt = [A.tile(512)] * 2
        vf = [A.tile(512) for _ in range(2)]
        Pb = [A.tile(512, BF16) for _ in range(4)]
        sg = vf
        numt = kf
        dent = kt
        ogc = [A.tile(512, BF16) for _ in range(2)]
        cnt = {"w": 0, "t": 0, "p": 0}

        def w3(t):
            return t.ap.rearrange("p (k c) -> p k c", k=8)

        def inproj_fm(wt, c0, tc, ps):
            N = 512 if tc < 8 else 32
            t0 = tc * 512
            wv = wt.ap.rearrange("p (k c) -> p k c", k=8)
            for kc in range(8):
                mm(ps.ap[:, :N], wv[:, kc, c0:c0 + 128], hT3[:, kc, t0:t0 + N], kc == 0, kc == 7, [wt, hTb[tc]], [ps])
            return N, t0

        def unit(u, d):
            sp_, r_ = (u, 0) if d == 1 else (u // d, u % d)
            return sp_, r_, sp_ * 128 * d + r_

        def sl(st, d):
            return slice(st, st + 127 * d + 1, d)

        npairs = 8 if STAGE >= 2 else (1 if STAGE >= 0 else 0)
        wbg = A.tile(8 * 128, BF16)

        def run_group(hp, g, W, d, wb, mid_hook):
            span = 128 * d
            def qk_stage1(s_, tc):
                ps = PS("mm")
                N, t0 = inproj_fm(wb, s_ * 128, tc, ps)
                i2 = cnt["t"] % 2
                cnt["t"] += 1
                act(sq[i2].ap[:, :N], ps.ap[:, :N], AF.Square, [ps], [sq[i2]])
                return (s_, tc, ps, N, t0, i2)

            def qk_stage2(st8):
                s_, tc, ps, N, t0, i2 = st8
                dst = qn if s_ == 0 else kn
                p2 = PS("aux")
                mm(p2.ap[:, :N], bones_b, sq[i2].ap[:, :N], True, True, [sq[i2], cb], [p2])
                act(lnv[i2].ap[:, :N], p2.ap[:, :N], AF.Ln, [p2, eps6], [lnv[i2]], bias=eps6.ap, scale=1.0 / 64)
                act(rs[i2].ap[:, :N], lnv[i2].ap[:, :N], AF.Exp, [lnv[i2]], [rs[i2]], scale=-0.5)
                gc = gcol.ap[:, 2 * g + s_:2 * g + s_ + 1]
                stt(dst.ap[:, t0:t0 + N], ps.ap[:, :N], gc, rs[i2].ap[:, :N], ALU.mult, ALU.mult, [ps, rs[i2], gcol], [dst])
                need = (s_ == 1) and (tc == 8 or (t0 + N > NTOK - W)) and ('kout' not in SKIP)
                if need:
                    stt(kf[i2].ap[:, :N], ps.ap[:, :N], gc, rs[i2].ap[:, :N], ALU.mult, ALU.mult, [ps, rs[i2], gcol], [kf[i2]])
                    p3 = PS("aux")
                    nb = (N + 127) // 128
                    for j in range(nb):
                        rws = min(128, N - j * 128)
                        tr(p3.ap[:rws, j * 128:(j + 1) * 128], kf[i2].ap[:, j * 128:j * 128 + rws], identf, [kf[i2], cst], [p3])
                    rmax = min(128, N)
                    act(kt[i2].ap[:rmax, :nb * 128], p3.ap[:rmax, :nb * 128], AF.Copy, [p3], [kt[i2]])
                    if tc == 8:
                        dma(nkv_s[g][:, 0, hp * 128:(hp + 1) * 128], kt[i2].ap[:32, 0:128], reads=[kt[i2]])
                    else:
                        for j in range(nb):
                            tk = t0 + j * 128
                            if tk >= NTOK - W:
                                r0 = tk - (NTOK - W)
                                dma(nkv_p[g][r0:r0 + 128, 0, hp * 128:(hp + 1) * 128], kt[i2].ap[:, j * 128:(j + 1) * 128], reads=[kt[i2]])

            pend = None
            kc_min = 0 if g == 2 else 3
            for s_ in range(2):
                for tc in (range(4, 9) if s_ == 0 else range(kc_min, 9)):
                    cur = qk_stage1(s_, tc)
                    if pend is not None:
                        qk_stage2(pend)
                    pend = cur
            qk_stage2(pend)
            cp("pool", qs4[:, hp, g, :], qn.ap[:, NTOK:NT], [qn], [qs_all])
            cp("pool", ks4[:, hp, g, :], kn.ap[:, NTOK:NT], [kn], [ks_all])
            for u0 in range(0 if g == 2 else 12, 36, 4):
                ps = PS("mm")
                us = [u for u in range(u0, u0 + 4) if u <= 32]
                for j, u in enumerate(us):
                    for kc in range(8):
                        if u < 32:
                            sp_, r_, st = unit(u, d)
                            mm(ps.ap[:, j * 128:(j + 1) * 128], hT3[:, kc, sl(st, d)], w3(wb)[:, kc, 256:384], kc == 0, kc == 7,
                               [wb] + hchunks(st, st + 127 * d + 1), [ps])
                        else:
                            mm(ps.ap[:32, j * 128:(j + 1) * 128], hT3[:, kc, NTOK:NT], w3(wb)[:, kc, 256:384], kc == 0, kc == 7,
                               [wb, hTb[8]], [ps])
                if us[0] < 32:
                    n = len(us)
                    for j in range(n):
                        cp("dve", Vaug4[:, u0 + j, :, 0:64], ps.ap[:, j * 128:(j + 1) * 128].rearrange("p (h c) -> p h c", h=2), [ps], [Vaug])
                    outs = [(j, u) for j, u in enumerate(us) if unit(u, d)[2] >= NTOK - W]
                    if outs and 'vout' not in SKIP:
                        i2 = cnt["t"] % 2
                        cnt["t"] += 1
                        cp("dve", vf[i2].ap, ps.ap, [ps], [vf[i2]])
                        for j, u in outs:
                            sp_, r_, st = unit(u, d)
                            rel = st - (NTOK - W)
                            dma(nkv_p[g][sl(rel, d), 1, hp * 128:(hp + 1) * 128], vf[i2].ap[:, j * 128:(j + 1) * 128], reads=[vf[i2]])
                else:
                    act(Vaug4[:32, 32, :, 0:64], ps.ap[:32, 0:128].rearrange("p (h c) -> p h c", h=2), AF.Copy, [ps], [Vaug])
                    i2 = cnt["t"] % 2
                    cnt["t"] += 1
                    cp("dve", vf[i2].ap[:32, 0:128], ps.ap[:32, 0:128], [ps], [vf[i2]])
                    dma(nkv_s[g][:, 1, hp * 128:(hp + 1) * 128], vf[i2].ap[:32, 0:128], reads=[vf[i2]])
                    cp("pool", vs4[:32, hp, g, :].rearrange("p (h c) -> p h c", h=2), Vaug4[:32, 32, :, 0:64], [Vaug], [vs_all])
            if DEBUG and hp == 0 and g == 1:
                dump("qn", qn.ap, [128, NT], [qn], BF16)
                dump("kn", kn.ap, [128, NT], [kn], BF16)
                dump("Vaug", Vaug.ap, [128, 33 * 256], [Vaug], BF16)
            if STAGE < 1:
                return
            def att_scores(hh, uu):
                hs = slice(64 * hh, 64 * hh + 64)
                psS = PS("mm")
                P = Pb[cnt["p"] % 4]
                cnt["p"] += 1
                for j in range(2):
                    u = uu + j
                    sp_, r_, st = unit(u, d)
                    qap = qn.ap[hs, sl(st, d)]
                    if sp_ > 0:
                        mm(psS.ap[:, j * 256:j * 256 + 128], kn.ap[hs, sl(st - span, d)], qap, True, True, [qn, kn], [psS])
                    mm(psS.ap[:, j * 256 + 128:j * 256 + 256], kn.ap[hs, sl(st, d)], qap, True, True, [qn, kn], [psS])
                act(P.ap, psS.ap, AF.Exp, [psS], [P])
                if g == 2 or (g == 1 and uu < 20):
                    mv = 2
                elif g == 0 and uu == 16:
                    mv = 1
                else:
                    mv = 0
                tt("pool", P.ap, P.ap, mp3[:, mv, :], ALU.mult, [P, mpair], [P])
                return P

            def att_pv(hh, u0, uu, P, po):
                for j in range(2):
                    u = uu + j
                    sp_, r_, st = unit(u, d)
                    jj = u - u0
                    if sp_ > 0:
                        mm(po.ap[:, jj * 128:(jj + 1) * 128], Vaug4[:, u - d, hh, :], P.ap[:, j * 256:j * 256 + 128], True, False, [Vaug, P], [po])
                    mm(po.ap[:, jj * 128:(jj + 1) * 128], Vaug4[:, u, hh, :], P.ap[:, j * 256 + 128:j * 256 + 256], sp_ == 0, True, [Vaug, P], [po])

            def att_evac(hh, u0, po):
                Uh = U[hh]
                if g == 0:
                    cp("dve", Uh.ap[:, u0 * 128 - 2048:(u0 + 4) * 128 - 2048], po.ap, [po], [Uh])
                else:
                    sp_, r0, _ = unit(u0, d)
                    uv = Uh.ap[:, sp_ * span - 2048:(sp_ + 1) * span - 2048].rearrange("p (i r) -> p r i", r=d)[:, r0:r0 + 4, :]
                    tt("dve", uv, v3(po.ap, a=4), uv, ALU.add, [po, Uh], [Uh])

            tasks = [(hh, u0, uu) for hh in range(2) for u0 in range(16, 32, 4) for uu in (u0, u0 + 2)]
            pendq = []
            po = None
            for ti_, tk_ in enumerate(tasks + [None, None]):
                if mid_hook and ti_ % 2 == 0:
                    mid_hook.pop(0)()
                if tk_ is not None:
                    pendq.append((tk_[0], tk_[1], tk_[2], att_scores(tk_[0], tk_[2])))
                if pendq and (len(pendq) > 2 or tk_ is None):
                    ph, pu0, puu, pP = pendq.pop(0)
                    if puu == pu0:
                        po = PS("pv")
                    att_pv(ph, pu0, puu, pP, po)
                    if puu == pu0 + 2:
                        att_evac(ph, pu0, po)

        def finalize_steps(hp):
            return [(lambda tc=tc: finalize_chunk(hp, tc)) for tc in range(4, 9)]

        def finalize_chunk(hp, tc):
            if True:
                ps = PS("mm")
                N, t0 = inproj_fm(wbg, 0, tc, ps)
                i2 = cnt["t"] % 2
                cnt["t"] += 1
                if tc == 8:
                    act(sgs3[:, hp, :], ps.ap[:, :32], AF.Silu, [ps], [sgs])
                    return
                act(sg[i2].ap, ps.ap, AF.Silu, [ps], [sg[i2]])
                cs = slice(t0 - 2048, t0 - 2048 + 512)
                cp("pool", numt[i2].ap[0:64], U[0].ap[0:64, cs], [U[0]], [numt[i2]])
                cp("dve", numt[i2].ap[64:128], U[1].ap[0:64, cs], [U[1]], [numt[i2]])
                cp("dve", dent[i2].ap[0:64], U[0].ap[64:128, cs], [U[0]], [dent[i2]])
                cp("pool", dent[i2].ap[64:128], U[1].ap[64:128, cs], [U[1]], [dent[i2]])
                act(dent[i2].ap, dent[i2].ap, AF.Ln, [dent[i2]], [dent[i2]])
                act(dent[i2].ap, dent[i2].ap, AF.Exp, [dent[i2]], [dent[i2]], scale=-1.0)
                tt("dve", numt[i2].ap, numt[i2].ap, dent[i2].ap, ALU.mult, [numt[i2], dent[i2]], [numt[i2]])
                tt("dve", ogc[i2].ap, numt[i2].ap, sg[i2].ap, ALU.mult, [numt[i2], sg[i2]], [ogc[i2]])
                dma(ogd[hp, :, cs], ogc[i2].ap, reads=[ogc[i2]])
                if DEBUG and hp == 0 and tc == 3:
                    dump("ogc", ogc[i2].ap, [128, 512], [ogc[i2]], BF16)


        jobs = [(hp, g) for hp in range(npairs) for g in (0, 1, 2, 3)]
        wslot = {}

        def issue_load(ji):
            hp, g = jobs[ji]
            ws = wst[ji % 2]
            if g < 3:
                nload = sum(1 for (h2, g2) in jobs[:ji] if g2 < 3)
                wb = wbf[nload % 2]
                for s_ in range(3):
                    c0 = g * 3072 + s_ * 1024 + hp * 128
                    dma(w3(ws)[:, :, s_ * 128:(s_ + 1) * 128], a_win[:, c0:c0 + 128].rearrange("(k p) c -> p k c", p=128), writes=[ws])
                cp("pool", wb.ap, ws.ap, [ws], [wb])
                wslot[ji] = wb
            else:
                c0 = 9216 + hp * 128
                dma(w3(ws)[:, :, 0:128], a_win[:, c0:c0 + 128].rearrange("(k p) c -> p k c", p=128), writes=[ws])
                cp("pool", wbg.ap.rearrange("p (k c) -> p k c", k=8), w3(ws)[:, :, 0:128], [ws], [wbg])

        pending_steps = []
        loaded = set()

        def ensure_load(ji):
            if ji < len(jobs) and ji not in loaded:
                loaded.add(ji)
                issue_load(ji)

        ensure_load(0)
        for ji, (hp, g) in enumerate(jobs):
            ensure_load(ji + 1)
            if ji + 1 < len(jobs) and jobs[ji + 1][1] == 3:
                ensure_load(ji + 2)
            if g < 3:
                W, d = GROUPS[g]
                run_group(hp, g, W, d, wslot[ji], pending_steps if g == 0 else None)
            elif STAGE >= 1:
                while pending_steps:
                    pending_steps.pop(0)()
                pending_steps.extend(finalize_steps(hp))
        while pending_steps:
            pending_steps.pop(0)()

        if STAGE >= 3:
            S.barrier()
            A.off = hmark
            if STAGE >= 4:
                top0 = A.n - (8 * 5152 + 16 * 1024) // 2
                Wss = Tile(A.t[:, top0:top0 + 8 * 5152 // 2].bitcast(BF16))
                Ws3 = Wss.ap.rearrange("p (k c) -> p k c", k=8)
                wos = Tile(A.t[:, top0 + 8 * 5152 // 2:A.n].bitcast(BF16))
                wos3 = wos.ap.rearrange("p (h c) -> p h c", h=16)

                def pdma(out, in_, reads=(), writes=()):
                    return S.op("pool", lambda e: e.dma_start(out=out, in_=in_), reads=reads, writes=writes, dma=True)

                for kc in range(8):
                    pdma(Ws3[:, kc, :], s_win[kc * 128:(kc + 1) * 128, :], writes=[Wss])
            kvb = [A.tile(2048) for _ in range(2)]
            vcb = [A.tile(1024, BF16) for _ in range(2)]
            kcT = [A.tile(1024, BF16) for _ in range(2)]
            qbd = A.tile(8 * 3 * 16, BF16)
            qbd4 = qbd.ap.rearrange("p (h g t) -> p h g t", h=8, g=3)
            zer = A.tile(128, BF16)
            Psm = [A.tile(128, BF16) for _ in range(2)]
            PN = A.tile(384, BF16)
            rden = A.tile(128)
            osb = A.tile(128)
            mset("dve", qbd.ap, 0.0, [qbd])
            mset("dve", zer.ap, 0.0, [zer])
            blocks = []
            for g, (W, d) in enumerate(GROUPS):
                for rp in range(min(d, 8)):
                    blocks.append((g, d, rp))
            for b in range(4):
                cp("dve", qbd4[0:64, :, :, 0:8], qs4[0:64, :, :, b * 8:(b + 1) * 8], [qs_all], [qbd])
                cp("dve", qbd4[64:128, :, :, 8:16], qs4[64:128, :, :, b * 8:(b + 1) * 8], [qs_all], [qbd])
                psA = PS("pv")
                mm(psA.ap[:, 0:256], zer.ap, mask2_b, True, False, [zer, cb], [psA])
                psN = PS("aux")
                for g in range(3):
                    for hp in range(8):
                        mm(psN.ap[:32, g * 128 + hp * 16:g * 128 + hp * 16 + 16], ks4[:, hp, g, :], qbd4[:, hp, g, :], True, True, [ks_all, qbd], [psN])
                act(PN.ap[:32], psN.ap[:32, 0:384], AF.Exp, [psN], [PN])
                mN = maskN_b[:32, b * 48:(b + 1) * 48].rearrange("p (g t) -> p g t", g=3).unsqueeze(2).broadcast_to([32, 3, 8, 16])
                pn4 = PN.ap[:32].rearrange("p (g h t) -> p g h t", g=3, h=8)
                tt("dve", pn4, pn4, mN, ALU.mult, [PN, cb], [PN])
                for bi, (g, d, rp) in enumerate(blocks):
                    kv = kvb[bi % 2]
                    W = GROUPS[g][0]
                    dma(kv.ap, kvc[g][b, rp:rp + 127 * d + 1:d, :, :].rearrange("r s c -> r (s c)"), writes=[kv])
                    vb = vcb[bi % 2]
                    act(vb.ap, kv.ap[:, 1024:2048], AF.Copy, [kv], [vb]) if bi % 2 else cp("dve", vb.ap, kv.ap[:, 1024:2048], [kv], [vb])
                    kT = kcT[bi % 2]
                    for half in range(2):
                        pT = PS("aux")
                        for j in range(4):
                            hp = half * 4 + j
                            tr(pT.ap[:, j * 128:(j + 1) * 128], kv.ap[:, hp * 128:(hp + 1) * 128], identf, [kv, cst], [pT])
                        act(kT.ap[:, half * 512:(half + 1) * 512], pT.ap, AF.Copy, [pT], [kT])
                    psS = PS("mm")
                    for hp in range(8):
                        mm(psS.ap[:, hp * 16:(hp + 1) * 16], kT.ap[:, hp * 128:(hp + 1) * 128], qbd4[:, hp, g, :], True, True, [kT, qbd], [psS])
                    Pm = Psm[bi % 2]
                    act(Pm.ap, psS.ap[:, 0:128], AF.Exp, [psS], [Pm])
                    mS = maskS_b[:, bi * 16:(bi + 1) * 16].unsqueeze(1).broadcast_to([128, 8, 16])
                    tt("dve", v3(Pm.ap, a=8), v3(Pm.ap, a=8), mS, ALU.mult, [Pm, cb], [Pm])
                    for hp in range(8):
                        mm(psA.ap[:, hp * 16:(hp + 1) * 16], vb.ap[:, hp * 128:(hp + 1) * 128], Pm.ap[:, hp * 16:(hp + 1) * 16], False, False, [vb, Pm], [psA])
                    mm(psA.ap[:, 128:256], ones_b, Pm.ap, False, False, [Pm, cb], [psA])
                for g in range(3):
                    for hp in range(8):
                        mm(psA.ap[:, hp * 16:(hp + 1) * 16], vs4[:32, hp, g, :], PN.ap[:32, g * 128 + hp * 16:g * 128 + hp * 16 + 16], False, False, [vs_all, PN], [psA])
                    mm(psA.ap[:, 128:256], ones_b[:32, :], PN.ap[:32, g * 128:(g + 1) * 128], False, g == 2, [PN, cb], [psA])
                S.op("dve", lambda e, psA=psA: e.reciprocal(out=rden.ap, in_=psA.ap[:, 128:256]), reads=[psA], writes=[rden])
                tt("dve", osb.ap, psA.ap[:, 0:128], rden.ap, ALU.mult, [psA, rden], [osb])
                o3 = v3(osb.ap, a=8)
                tt("dve", ogs3[0:64, :, b * 8:(b + 1) * 8], o3[0:64, :, 0:8], sgs3[0:64, :, b * 8:(b + 1) * 8], ALU.mult, [osb, sgs], [ogs])
                tt("dve", ogs3[64:128, :, b * 8:(b + 1) * 8], o3[64:128, :, 8:16], sgs3[64:128, :, b * 8:(b + 1) * 8], ALU.mult, [osb, sgs], [ogs])
            if DEBUG:
                dump("ogs", ogs.ap, [128, 256], [ogs], BF16)

            S.barrier()
            A.off = hmark
            woutb = A.tile(8 * 1024, BF16)
            S.op("pool", lambda e: e.dma_start(out=woutb.ap.rearrange("p (h c) -> p h c", h=8), in_=a_wout.rearrange("(h p) c -> p h c", p=128)),
                 writes=[woutb], dma=True)
            wo3 = woutb.ap.rearrange("p (h c) -> p h c", h=8)
            ogb = [A.tile(1024, BF16) for _ in range(2)]
            xres = [A.tile(1024) for _ in range(2)]
            x1t = [A.tile(1024) for _ in range(2)]
            def op_loads(blk):
                i2 = blk % 2
                if blk < 32:
                    dma(v3(ogb[i2].ap, a=8), ogd[:, :, (blk - 16) * 128:(blk - 15) * 128].rearrange("h p t -> p h t"), writes=[ogb[i2]])
                    dma(xres[i2].ap, xp[blk * 128:(blk + 1) * 128, :], writes=[xres[i2]])
                else:
                    dma(xres[i2].ap[:32], xs, writes=[xres[i2]])

            op_loads(16)
            for blk in range(16, 33):
                rows = 128 if blk < 32 else 32
                i2 = blk % 2
                if blk + 1 < 33:
                    op_loads(blk + 1)
                if blk < 32:
                    og3 = v3(ogb[i2].ap, a=8)
                    rd_og = [ogb[i2]]
                else:
                    og3 = ogs3
                    rd_og = [ogs]
                for ncol in range(2):
                    ps = PS("mm")
                    for hp in range(8):
                        mm(ps.ap[:rows, :], og3[:, hp, :rows], wo3[:, hp, ncol * 512:(ncol + 1) * 512], hp == 0, hp == 7, rd_og + [woutb], [ps])
                    tt("dve", x1t[i2].ap[:rows, ncol * 512:(ncol + 1) * 512], ps.ap[:rows, :], xres[i2].ap[:rows, ncol * 512:(ncol + 1) * 512], ALU.add,
                       [ps, xres[i2]], [x1t[i2]])
                dma(x1d[(blk - 16) * 128:(blk - 16) * 128 + rows, :], x1t[i2].ap[:rows], reads=[x1t[i2]])
                if STAGE == 3:
                    if blk < 32:
                        dma(y_p[(blk - 16) * 128:(blk - 15) * 128, :], x1t[i2].ap, reads=[x1t[i2]])
                    else:
                        dma(y_s, x1t[i2].ap[:32], reads=[x1t[i2]])

        if STAGE >= 4:
            S.barrier()
            A.off = hmark
            AX = mybir.AxisListType.X
            roles["aux"] = [4, 5, 6, 7]
            x1b_pre = [A.tile(1024) for _ in range(2)]
            prm = A.tile(8 + 96 + 24 + 32 * 3 + 16 + 4)
            normw2 = prm.ap[:, 0:8]
            cw = prm.ap[:, 8:104]
            cbias = prm.ap[:, 104:128]
            dtb = prm.ap[:, 128:160]
            Aneg = prm.ap[:, 160:192]
            Drep = prm.ap[:, 192:224]
            gnT = prm.ap[:, 224:240]
            one1 = prm.ap[:, 240:241]
            eps5 = prm.ap[:, 241:242]
            for dst, src in ((normw2, s_normw), (cw, s_cw), (cbias, s_cb), (dtb, s_dtb), (Aneg, s_alog), (Drep, s_D), (gnT, s_gn)):
                dma(dst, src, writes=[prm])
            act(Aneg, Aneg, AF.Exp, [prm], [prm])
            ts("dve", Aneg, Aneg, -1.0, ALU.mult, [prm], [prm])
            mset("dve", one1, 1.0, [prm])
            mset("dve", eps5, 1e-5, [prm])
            triM = cview("tri")

            x1b = x1b_pre
            jk = A.tile(2048)
            xn2 = Tile(jk.ap[:, 1024:2048], jk.buf)
            yout = [Tile(jk.ap[:, 0:1024], jk.buf), Tile(jk.ap[:, 1024:2048], jk.buf)]
            sm2 = A.tile(8)
            hT2 = A.tile(8 * 128, BF16)
            hT23 = hT2.ap.rearrange("p (k t) -> p k t", k=8)
            xpre = A.tile(24 * 131)
            xpre3 = xpre.ap.rearrange("p (s t) -> p s t", s=24)
            blockA = A.alloc(3072)
            yacc = Tile(blockA[:, 0:2048])
            Tt = Tile(blockA[:, 2048:3072])
            Ee = Tt
            cacc3 = blockA.rearrange("p (s t) -> p s t", s=24)
            caccT = [yacc, Tt]
            cslab = [Tile(None) for _ in range(24)]
            xbc_w = A.alloc(1536)
            xbcT = Tile(xbc_w.bitcast(BF16))
            xbc3 = xbcT.ap.rearrange("p (s t) -> p s t", s=24)
            ynT = Tile(xbc_w[:, 0:1024].bitcast(BF16), xbcT.buf)
            ynT3 = ynT.ap.rearrange("p (s t) -> p s t", s=16)
            blockB = A.alloc(2048)
            xtok = Tile(blockB[:, 0:1024].bitcast(BF16))
            xdt = Tile(blockB[:, 1024:2048].bitcast(BF16))
            szT = [xtok, xdt]
            sz_ap = blockB
            yb = A.tile(2048, BF16)
            xdte = Tile(yb.ap, yb.buf)
            Btok = A.tile(512, BF16)
            Hf = A.tile(2048)
            Hb = A.tile(2048, BF16)
            dts = A.tile(32 * 8)
            wreg = A.t[:, top0 + 8 * 5152 // 2:A.n]
            xbcT_b = Tile(wreg[:, 0:1536].bitcast(BF16))
            dts_b = Tile(wreg[:, 1536:1792])
            XBC = [xbcT, xbcT_b]
            DTS = [dts, dts_b]
            aT = A.tile(128)
            CBm = A.tile(128)
            MT2 = [Tile(jk.ap[:, 1536:2048].bitcast(BF16)), Tile(jk.ap[:, 0:512].bitcast(BF16))]
            tmpg = Tile(jk.ap[:, 1024:1536])
            CBm2 = [CBm, Tile(jk.ap[:, 512:640])]
            ssd_t = MT2 + [tmpg, CBm2[1]]
            assert A.off <= top0, ("L1 tiles overlap resident weights", A.off, top0)
            mset("dve", xpre.ap, 0.0, [xpre])
            mset("dve", Hf.ap, 0.0, [Hf])
            mset("pool", Hb.ap, 0.0, [Hb])

            def bf_bank(t):
                return t.ap.bitcast(BF16)

            def state_out(dst):
                for q4 in range(4):
                    pT = PS("aux")
                    for j in range(4):
                        t_ = q4 * 4 + j
                        tr(pT.ap[:, j * 128:(j + 1) * 128], Hf.ap[:, t_ * 128:(t_ + 1) * 128], identf, [Hf, cst], [pT])
                    act(jk.ap[:, q4 * 512:(q4 + 1) * 512], pT.ap, AF.Copy, [pT], [jk])
                dma(dst.rearrange("(t p) n -> p t n", p=128), v3(jk.ap, a=16), reads=[jk])

            def state_in(src):
                dma(v3(jk.ap, a=16), src.rearrange("(t p) n -> p t n", p=128), writes=[jk])
                for q4 in range(4):
                    pT = PS("aux")
                    for j in range(4):
                        t_ = q4 * 4 + j
                        tr(pT.ap[:, j * 128:(j + 1) * 128], jk.ap[:, t_ * 128:(t_ + 1) * 128], identf, [jk, cst], [pT])
                    act(Hf.ap[:, q4 * 512:(q4 + 1) * 512], pT.ap, AF.Copy, [pT], [Hf])
                cp("pool", Hb.ap, Hf.ap, [Hf], [Hb])

            def nc_dma(out, in_, reads=(), writes=()):
                return S.op("sp", lambda e: e.dma_start(out=out, in_=in_, allow_slow_non_contiguous=True), reads=reads, writes=writes)

            def l1_block(tok0, L, ydst, bi, pre=False, allproj=False, save=None, reuse=None, par=0):
                xbcT = XBC[par]
                xbc3 = xbcT.ap.rearrange("p (s t) -> p s t", s=24)
                dts = DTS[par]

                def dsl(i):
                    return dts.ap[:, i * 32:(i + 1) * 32]
                NSL = 20 if pre else 24
                NPJ = 24 if (allproj or not pre) else 20
                SL0 = 20 if reuse is not None else 0
                x1 = x1b[bi % 2]
                dma(x1.ap[:L], x1d[tok0:tok0 + L, :], writes=[x1])
                tt("dve", jk.ap[:L, 0:1024], x1.ap[:L], x1.ap[:L], ALU.mult, [x1], [jk])
                S.op("dve", lambda e: e.tensor_reduce(out=sm2.ap[:L, 0:1], in_=jk.ap[:L, 0:1024], axis=AX, op=ALU.add), reads=[jk], writes=[sm2])
                act(sm2.ap[:L, 1:2], sm2.ap[:L, 0:1], AF.Ln, [sm2, eps6], [sm2], bias=eps6.ap[:L], scale=1.0 / 1024)
                act(sm2.ap[:L, 2:3], sm2.ap[:L, 1:2], AF.Exp, [sm2], [sm2], scale=-0.5)
                act(xn2.ap[:L], x1.ap[:L], AF.Copy, [x1, sm2], [xn2], scale=sm2.ap[:L, 2:3])
                for half in range(2):
                    pb = PS("aux")
                    for k4 in range(4):
                        kc = half * 4 + k4
                        tr(pb.ap[:, k4 * 128:k4 * 128 + L], xn2.ap[:L, kc * 128:(kc + 1) * 128], identf[:L, :L], [xn2, cst], [pb])
                    tt("dve", hT23[:, half * 4:half * 4 + 4, :L], v3(pb.ap, a=4)[:, :, :L],
                       normw2[:, half * 4:half * 4 + 4].unsqueeze(2).broadcast_to([128, 4, L]), ALU.mult, [pb, prm], [hT2])
                if reuse is None:
                    ps = PS("aux")
                    for kc in range(8):
                        mm(ps.ap[:L, 0:32], hT23[:, kc, :L], Ws3[:, kc, 5120:5152], kc == 0, kc == 7, [hT2, Wss], [ps])
                    tt("dve", dsl(0)[:L], ps.ap[:L, 0:32], dtb[:L], ALU.add, [ps, prm], [dts])
                for s4 in range((0 if allproj else SL0 // 4), NPJ // 4):
                    ps = PS("mm")
                    for j in range(4):
                        slab = s4 * 4 + j
                        for kc in range(8):
                            mm(ps.ap[:, j * 128:j * 128 + L], Ws3[:, kc, 2048 + slab * 128:2048 + (slab + 1) * 128], hT23[:, kc, :L],
                               kc == 0, kc == 7, [hT2, Wss], [ps])
                    act(xpre3[:, s4 * 4:s4 * 4 + 4, 3:3 + L], v3(ps.ap, a=4)[:, :, :L], AF.Copy, [ps], [xpre])
                if reuse is None:
                    act(dsl(1)[:L], dsl(0)[:L], AF.Abs, [dts], [dts])
                    act(dsl(1)[:L], dsl(1)[:L], AF.Exp, [dts], [dts], scale=-1.0)
                    act(dsl(1)[:L], dsl(1)[:L], AF.Ln, [dts, prm], [dts], bias=one1[:L], scale=1.0)
                    stt(dsl(2)[:L], dsl(0)[:L], 0.0, dsl(1)[:L], ALU.max, ALU.add, [dts], [dts])
                    tt("dve", dsl(3)[:L], dsl(2)[:L], Aneg[:L], ALU.mult, [dts, prm], [dts])
                    ps = PS("aux")
                    mm(ps.ap[:L, 0:32], triM[:L, :L], dsl(3)[:L], True, True, [cst, dts], [ps])
                    cp("dve", dsl(4)[:L], ps.ap[:L, 0:32], [ps], [dts])
                    act(dsl(6)[:L], dsl(4)[:L], AF.Exp, [dts], [dts])
                    ps = PS("aux")
                    mm(ps.ap[:, 0:32], identf[:L, L - 1:L].broadcast_to([L, 128]), dsl(4)[:L], True, True, [cst, dts], [ps])
                    cp("dve", dsl(7), ps.ap[:, 0:32], [ps], [dts])
                    tt("dve", dsl(5)[:L], dsl(7)[:L], dsl(4)[:L], ALU.subtract, [dts], [dts])
                    act(dsl(5)[:L], dsl(5)[:L], AF.Exp, [dts], [dts])
                    act(dsl(7), dsl(7), AF.Exp, [dts], [dts])
                    ps = PS("aux")
                    tr(ps.ap[:32, :L], dsl(4)[:L], identf[:L, :L], [dts, cst], [ps])
                    cp("dve", aT.ap[:32, :L], ps.ap[:32, :L], [ps], [aT])
                else:
                    dma(dts.ap, dtsd[reuse], writes=[dts])
                    dma(aT.ap[:32, :128], aTd[reuse], writes=[aT])
                if save is not None:
                    dma(dtsd[save], dts.ap, reads=[dts])
                    dma(aTd[save], aT.ap[:32, :128], reads=[aT])
                S.touch("dve", caccT)
                S.touch("act", caccT)
                if L <= 32:
                    S.touch("dve", [jk])
                    cw3 = cw.rearrange("p (j s) -> p j s", j=4)
                    accv = cacc3[:, :, :L]
                    tmpv = tmpg.ap[:, 0:24 * L].rearrange("p (s t) -> p s t", s=24)
                    allc = cslab + caccT
                    tt("dve", accv, xpre3[:, :, 0:L], cw3[:, 0, :].unsqueeze(2).broadcast_to([128, 24, L]), ALU.mult, [xpre, prm], allc)
                    tt("dve", accv, accv, cbias.unsqueeze(2).broadcast_to([128, 24, L]), ALU.add, [prm] + allc, allc)
                    for j in range(1, 4):
                        tt("dve", tmpv, xpre3[:, :, j:j + L], cw3[:, j, :].unsqueeze(2).broadcast_to([128, 24, L]), ALU.mult, [xpre, prm], [tmpg])
                        tt("dve", accv, accv, tmpv, ALU.add, [tmpg] + allc, allc)
                else:
                    for j in range(4):
                        for slab in range(SL0, NSL):
                            cs_ = cslab[slab]
                            if j == 0:
                                act(cacc3[:, slab, :L], xpre3[:, slab, 0:L], AF.Identity, [xpre, prm], [cs_], bias=cbias[:, slab:slab + 1], scale=cw[:, slab:slab + 1])
                            else:
                                stt(cacc3[:, slab, :L], xpre3[:, slab, j:j + L], cw[:, j * 24 + slab:j * 24 + slab + 1], cacc3[:, slab, :L],
                                    ALU.mult, ALU.add, [xpre, prm, cs_], [cs_])
                if reuse is not None:
                    dma(xbc3[:, 0:20, :], xbs[reuse].rearrange("p (s t) -> p s t", s=20), writes=[xbcT])
                act(xbc3[:, SL0:NSL, :L], cacc3[:, SL0:NSL, :L], AF.Silu, cslab + caccT, [xbcT])
                if save is not None:
                    dma(xbs[save].rearrange("p (s t) -> p s t", s=20), xbc3[:, 0:20, :], reads=[xbcT])
                cp("dve", xpre3[:, :, 0:3], xpre3[:, :, L:L + 3], [xpre], [xpre])
                if pre:
                    yield
                for half in range(2):
                    pb = PS("aux")
                    pv_ = bf_bank(pb)
                    for j in range(8):
                        slab = half * 8 + j
                        tr(pv_[:L, j * 128:(j + 1) * 128], xbc3[:, slab, :L], identb, [xbcT, cb], [pb])
                    act(xtok.ap[:L, half * 1024:(half + 1) * 1024], pv_[:L, :], AF.Copy, [pb], [xtok])
                pb = PS("aux")
                pv_ = bf_bank(pb)
                for gi in range(4):
                    tr(pv_[:L, gi * 128:(gi + 1) * 128], xbc3[:, 16 + gi, :L], identb, [xbcT, cb], [pb])
                act(Btok.ap[:L], pv_[:L, 0:512], AF.Copy, [pb], [Btok])
                x3 = xtok.ap[:L].rearrange("p (h c) -> p h c", h=32)
                if pre:
                    tt("dve", dsl(1)[:L], dsl(2)[:L], dsl(5)[:L], ALU.mult, [dts], [dts])
                    tt("dve", xdte.ap[:L].rearrange("p (h c) -> p h c", h=32), x3, dsl(1)[:L].unsqueeze(2).broadcast_to([L, 32, 64]), ALU.mult,
                       [xtok, dts], [xdte])
                else:
                    tt("dve", xdt.ap[:L].rearrange("p (h c) -> p h c", h=32), x3, dsl(2)[:L].unsqueeze(2).broadcast_to([L, 32, 64]), ALU.mult,
                       [xtok, dts], [xdt])
                    tt("dve", xdte.ap[:L].rearrange("p (h c) -> p h c", h=32), xdt.ap[:L].rearrange("p (h c) -> p h c", h=32),
                       dsl(5)[:L].unsqueeze(2).broadcast_to([L, 32, 64]), ALU.mult, [xdt, dts], [xdte])
                g3 = lambda ap: ap.rearrange("p (h c) -> p h c", h=8)
                S.touch("dve", [jk])
                S.touch("act", [jk])
                S.touch("pe", [jk])

                def ssd_stage1(gi):
                    BT = xbc3[:, 16 + gi, :L]
                    CT = xbc3[:, 20 + gi, :L]
                    cbm = CBm2[gi % 2]
                    mt = MT2[gi % 2]
                    ps = PS("aux")
                    mm(ps.ap[:L, :L], BT, CT, True, True, [xbcT], [ps])
                    tt("dve", cbm.ap[:L, :L], ps.ap[:L, :L], triM[:L, :L], ALU.mult, [ps, cst], [cbm])
                    for half in range(2):
                        pe_ = PS("mm")
                        for j in range(4):
                            h = gi * 8 + half * 4 + j
                            mm(pe_.ap[:L, j * 128:j * 128 + L], identf[0:32, h:h + 1].broadcast_to([32, L]), aT.ap[:32, :L], True, True, [cst, aT], [pe_])
                        h0 = gi * 8 + half * 4
                        tt("dve", v3(Tt.ap[:L, half * 512:(half + 1) * 512], a=4)[:, :, :L], v3(pe_.ap[:L, :], a=4)[:, :, :L],
                           dsl(4)[:L, h0:h0 + 4].unsqueeze(2).broadcast_to([L, 4, L]), ALU.subtract, [pe_, dts], [Tt])
                    T3 = v3(Tt.ap, a=8)[:L, :, :L]
                    M3 = v3(mt.ap, a=8)[:L, :, :L]
                    tt("dve", T3, T3, triM[:L, :L].unsqueeze(1).broadcast_to([L, 8, L]), ALU.mult, [Tt, cst], [Tt])
                    act(T3, T3, AF.Exp, [Tt], [Tt])
                    tt("dve", M3, T3, cbm.ap[:L, :L].unsqueeze(1).broadcast_to([L, 8, L]), ALU.mult, [Tt, cbm], [mt])

                def ssd_stage2(gi):
                    CT = xbc3[:, 20 + gi, :L]
                    mt = MT2[gi % 2]
                    if not pre:
                        psY = PS("mm")
                        for j in range(8):
                            h = gi * 8 + j
                            mm(psY.ap[:L, j * 64:(j + 1) * 64], v3(mt.ap, a=8)[:L, j, :L], xdt.ap[:L, h * 64:(h + 1) * 64], True, True, [mt, xdt], [psY])
                        psO = PS("mm")
                        mm(psO.ap[:L, :], CT, Hb.ap[:, gi * 512:(gi + 1) * 512], True, True, [xbcT, Hb], [psO])
                        tt("dve", g3(tmpg.ap[:L]), g3(psO.ap[:L, :]), dsl(6)[:L, gi * 8:(gi + 1) * 8].unsqueeze(2).broadcast_to([L, 8, 64]), ALU.mult,
                           [psO, dts], [tmpg])
                        yg = yacc.ap[:L, gi * 512:(gi + 1) * 512]
                        tt("dve", yg, tmpg.ap[:L], psY.ap[:L, :], ALU.add, [tmpg, psY], [yacc])
                        tt("dve", g3(tmpg.ap[:L]), g3(xtok.ap[:L, gi * 512:(gi + 1) * 512]), Drep[:L, gi * 8:(gi + 1) * 8].unsqueeze(2).broadcast_to([L, 8, 64]),
                           ALU.mult, [xtok, prm], [tmpg])
                        tt("dve", yg, yg, tmpg.ap[:L], ALU.add, [tmpg, yacc], [yacc])
                    psS = PS("mm")
                    mm(psS.ap[:, :], Btok.ap[:L, gi * 128:(gi + 1) * 128], xdte.ap[:L, gi * 512:(gi + 1) * 512], True, True, [Btok, xdte], [psS])
                    Hg = Hf.ap[:, gi * 512:(gi + 1) * 512]
                    tt("dve", g3(Hg), g3(Hg), dsl(7)[:, gi * 8:(gi + 1) * 8].unsqueeze(2).broadcast_to([128, 8, 64]), ALU.mult,
                       [Hf, dts] + ([] if pre else [psO]), [Hf])
                    tt("dve", Hg, Hg, psS.ap[:, :], ALU.add, [Hf, psS], [Hf])
                    if not pre:
                        cp("pool", Hb.ap[:, gi * 512:(gi + 1) * 512], Hg, [Hf], [Hb])

                if pre:
                    for gi in range(4):
                        ssd_stage2(gi)
                else:
                    ssd_stage1(0)
                    for gi in range(4):
                        if gi + 1 < 4:
                            ssd_stage1(gi + 1)
                        ssd_stage2(gi)
                if pre:
                    return
                for c in range(4):
                    ps = PS("mm")
                    for kc in range(8):
                        mm(ps.ap[:L, :], hT23[:, kc, :L], Ws3[:, kc, c * 512:(c + 1) * 512], kc == 0, kc == 7, [hT2, Wss], [ps])
                    act(sz_ap[:L, c * 512:(c + 1) * 512], ps.ap[:L, :], AF.Silu, [ps], szT)
                tt("dve", yacc.ap[:L], yacc.ap[:L], sz_ap[:L], ALU.mult, [yacc] + szT, [yacc])
                tt("dve", jk.ap[:L], yacc.ap[:L], yacc.ap[:L], ALU.mult, [yacc], [jk] + ssd_t)
                S.op("dve", lambda e: e.tensor_reduce(out=sm2.ap[:L, 4:8], in_=v3(jk.ap, a=4)[:L], axis=AX, op=ALU.add), reads=[jk], writes=[sm2])
                act(sm2.ap[:L, 4:8], sm2.ap[:L, 4:8], AF.Ln, [sm2, prm], [sm2], bias=eps5[:L], scale=1.0 / 512)
                act(sm2.ap[:L, 4:8], sm2.ap[:L, 4:8], AF.Exp, [sm2], [sm2], scale=-0.5)
                tt("dve", v3(yb.ap, a=4)[:L], v3(yacc.ap, a=4)[:L], sm2.ap[:L, 4:8].unsqueeze(2).broadcast_to([L, 4, 512]), ALU.mult, [yacc, sm2], [yb])
                for half in range(2):
                    pb = PS("aux")
                    pv_ = bf_bank(pb)
                    for j in range(8):
                        slab = half * 8 + j
                        tr(pv_[:, j * 128:j * 128 + L], yb.ap[:L, slab * 128:(slab + 1) * 128], identb[:L, :L], [yb, cb], [pb])
                    tt("dve", ynT3[:, half * 8:half * 8 + 8, :L], v3(pv_, a=8)[:, :, :L], gnT[:, half * 8:half * 8 + 8].unsqueeze(2).broadcast_to([128, 8, L]),
                       ALU.mult, [pb, prm], [ynT])
                yo = yout[bi % 2]
                for ncol in range(2):
                    ps = PS("mm")
                    for slab in range(16):
                        mm(ps.ap[:L, :], ynT3[:, slab, :L], wos3[:, slab, ncol * 512:(ncol + 1) * 512], slab == 0, slab == 15, [ynT, wos], [ps])
                    tt("dve", yo.ap[:L, ncol * 512:(ncol + 1) * 512], ps.ap[:L, :], x1.ap[:L, ncol * 512:(ncol + 1) * 512], ALU.add, [ps, x1], [yo])
                dma(ydst, yo.ap[:L], reads=[yo])

            nblk = 16 if STAGE >= 5 else 2
            def run(gen):
                for _ in gen:
                    pass

            gens = [l1_block(blk * 128, 128, None, blk, pre=True, allproj=(blk == nblk - 1), save=blk, par=blk % 2) for blk in range(nblk)]
            next(gens[0])
            for blk in range(nblk):
                if blk + 1 < nblk:
                    next(gens[blk + 1])
                run(gens[blk])
            for sl_ in range(0, 16, 4):
                pdma(wos3[:, sl_:sl_ + 4, :], s_wout[sl_ * 128:(sl_ + 4) * 128, :].rearrange("(h p) c -> p h c", p=128), writes=[wos, xbcT_b, dts_b])
            ccin_t = Tile(None)
            dma(cc_in[:, 0:2048], Hf.ap, reads=[Hf], writes=[ccin_t])
            dma(cc_in[:, 2048:2120].rearrange("p (s j) -> p s j", s=24), xpre3[:, :, 0:3], reads=[xpre, ccin_t], writes=[ccin_t])
            cct = Tile(None)
            S.op("pool", lambda e: e.collective_compute("AllGather", ALU.bypass, replica_groups=[[0, 1], [2, 3], [4, 5], [6, 7]],
                                                         ins=[cc_in], outs=[cc_out]), reads=[ccin_t], writes=[cct])
            dma(Hf.ap, cc_out[0:128, 0:2048], reads=[cct], writes=[Hf])
            dma(xpre3[:, :, 0:3], cc_out[0:128, 2048:2120].rearrange("p (s j) -> p s j", s=24), reads=[cct], writes=[xpre])
            ts("dve", Hf.ap, Hf.ap, flagt.ap[:, 0:1], ALU.mult, [Hf, flagt], [Hf])
            ts("dve", xpre3[:, :, 0:3], xpre3[:, :, 0:3], flagt.ap[:, 0:1], ALU.mult, [xpre, flagt], [xpre])
            cp("pool", Hb.ap, Hf.ap, [Hf], [Hb])
            for blk in range(nblk):
                run(l1_block(blk * 128, 128, y_p[blk * 128:(blk + 1) * 128, :], blk, reuse=(blk if blk >= 1 else None), allproj=(blk == nblk - 1)))
            for j in range(3):
                nc_dma(nconv_p[j].rearrange("(s p) -> p s", p=128), xpre3[:, :, j], reads=[xpre])
            state_out(nssm_p)
            for b in range(4):
                for j in range(3):
                    nc_dma(xpre3[:, :, j], sconv[b, j].rearrange("(s p) -> p s", p=128), writes=[xpre])
                state_in(sssm[b])
                run(l1_block(2048 + b * 8, 8, y_s[b * 8:(b + 1) * 8, :], b))
                for j in range(3):
                    nc_dma(nconv_s[b, j].rearrange("(s p) -> p s", p=128), xpre3[:, :, 8 + j], reads=[xpre])
                state_out(nssm_s[b])

        S.barrier()
        blockc = es.enter_context(nc.Block())
        S.emit(blockc)
    print("sched: instr counts", {k: len(v) for k, v in S.prog.items()}, "waits", S.nwait, "arena", A.off)
    return nc, cpack, dbg


def prep_inputs(inp, core):
    b = core // 2
    half = core % 2
    m = {}
    if half == 1:
        m["xp"] = np.ascontiguousarray(inp["x_prompt"][b])
    else:
        m["xp"] = np.concatenate([np.zeros((2048, DM), np.float32), inp["x_prompt"][b][:2048]], 0)
    m["flag"] = np.full((128, 1), float(half), np.float32)
    m["xs"] = np.ascontiguousarray(inp["x_sample"][4 * core:4 * core + 4].reshape(32, 1024))
    for g, key in enumerate(["cache_kv_g0", "cache_kv_g1", "cache_kv_g2"]):
        m["kv%d" % g] = np.ascontiguousarray(inp[key][0, 4 * core:4 * core + 4].reshape(4, GROUPS[g][0], 2, 1024))
    m["a_normw"] = np.ascontiguousarray(inp["attn_norm"][0].reshape(8, 128).T)
    m["a_win"] = np.ascontiguousarray(inp["attn_w_in"][0])
    gq = inp["attn_q_gain"][0]
    gk = inp["attn_k_gain"][0]
    gc = np.zeros((128, 6), np.float32)
    for g in range(3):
        gc[:, 2 * g] = np.tile(gq[g], 2)
        gc[:, 2 * g + 1] = np.tile(gk[g], 2)
    m["a_gain"] = gc
    m["a_wout"] = np.ascontiguousarray(inp["attn_w_out"][0])
    m["sconv"] = np.ascontiguousarray(inp["state_conv"][0, 4 * core:4 * core + 4])
    m["sssm"] = np.ascontiguousarray(inp["state_ssm"][0, 4 * core:4 * core + 4].reshape(4, 2048, 128))
    m["s_normw"] = np.ascontiguousarray(inp["ssm_norm"][0].reshape(8, 128).T)
    m["s_win"] = np.ascontiguousarray(inp["ssm_w_in"][0])
    m["s_cw"] = np.ascontiguousarray(inp["ssm_conv_w"][0].reshape(4, 24, 128).transpose(2, 0, 1).reshape(128, 96))
    m["s_cb"] = np.ascontiguousarray(inp["ssm_conv_b"][0].reshape(24, 128).T)
    m["s_dtb"] = np.ascontiguousarray(np.broadcast_to(inp["ssm_dt_bias"][0][None, :], (128, 32)))
    m["s_alog"] = np.ascontiguousarray(np.broadcast_to(inp["ssm_A_log"][0][None, :], (128, 32)))
    m["s_D"] = np.ascontiguousarray(np.broadcast_to(inp["ssm_D"][0][None, :], (128, 32)))
    m["s_gn"] = np.ascontiguousarray(inp["ssm_gate_norm"][0].reshape(16, 128).T)
    m["s_wout"] = np.ascontiguousarray(inp["ssm_w_out"][0])
    return m


_CACHE = {}


def kernel(**inputs):
    inp = {k: np.asarray(v) for k, v in inputs.items()}
    if "nc" not in _CACHE:
        _CACHE["nc"] = build()
    nc, cpack, dbg = _CACHE["nc"]
    in_maps = []
    ncores = int(os.environ.get('K_CORES', '8'))
    for c in range(ncores):
        m = prep_inputs(inp, c)
        m["consts"] = cpack
        in_maps.append(m)
    res = run_bass_kernel_spmd(nc, in_maps, core_ids=list(range(ncores)))
    R = res.results
    if ncores < 8 or STAGE < 5:
        return R
    f = np.float32
    y_prompt = np.stack([np.concatenate([np.asarray(R[2 * b]["y_p"], f), np.asarray(R[2 * b + 1]["y_p"], f)], 0) for b in range(4)])
    y_sample = np.concatenate([np.asarray(R[c]["y_s"], f).reshape(4, 8, DM) for c in range(8)], 0)
    outs = [y_prompt, y_sample]
    for g in range(3):
        W = GROUPS[g][0]
        outs.append(np.stack([np.asarray(R[2 * b + 1]["nkv%d_p" % g], f).reshape(W, 2, 16, 64) for b in range(4)])[None])
    for g in range(3):
        outs.append(np.concatenate([np.asarray(R[c]["nkv%d_s" % g], f).reshape(4, 8, 2, 16, 64) for c in range(8)], 0)[None])
    outs.append(np.stack([np.asarray(R[2 * b + 1]["nconv_p"], f) for b in range(4)])[None])
    outs.append(np.concatenate([np.asarray(R[c]["nconv_s"], f) for c in range(8)], 0)[None])
    outs.append(np.stack([np.asarray(R[2 * b + 1]["nssm_p"], f).reshape(32, 64, 128) for b in range(4)])[None])
    outs.append(np.concatenate([np.asarray(R[c]["nssm_s"], f).reshape(4, 32, 64, 128) for c in range(8)], 0)[None])
    return tuple(outs)
```

```python
import os
import numpy as np
import ml_dtypes
from contextlib import ExitStack
import concourse.bass as bass
import concourse.mybir as mybir
from concourse.bass_utils import run_bass_kernel_spmd

F32 = mybir.dt.float32
BF16 = mybir.dt.bfloat16
ALU = mybir.AluOpType
AF = mybir.ActivationFunctionType

NTOK = 4096
NSAM = 32
NT = NTOK + NSAM
DM = 1024
GROUPS = ((128, 1), (512, 4), (2048, 16))
NDS = 24


class Buf:
    __slots__ = ("w", "r")

    def __init__(self):
        self.w = None
        self.r = {}


class Tile:
    def __init__(self, ap, buf=None, excl=False):
        self.ap = ap
        self.buf = buf if buf is not None else Buf()
        self.excl = excl


class Sched:
    def __init__(self, nc, es):
        self.nc = nc
        self.keys = ["pe", "act", "dve", "pool", "sp"]
        self.sem = {k: es.enter_context(nc.semaphore("s_" + k)) for k in ["pe", "act", "dve", "pool"]}
        self.cnt = {k: 0 for k in self.sem}
        self.dsem = [es.enter_context(nc.semaphore("d%d" % i)) for i in range(NDS)]
        self.es = es
        self.psem = []
        self.dcnt = [0] * NDS
        self.dnext = 0
        self.seen = {k: {} for k in self.keys}
        self.prog = {k: [] for k in self.keys}
        self.nwait = 0

    def _wait(self, ek, sk, val):
        if self.seen[ek].get(sk, 0) >= val:
            return
        self.seen[ek][sk] = val
        self.prog[ek].append(("w", sk, val))
        self.nwait += 1

    def op(self, ek, fn, reads=(), writes=(), dma=False):
        deps = {}
        for t in reads:
            b = t.buf
            if b.w is not None:
                deps[b.w[0]] = max(deps.get(b.w[0], 0), b.w[1])
            if t.excl:
                for sk, v in b.r.items():
                    if sk != ek:
                        deps[sk] = max(deps.get(sk, 0), v)
        for t in writes:
            b = t.buf
            if b.w is not None:
                deps[b.w[0]] = max(deps.get(b.w[0], 0), b.w[1])
            for sk, v in b.r.items():
                deps[sk] = max(deps.get(sk, 0), v)
        for sk, v in deps.items():
            if sk == ek and ek in ("pe", "pool"):
                continue
            self._wait(ek, sk, v)
        if dma and ek == "pool":
            self.psem.append(self.es.enter_context(self.nc.semaphore("p%d" % len(self.psem))))
            ticket = (("p", len(self.psem) - 1), 16)
        elif ek == "sp" or dma:
            i = self.dnext
            self.dnext = (i + 1) % NDS
            if self.dcnt[i] > 0:
                self._wait(ek, ("d", i), self.dcnt[i] * 16)
            self.dcnt[i] += 1
            ticket = (("d", i), self.dcnt[i] * 16)
        else:
            self.cnt[ek] += 1
            ticket = (ek, self.cnt[ek])
        self.prog[ek].append(("i", fn, ticket))
        for t in reads:
            r = t.buf.r
            r[ticket[0]] = max(r.get(ticket[0], 0), ticket[1])
        for t in writes:
            t.buf.w = ticket
            t.buf.r = {}
        return ticket

    def touch(self, ek, tiles):
        for t in tiles:
            b = t.buf
            if b.w is not None and not (b.w[0] == ek and ek == "pe"):
                self._wait(ek, b.w[0], b.w[1])
            for sk, v in b.r.items():
                self._wait(ek, sk, v)

    def barrier(self):
        for ek in self.keys:
            for k in ["pe", "act", "dve", "pool"]:
                if k != ek and self.cnt[k] > 0:
                    self._wait(ek, k, self.cnt[k])
            for i in range(NDS):
                if self.dcnt[i] > 0:
                    self._wait(ek, ("d", i), self.dcnt[i] * 16)
            for i in range(len(self.psem)):
                self._wait(ek, ("p", i), 16)

    def semof(self, sk):
        if isinstance(sk, str):
            return self.sem[sk]
        return self.dsem[sk[1]] if sk[0] == "d" else self.psem[sk[1]]

    def emit(self, block):
        def run(ek):
            def body(e):
                for it in self.prog[ek]:
                    if it[0] == "w":
                        e.wait_ge(self.semof(it[1]), it[2])
                    else:
                        ins = it[1](e)
                        sk, v = it[2]
                        ins.then_inc(self.semof(sk), 16 if not isinstance(sk, str) else 1)
            return body
        block.tensor(run("pe"))
        block.scalar(run("act"))
        block.vector(run("dve"))
        block.gpsimd(run("pool"))
        block.sync(run("sp"))


class Arena:
    def __init__(self, nc, es, nwords):
        self.t = es.enter_context(nc.sbuf_tensor("arena", [128, nwords], F32))
        self.n = nwords
        self.off = 0

    def alloc(self, nelem, dtype=F32):
        w = nelem if dtype == F32 else (nelem + 1) // 2
        assert self.off + w <= self.n, ("SBUF arena overflow", self.off, w, self.n)
        a = self.t[:, self.off:self.off + w]
        self.off += w
        return a if dtype == F32 else a.bitcast(BF16)

    def tile(self, nelem, dtype=F32):
        return Tile(self.alloc(nelem, dtype))


def host_consts():
    c = {}
    c["ident"] = np.eye(128, dtype=np.float32)
    k = np.arange(128)
    c["bones"] = (k[:, None] // 64 == k[None, :] // 64).astype(np.float32)
    cc = np.arange(256)
    m = np.zeros((128, 256), np.float32)
    m[:, :128] = (cc[None, :128] <= k[:, None])
    m[:, 128:] = ((cc[None, 128:] - 128) >= k[:, None])
    c["mask2"] = m
    c["ones"] = np.ones((128, 128), np.float32)
    c["tri"] = (k[:, None] <= k[None, :]).astype(np.float32)
    ms = np.zeros((128, 13, 16), np.float32)
    bi = 0
    for g, (W, d) in enumerate(GROUPS):
        nres = min(d, 8)
        for rp in range(nres):
            for t in range(8):
                ok = ((t - rp) % d == 0) & (rp + d * k >= t)
                ms[:, bi, t] = ok
                ms[:, bi, 8 + t] = ok
            bi += 1
    c["maskS"] = ms.reshape(128, 13 * 16)
    mn = np.zeros((32, 4, 3, 16), np.float32)
    for b in range(4):
        for g, (W, d) in enumerate(GROUPS):
            for u in range(8):
                for t in range(8):
                    ok = (u <= t) and ((t - u) % d == 0)
                    mn[b * 8 + u, b, g, t] = ok
                    mn[b * 8 + u, b, g, 8 + t] = ok
    mnp = np.zeros((128, 4 * 3 * 16), np.float32)
    mnp[:32] = mn.reshape(32, -1)
    c["maskN"] = mnp
    return c


CONST_ORDER = ["ident", "bones", "mask2", "ones", "maskS", "maskN", "tri"]


def pack_consts():
    c = host_consts()
    offs = {}
    cols = []
    o = 0
    for n in CONST_ORDER:
        offs[n] = (o, c[n].shape[1])
        o += c[n].shape[1]
        cols.append(c[n])
    return np.concatenate(cols, axis=1).astype(np.float32), offs


STAGE = int(os.environ.get("K_STAGE", "99"))
DEBUG = os.environ.get("K_DEBUG", "") != ""
SKIP = os.environ.get("K_SKIP", "").split(",")


def build():
    cpack, coffs = pack_consts()
    NCC = cpack.shape[1]
    nc = bass.Bass("TRN2", target_bir_lowering=False)

    def din(name, shape, dt=F32):
        return nc.dram_tensor(name, list(shape), dt, kind="ExternalInput").ap()

    def dout(name, shape, dt=F32):
        return nc.dram_tensor(name, list(shape), dt, kind="ExternalOutput").ap()

    def dscr(name, shape, dt=F32):
        return nc.dram_tensor(name, list(shape), dt, kind="Internal").ap()

    xp = din("xp", [NTOK, DM])
    xs = din("xs", [NSAM, DM])
    kvc = [din("kv%d" % g, [4, GROUPS[g][0], 2, 1024]) for g in range(3)]
    consts = din("consts", [128, NCC])
    a_normw = din("a_normw", [128, 8])
    a_win = din("a_win", [DM, 10240])
    a_gain = din("a_gain", [128, 6])
    a_wout = din("a_wout", [1024, DM])

    sconv = din("sconv", [4, 3, 3072])
    sssm = din("sssm", [4, 2048, 128])
    s_normw = din("s_normw", [128, 8])
    s_win = din("s_win", [DM, 5152])
    s_cw = din("s_cw", [128, 96])
    s_cb = din("s_cb", [128, 24])
    s_dtb = din("s_dtb", [128, 32])
    s_alog = din("s_alog", [128, 32])
    s_D = din("s_D", [128, 32])
    s_gn = din("s_gn", [128, 16])
    s_wout = din("s_wout", [2048, DM])
    x1d = dscr("x1d", [2048 + NSAM, DM])
    flag_in = din("flag", [128, 1])
    xbs = dscr("xbs", [16, 128, 20 * 128], BF16)
    dtsd = dscr("dtsd", [16, 128, 256])
    aTd = dscr("aTd", [16, 32, 128])
    cc_in = dscr("cc_in", [128, 2048 + 72])
    cc_out = dscr("cc_out", [256, 2048 + 72])
    nconv_p = dout("nconv_p", [3, 3072])
    nconv_s = dout("nconv_s", [4, 3, 3072])
    nssm_p = dout("nssm_p", [2048, 128])
    nssm_s = dout("nssm_s", [4, 2048, 128])
    y_p = dout("y_p", [2048, DM])
    y_s = dout("y_s", [NSAM, DM])
    nkv_p = [dout("nkv%d_p" % g, [GROUPS[g][0], 2, 1024]) for g in range(3)]
    nkv_s = [dout("nkv%d_s" % g, [NSAM, 2, 1024]) for g in range(3)]
    ogd = dscr("ogd", [8, 128, 2048], BF16)
    dbg = {}

    es = ExitStack()
    with es:
        S = Sched(nc, es)
        A = Arena(nc, es, int(os.environ.get('K_ARENA', '53200')))
        psum = es.enter_context(nc.psum_tensor("psum", [128, 4096], F32))
        banks = [Tile(psum[:, i * 512:(i + 1) * 512], excl=True) for i in range(8)]
        roles = {"mm": [0, 1, 2, 3], "pv": [4, 5], "aux": [6, 7]}
        rr = {k: 0 for k in roles}

        def PS(role):
            lst = roles[role]
            b = banks[lst[rr[role] % len(lst)]]
            rr[role] += 1
            return b

        def dma(out, in_, reads=(), writes=()):
            return S.op("sp", lambda e: e.dma_start(out=out, in_=in_), reads=reads, writes=writes)

        def mm(out, lhsT, rhs, start, stop, reads, writes):
            return S.op("pe", lambda e: e.matmul(out, lhsT=lhsT, rhs=rhs, start=start, stop=stop), reads=reads, writes=writes)

        def tr(out, in_, ident, reads, writes):
            return S.op("pe", lambda e: e.transpose(out=out, in_=in_, identity=ident), reads=reads, writes=writes)

        def act(out, in_, func, reads, writes, bias=None, scale=None):
            kw = {}
            if bias is not None:
                kw["bias"] = bias
            if scale is not None:
                kw["scale"] = scale
            return S.op("act", lambda e: e.activation(out=out, in_=in_, func=func, **kw), reads=reads, writes=writes)

        def tt(ek, out, in0, in1, op, reads, writes):
            return S.op(ek, lambda e: e.tensor_tensor(out=out, in0=in0, in1=in1, op=op), reads=reads, writes=writes)

        def ts(ek, out, in0, s1, op0, reads, writes, s2=None, op1=None):
            if op1 is None:
                return S.op(ek, lambda e: e.tensor_scalar(out=out, in0=in0, scalar1=s1, scalar2=None, op0=op0), reads=reads, writes=writes)
            return S.op(ek, lambda e: e.tensor_scalar(out=out, in0=in0, scalar1=s1, scalar2=s2, op0=op0, op1=op1), reads=reads, writes=writes)

        def stt(out, in0, scalar, in1, op0, op1, reads, writes):
            return S.op("dve", lambda e: e.scalar_tensor_tensor(out=out, in0=in0, scalar=scalar, in1=in1, op0=op0, op1=op1), reads=reads, writes=writes)

        def cp(ek, out, in_, reads, writes):
            return S.op(ek, lambda e: e.tensor_copy(out=out, in_=in_), reads=reads, writes=writes)

        def mset(ek, ap, val, writes):
            return S.op(ek, lambda e: e.memset(ap, val), writes=writes)

        def dump(name, ap, shape, tiles, dt=F32):
            d = dout("dbg_" + name, shape, dt)
            dbg[name] = d
            dma(d, ap, reads=tiles)

        def v3(ap, **kw):
            k = list(kw.keys())[0]
            return ap.rearrange("p (a b) -> p a b", a=kw[k])

        cst = A.tile(NCC)
        dma(cst.ap, consts, writes=[cst])

        def cview(n, rows=128):
            o, w = coffs[n]
            return cst.ap[:rows, o:o + w]

        identf = cview("ident")
        normw = A.tile(8)
        dma(normw.ap, a_normw, writes=[normw])
        gcol = A.tile(6)
        dma(gcol.ap, a_gain, writes=[gcol])
        for g in range(3):
            ts("dve", gcol.ap[:, 2 * g:2 * g + 1], gcol.ap[:, 2 * g:2 * g + 1], 0.125, ALU.mult, [gcol], [gcol])
        eps6 = A.tile(1)
        mset("dve", eps6.ap, 1e-6, [eps6])
        cb = A.tile(128 + 256 + 128 + 13 * 16 + 192 + 128, BF16)
        bones_b = cb.ap[:, 0:128]
        mask2_b = cb.ap[:, 128:384]
        ones_b = cb.ap[:, 384:512]
        maskS_b = cb.ap[:, 512:512 + 208]
        maskN_b = cb.ap[:, 720:720 + 192]
        identb = cb.ap[:, 912:912 + 128]
        for dst, nm in ((bones_b, "bones"), (mask2_b, "mask2"), (ones_b, "ones"), (maskS_b, "maskS"), (maskN_b, "maskN"), (identb, "ident")):
            cp("dve", dst, cview(nm), [cst], [cb])

        flagt = A.tile(1)
        dma(flagt.ap, flag_in, writes=[flagt])
        hTb = [Tile(None) for _ in range(9)]

        def hchunks(t0, t1):
            return [hTb[c] for c in range(t0 // 512, min((t1 - 1) // 512, 8) + 1)]

        qs_all = A.tile(8 * 3 * 32, BF16)
        ks_all = A.tile(8 * 3 * 32, BF16)
        vs_all = A.tile(8 * 3 * 128, BF16)
        sgs = A.tile(8 * 32)
        qs4 = qs_all.ap.rearrange("p (h g t) -> p h g t", h=8, g=3)
        ks4 = ks_all.ap.rearrange("p (h g t) -> p h g t", h=8, g=3)
        vs4 = vs_all.ap.rearrange("p (h g c) -> p h g c", h=8, g=3)
        sgs3 = sgs.ap.rearrange("p (h t) -> p h t", h=8)
        ogs = A.tile(8 * 32, BF16)
        ogs3 = ogs.ap.rearrange("p (h t) -> p h t", h=8)
        hmark = A.off
        hT = A.alloc(8 * NT, BF16)
        hT3 = hT.rearrange("p (k t) -> p k t", k=8)
        p0mark = A.off
        xbuf = [A.tile(1024) for _ in range(2)]
        xn_l = [A.tile(1024) for _ in range(2)]
        junk_l = [A.tile(1024) for _ in range(2)]
        small_l = [A.tile(8) for _ in range(2)]

        def ttr(out, in0, in1, accum, reads, writes):
            return S.op("dve", lambda e: e.tensor_tensor_reduce(out=out, in0=in0, in1=in1, scale=1.0, scalar=0.0,
                                                                op0=ALU.mult, op1=ALU.add, accum_out=accum), reads=reads, writes=writes)

        roles["aux"] = [0, 1, 2, 3, 4, 5, 6, 7]

        def p0_stageA(blk):
            rows = 128 if blk < 32 else 32
            src = xp[blk * 128:(blk + 1) * 128, :] if blk < 32 else xs
            xb = xbuf[blk % 2]
            xn, junk, small = xn_l[blk % 2], junk_l[blk % 2], small_l[blk % 2]
            dma(xb.ap[:rows], src, writes=[xb])
            tt("dve", junk.ap[:rows], xb.ap[:rows], xb.ap[:rows], ALU.mult, [xb], [junk])
            S.op("dve", lambda e, rows=rows, small=small, junk=junk: e.tensor_reduce(out=small.ap[:rows, 0:1], in_=junk.ap[:rows], axis=mybir.AxisListType.X, op=ALU.add),
                 reads=[junk], writes=[small])
            act(small.ap[:rows, 1:2], small.ap[:rows, 0:1], AF.Ln, [small, eps6], [small], bias=eps6.ap[:rows], scale=1.0 / 1024)
            act(small.ap[:rows, 2:3], small.ap[:rows, 1:2], AF.Exp, [small], [small], scale=-0.5)
            act(xn.ap[:rows], xb.ap[:rows], AF.Copy, [xb, small], [xn], scale=small.ap[:rows, 2:3])

        def p0_stageB(blk):
            rows = 128 if blk < 32 else 32
            xn = xn_l[blk % 2]
            tok0 = blk * 128
            for half in range(2):
                pb = PS("aux")
                for k4 in range(4):
                    kc = half * 4 + k4
                    tr(pb.ap[:, k4 * 128:k4 * 128 + rows], xn.ap[:rows, kc * 128:(kc + 1) * 128], identf[:rows, :rows], [xn, cst], [pb])
                tt("dve", hT3[:, half * 4:half * 4 + 4, tok0:tok0 + rows], v3(pb.ap, a=4)[:, :, :rows],
                   normw.ap[:, half * 4:half * 4 + 4].unsqueeze(2).broadcast_to([128, 4, rows]), ALU.mult, [pb, normw], [hTb[blk // 4]])

        p0_stageA(0)
        for blk in range(33):
            if blk + 1 < 33:
                p0_stageA(blk + 1)
            p0_stageB(blk)
        if DEBUG:
            dump("hT", hT, [128, 8 * NT], hTb, BF16)

        roles["aux"] = [6, 7]
        S.barrier()
        A.off = p0mark
        wst = [A.tile(8 * 384) for _ in range(2)]
        wbf = [A.tile(8 * 384, BF16) for _ in range(2)]
        qn = A.tile(NT, BF16)
        kn = A.tile(NT, BF16)
        Vaug = A.tile(33 * 2 * 128, BF16)
        Vaug4 = Vaug.ap.rearrange("p (u h c) -> p u h c", u=33, h=2)
        mset("pool", Vaug.ap, 1.0, [Vaug])
        mpair = A.tile(3 * 512, BF16)
        mp3 = mpair.ap.rearrange("p (v c) -> p v c", v=3)
        for v_ in range(3):
            for hlf in range(2):
                cp("dve", mp3[:, v_, hlf * 256:(hlf + 1) * 256], cview("mask2"), [cst], [mpair])
        for v_, hlf in ((1, 0), (2, 0), (2, 1)):
            ts("dve", mp3[:, v_, hlf * 256:hlf * 256 + 128], mp3[:, v_, hlf * 256:hlf * 256 + 128], flagt.ap[:, 0:1], ALU.mult, [mpair, flagt], [mpair])
        U = [A.tile(2048) for _ in range(2)]
        sq = [A.tile(512, BF16) for _ in range(2)]
        rs = [A.tile(512) for _ in range(2)]
        lnv = rs
        kf = [A.tile(512) for _ in range(2)]
        kt = [A.tile(512)] * 2
        vf = [A.tile(512) for _ in range(2)]
        Pb = [A.tile(512, BF16) for _ in range(5)]
        sg = vf
        numt = kf
        dent = kt
        ogc = [A.tile(512, BF16) for _ in range(2)]
        cnt = {"w": 0, "t": 0, "p": 0}

        def w3(t):
            return t.ap.rearrange("p (k c) -> p k c", k=8)

        def inproj_fm(wt, c0, tc, ps):
            N = 512 if tc < 8 else 32
            t0 = tc * 512
            wv = wt.ap.rearrange("p (k c) -> p k c", k=8)
            for kc in range(8):
                mm(ps.ap[:, :N], wv[:, kc, c0:c0 + 128], hT3[:, kc, t0:t0 + N], kc == 0, kc == 7, [wt, hTb[tc]], [ps])
            return N, t0

        def unit(u, d):
            sp_, r_ = (u, 0) if d == 1 else (u // d, u % d)
            return sp_, r_, sp_ * 128 * d + r_

        def sl(st, d):
            return slice(st, st + 127 * d + 1, d)

        npairs = 8 if STAGE >= 2 else (1 if STAGE >= 0 else 0)
        wbg = A.tile(8 * 128, BF16)

        def run_group(hp, g, W, d, wb, mid_hook):
            span = 128 * d
            def qk_stage1(s_, tc):
                ps = PS("mm")
                N, t0 = inproj_fm(wb, s_ * 128, tc, ps)
                i2 = cnt["t"] % 2
                cnt["t"] += 1
                act(sq[i2].ap[:, :N], ps.ap[:, :N], AF.Square, [ps], [sq[i2]])
                return (s_, tc, ps, N, t0, i2)

            def qk_stage2(st8):
                s_, tc, ps, N, t0, i2 = st8
                dst = qn if s_ == 0 else kn
                p2 = PS("aux")
                mm(p2.ap[:, :N], bones_b, sq[i2].ap[:, :N], True, True, [sq[i2], cb], [p2])
                act(lnv[i2].ap[:, :N], p2.ap[:, :N], AF.Ln, [p2, eps6], [lnv[i2]], bias=eps6.ap, scale=1.0 / 64)
                act(rs[i2].ap[:, :N], lnv[i2].ap[:, :N], AF.Exp, [lnv[i2]], [rs[i2]], scale=-0.5)
                gc = gcol.ap[:, 2 * g + s_:2 * g + s_ + 1]
                stt(dst.ap[:, t0:t0 + N], ps.ap[:, :N], gc, rs[i2].ap[:, :N], ALU.mult, ALU.mult, [ps, rs[i2], gcol], [dst])
                need = (s_ == 1) and (tc == 8 or (t0 + N > NTOK - W)) and ('kout' not in SKIP)
                if need:
                    stt(kf[i2].ap[:, :N], ps.ap[:, :N], gc, rs[i2].ap[:, :N], ALU.mult, ALU.mult, [ps, rs[i2], gcol], [kf[i2]])
                    p3 = PS("aux")
                    nb = (N + 127) // 128
                    for j in range(nb):
                        rws = min(128, N - j * 128)
                        tr(p3.ap[:rws, j * 128:(j + 1) * 128], kf[i2].ap[:, j * 128:j * 128 + rws], identf, [kf[i2], cst], [p3])
                    rmax = min(128, N)
                    act(kt[i2].ap[:rmax, :nb * 128], p3.ap[:rmax, :nb * 128], AF.Copy, [p3], [kt[i2]])
                    if tc == 8:
                        dma(nkv_s[g][:, 0, hp * 128:(hp + 1) * 128], kt[i2].ap[:32, 0:128], reads=[kt[i2]])
                    else:
                        for j in range(nb):
                            tk = t0 + j * 128
                            if tk >= NTOK - W:
                                r0 = tk - (NTOK - W)
                                dma(nkv_p[g][r0:r0 + 128, 0, hp * 128:(hp + 1) * 128], kt[i2].ap[:, j * 128:(j + 1) * 128], reads=[kt[i2]])

            pend = None
            kc_min = 0 if g == 2 else 3
            for s_ in range(2):
                for tc in (range(4, 9) if s_ == 0 else range(kc_min, 9)):
                    cur = qk_stage1(s_, tc)
                    if pend is not None:
                        qk_stage2(pend)
                    pend = cur
            qk_stage2(pend)
            cp("pool", qs4[:, hp, g, :], qn.ap[:, NTOK:NT], [qn], [qs_all])
            cp("pool", ks4[:, hp, g, :], kn.ap[:, NTOK:NT], [kn], [ks_all])
            for u0 in range(0 if g == 2 else 12, 36, 4):
                ps = PS("mm")
                us = [u for u in range(u0, u0 + 4) if u <= 32]
                for j, u in enumerate(us):
                    for kc in range(8):
                        if u < 32:
                            sp_, r_, st = unit(u, d)
                            mm(ps.ap[:, j * 128:(j + 1) * 128], hT3[:, kc, sl(st, d)], w3(wb)[:, kc, 256:384], kc == 0, kc == 7,
                               [wb] + hchunks(st, st + 127 * d + 1), [ps])
                        else:
                            mm(ps.ap[:32, j * 128:(j + 1) * 128], hT3[:, kc, NTOK:NT], w3(wb)[:, kc, 256:384], kc == 0, kc == 7,
                               [wb, hTb[8]], [ps])
                if us[0] < 32:
                    n = len(us)
                    for j in range(n):
                        cp("dve", Vaug4[:, u0 + j, :, 0:64], ps.ap[:, j * 128:(j + 1) * 128].rearrange("p (h c) -> p h c", h=2), [ps], [Vaug])
                    outs = [(j, u) for j, u in enumerate(us) if unit(u, d)[2] >= NTOK - W]
                    if outs and 'vout' not in SKIP:
                        i2 = cnt["t"] % 2
                        cnt["t"] += 1
                        cp("dve", vf[i2].ap, ps.ap, [ps], [vf[i2]])
                        for j, u in outs:
                            sp_, r_, st = unit(u, d)
                            rel = st - (NTOK - W)
                            dma(nkv_p[g][sl(rel, d), 1, hp * 128:(hp + 1) * 128], vf[i2].ap[:, j * 128:(j + 1) * 128], reads=[vf[i2]])
                else:
                    act(Vaug4[:32, 32, :, 0:64], ps.ap[:32, 0:128].rearrange("p (h c) -> p h c", h=2), AF.Copy, [ps], [Vaug])
                    i2 = cnt["t"] % 2
                    cnt["t"] += 1
                    cp("dve", vf[i2].ap[:32, 0:128], ps.ap[:32, 0:128], [ps], [vf[i2]])
                    dma(nkv_s[g][:, 1, hp * 128:(hp + 1) * 128], vf[i2].ap[:32, 0:128], reads=[vf[i2]])
                    cp("pool", vs4[:32, hp, g, :].rearrange("p (h c) -> p h c", h=2), Vaug4[:32, 32, :, 0:64], [Vaug], [vs_all])
            if DEBUG and hp == 0 and g == 1:
                dump("qn", qn.ap, [128, NT], [qn], BF16)
                dump("kn", kn.ap, [128, NT], [kn], BF16)
                dump("Vaug", Vaug.ap, [128, 33 * 256], [Vaug], BF16)
            if STAGE < 1:
                return
            def att_scores(hh, uu):
                hs = slice(64 * hh, 64 * hh + 64)
                psS = PS("mm")
                P = Pb[cnt["p"] % 5]
                cnt["p"] += 1
                for j in range(2):
                    u = uu + j
                    sp_, r_, st = unit(u, d)
                    qap = qn.ap[hs, sl(st, d)]
                    if sp_ > 0:
                        mm(psS.ap[:, j * 256:j * 256 + 128], kn.ap[hs, sl(st - span, d)], qap, True, True, [qn, kn], [psS])
                    mm(psS.ap[:, j * 256 + 128:j * 256 + 256], kn.ap[hs, sl(st, d)], qap, True, True, [qn, kn], [psS])
                act(P.ap, psS.ap, AF.Exp, [psS], [P])
                if g == 2 or (g == 1 and uu < 20):
                    mv = 2
                elif g == 0 and uu == 16:
                    mv = 1
                else:
                    mv = 0
                tt("pool", P.ap, P.ap, mp3[:, mv, :], ALU.mult, [P, mpair], [P])
                return P

            def att_pv(hh, u0, uu, P, po):
                for j in range(2):
                    u = uu + j
                    sp_, r_, st = unit(u, d)
                    jj = u - u0
                    if sp_ > 0:
                        mm(po.ap[:, jj * 128:(jj + 1) * 128], Vaug4[:, u - d, hh, :], P.ap[:, j * 256:j * 256 + 128], True, False, [Vaug, P], [po])
                    mm(po.ap[:, jj * 128:(jj + 1) * 128], Vaug4[:, u, hh, :], P.ap[:, j * 256 + 128:j * 256 + 256], sp_ == 0, True, [Vaug, P], [po])

            def att_evac(hh, u0, po):
                Uh = U[hh]
                if g == 0:
                    cp("dve", Uh.ap[:, u0 * 128 - 2048:(u0 + 4) * 128 - 2048], po.ap, [po], [Uh])
                else:
                    sp_, r0, _ = unit(u0, d)
                    uv = Uh.ap[:, sp_ * span - 2048:(sp_ + 1) * span - 2048].rearrange("p (i r) -> p r i", r=d)[:, r0:r0 + 4, :]
                    tt("dve", uv, v3(po.ap, a=4), uv, ALU.add, [po, Uh], [Uh])

            tasks = [(hh, u0, uu) for hh in range(2) for u0 in range(16, 32, 4) for uu in (u0, u0 + 2)]
            pendq = []
            po = None
            for ti_, tk_ in enumerate(tasks + [None, None, None]):
                if mid_hook and ti_ % 2 == 0:
                    mid_hook.pop(0)()
                if tk_ is not None:
                    pendq.append((tk_[0], tk_[1], tk_[2], att_scores(tk_[0], tk_[2])))
                if pendq and (len(pendq) > 3 or tk_ is None):
                    ph, pu0, puu, pP = pendq.pop(0)
                    if puu == pu0:
                        po = PS("pv")
                    att_pv(ph, pu0, puu, pP, po)
                    if puu == pu0 + 2:
                        att_evac(ph, pu0, po)

        def finalize_steps(hp):
            return [(lambda tc=tc: finalize_chunk(hp, tc)) for tc in range(4, 9)]

        def finalize_chunk(hp, tc):
            if True:
                ps = PS("mm")
                N, t0 = inproj_fm(wbg, 0, tc, ps)
                i2 = cnt["t"] % 2
                cnt["t"] += 1
                if tc == 8:
                    act(sgs3[:, hp, :], ps.ap[:, :32], AF.Silu, [ps], [sgs])
                    return
                act(sg[i2].ap, ps.ap, AF.Silu, [ps], [sg[i2]])
                cs = slice(t0 - 2048, t0 - 2048 + 512)
                cp("pool", numt[i2].ap[0:64], U[0].ap[0:64, cs], [U[0]], [numt[i2]])
                cp("dve", numt[i2].ap[64:128], U[1].ap[0:64, cs], [U[1]], [numt[i2]])
                cp("dve", dent[i2].ap[0:64], U[0].ap[64:128, cs], [U[0]], [dent[i2]])
                cp("pool", dent[i2].ap[64:128], U[1].ap[64:128, cs], [U[1]], [dent[i2]])
                act(dent[i2].ap, dent[i2].ap, AF.Ln, [dent[i2]], [dent[i2]])
                act(dent[i2].ap, dent[i2].ap, AF.Exp, [dent[i2]], [dent[i2]], scale=-1.0)
                tt("dve", numt[i2].ap, numt[i2].ap, dent[i2].ap, ALU.mult, [numt[i2], dent[i2]], [numt[i2]])
                tt("dve", ogc[i2].ap, numt[i2].ap, sg[i2].ap, ALU.mult, [numt[i2], sg[i2]], [ogc[i2]])
                dma(ogd[hp, :, cs], ogc[i2].ap, reads=[ogc[i2]])
                if DEBUG and hp == 0 and tc == 3:
                    dump("ogc", ogc[i2].ap, [128, 512], [ogc[i2]], BF16)


        jobs = [(hp, g) for hp in range(npairs) for g in (0, 1, 2, 3)]
        wslot = {}

        def issue_load(ji):
            hp, g = jobs[ji]
            ws = wst[ji % 2]
            if g < 3:
                nload = sum(1 for (h2, g2) in jobs[:ji] if g2 < 3)
                wb = wbf[nload % 2]
                for s_ in range(3):
                    c0 = g * 3072 + s_ * 1024 + hp * 128
                    dma(w3(ws)[:, :, s_ * 128:(s_ + 1) * 128], a_win[:, c0:c0 + 128].rearrange("(k p) c -> p k c", p=128), writes=[ws])
                cp("pool", wb.ap, ws.ap, [ws], [wb])
                wslot[ji] = wb
            else:
                c0 = 9216 + hp * 128
                dma(w3(ws)[:, :, 0:128], a_win[:, c0:c0 + 128].rearrange("(k p) c -> p k c", p=128), writes=[ws])
                cp("pool", wbg.ap.rearrange("p (k c) -> p k c", k=8), w3(ws)[:, :, 0:128], [ws], [wbg])

        pending_steps = []
        loaded = set()

        def ensure_load(ji):
            if ji < len(jobs) and ji not in loaded:
                loaded.add(ji)
                issue_load(ji)

        ensure_load(0)
        for ji, (hp, g) in enumerate(jobs):
            ensure_load(ji + 1)
            if ji + 1 < len(jobs) and jobs[ji + 1][1] == 3:
                ensure_load(ji + 2)
            if g < 3:
                W, d = GROUPS[g]
                run_group(hp, g, W, d, wslot[ji], pending_steps if g == 0 else None)
            elif STAGE >= 1:
                while pending_steps:
                    pending_steps.pop(0)()
                pending_steps.extend(finalize_steps(hp))
        while pending_steps:
            pending_steps.pop(0)()

        if STAGE >= 3:
            S.barrier()
            A.off = hmark
            if STAGE >= 4:
                top0 = A.n - (8 * 5152 + 16 * 1024) // 2
                Wss = Tile(A.t[:, top0:top0 + 8 * 5152 // 2].bitcast(BF16))
                Ws3 = Wss.ap.rearrange("p (k c) -> p k c", k=8)
                wos = Tile(A.t[:, top0 + 8 * 5152 // 2:A.n].bitcast(BF16))
                wos3 = wos.ap.rearrange("p (h c) -> p h c", h=16)

                def pdma(out, in_, reads=(), writes=()):
                    return S.op("pool", lambda e: e.dma_start(out=out, in_=in_), reads=reads, writes=writes, dma=True)

                for kc in range(8):
                    pdma(Ws3[:, kc, :], s_win[kc * 128:(kc + 1) * 128, :], writes=[Wss])
            kvb = [A.tile(2048) for _ in range(2)]
            vcb = [A.tile(1024, BF16) for _ in range(2)]
            kcT = [A.tile(1024, BF16) for _ in range(2)]
            qbd = A.tile(8 * 3 * 16, BF16)
            qbd4 = qbd.ap.rearrange("p (h g t) -> p h g t", h=8, g=3)
            zer = A.tile(128, BF16)
            Psm = [A.tile(128, BF16) for _ in range(2)]
            PN = A.tile(384, BF16)
            rden = A.tile(128)
            osb = A.tile(128)
            mset("dve", qbd.ap, 0.0, [qbd])
            mset("dve", zer.ap, 0.0, [zer])
            blocks = []
            for g, (W, d) in enumerate(GROUPS):
                for rp in range(min(d, 8)):
                    blocks.append((g, d, rp))
            for b in range(4):
                cp("dve", qbd4[0:64, :, :, 0:8], qs4[0:64, :, :, b * 8:(b + 1) * 8], [qs_all], [qbd])
                cp("dve", qbd4[64:128, :, :, 8:16], qs4[64:128, :, :, b * 8:(b + 1) * 8], [qs_all], [qbd])
                psA = PS("pv")
                mm(psA.ap[:, 0:256], zer.ap, mask2_b, True, False, [zer, cb], [psA])
                psN = PS("aux")
                for g in range(3):
                    for hp in range(8):
                        mm(psN.ap[:32, g * 128 + hp * 16:g * 128 + hp * 16 + 16], ks4[:, hp, g, :], qbd4[:, hp, g, :], True, True, [ks_all, qbd], [psN])
                act(PN.ap[:32], psN.ap[:32, 0:384], AF.Exp, [psN], [PN])
                mN = maskN_b[:32, b * 48:(b + 1) * 48].rearrange("p (g t) -> p g t", g=3).unsqueeze(2).broadcast_to([32, 3, 8, 16])
                pn4 = PN.ap[:32].rearrange("p (g h t) -> p g h t", g=3, h=8)
                tt("dve", pn4, pn4, mN, ALU.mult, [PN, cb], [PN])
                for bi, (g, d, rp) in enumerate(blocks):
                    kv = kvb[bi % 2]
                    W = GROUPS[g][0]
                    dma(kv.ap, kvc[g][b, rp:rp + 127 * d + 1:d, :, :].rearrange("r s c -> r (s c)"), writes=[kv])
                    vb = vcb[bi % 2]
                    act(vb.ap, kv.ap[:, 1024:2048], AF.Copy, [kv], [vb]) if bi % 2 else cp("dve", vb.ap, kv.ap[:, 1024:2048], [kv], [vb])
                    kT = kcT[bi % 2]
                    for half in range(2):
                        pT = PS("aux")
                        for j in range(4):
                            hp = half * 4 + j
                            tr(pT.ap[:, j * 128:(j + 1) * 128], kv.ap[:, hp * 128:(hp + 1) * 128], identf, [kv, cst], [pT])
                        act(kT.ap[:, half * 512:(half + 1) * 512], pT.ap, AF.Copy, [pT], [kT])
                    psS = PS("mm")
                    for hp in range(8):
                        mm(psS.ap[:, hp * 16:(hp + 1) * 16], kT.ap[:, hp * 128:(hp + 1) * 128], qbd4[:, hp, g, :], True, True, [kT, qbd], [psS])
                    Pm = Psm[bi % 2]
                    act(Pm.ap, psS.ap[:, 0:128], AF.Exp, [psS], [Pm])
                    mS = maskS_b[:, bi * 16:(bi + 1) * 16].unsqueeze(1).broadcast_to([128, 8, 16])
                    tt("dve", v3(Pm.ap, a=8), v3(Pm.ap, a=8), mS, ALU.mult, [Pm, cb], [Pm])
                    for hp in range(8):
                        mm(psA.ap[:, hp * 16:(hp + 1) * 16], vb.ap[:, hp * 128:(hp + 1) * 128], Pm.ap[:, hp * 16:(hp + 1) * 16], False, False, [vb, Pm], [psA])
                    mm(psA.ap[:, 128:256], ones_b, Pm.ap, False, False, [Pm, cb], [psA])
                for g in range(3):
                    for hp in range(8):
                        mm(psA.ap[:, hp * 16:(hp + 1) * 16], vs4[:32, hp, g, :], PN.ap[:32, g * 128 + hp * 16:g * 128 + hp * 16 + 16], False, False, [vs_all, PN], [psA])
                    mm(psA.ap[:, 128:256], ones_b[:32, :], PN.ap[:32, g * 128:(g + 1) * 128], False, g == 2, [PN, cb], [psA])
                S.op("dve", lambda e, psA=psA: e.reciprocal(out=rden.ap, in_=psA.ap[:, 128:256]), reads=[psA], writes=[rden])
                tt("dve", osb.ap, psA.ap[:, 0:128], rden.ap, ALU.mult, [psA, rden], [osb])
                o3 = v3(osb.ap, a=8)
                tt("dve", ogs3[0:64, :, b * 8:(b + 1) * 8], o3[0:64, :, 0:8], sgs3[0:64, :, b * 8:(b + 1) * 8], ALU.mult, [osb, sgs], [ogs])
                tt("dve", ogs3[64:128, :, b * 8:(b + 1) * 8], o3[64:128, :, 8:16], sgs3[64:128, :, b * 8:(b + 1) * 8], ALU.mult, [osb, sgs], [ogs])
            if DEBUG:
                dump("ogs", ogs.ap, [128, 256], [ogs], BF16)

            S.barrier()
            A.off = hmark
            woutb = A.tile(8 * 1024, BF16)
            S.op("pool", lambda e: e.dma_start(out=woutb.ap.rearrange("p (h c) -> p h c", h=8), in_=a_wout.rearrange("(h p) c -> p h c", p=128)),
                 writes=[woutb], dma=True)
            wo3 = woutb.ap.rearrange("p (h c) -> p h c", h=8)
            ogb = [A.tile(1024, BF16) for _ in range(2)]
            xres = [A.tile(1024) for _ in range(2)]
            x1t = [A.tile(1024) for _ in range(2)]
            def op_loads(blk):
                i2 = blk % 2
                if blk < 32:
                    dma(v3(ogb[i2].ap, a=8), ogd[:, :, (blk - 16) * 128:(blk - 15) * 128].rearrange("h p t -> p h t"), writes=[ogb[i2]])
                    dma(xres[i2].ap, xp[blk * 128:(blk + 1) * 128, :], writes=[xres[i2]])
                else:
                    dma(xres[i2].ap[:32], xs, writes=[xres[i2]])

            op_loads(16)
            for blk in range(16, 33):
                rows = 128 if blk < 32 else 32
                i2 = blk % 2
                if blk + 1 < 33:
                    op_loads(blk + 1)
                if blk < 32:
                    og3 = v3(ogb[i2].ap, a=8)
                    rd_og = [ogb[i2]]
                else:
                    og3 = ogs3
                    rd_og = [ogs]
                for ncol in range(2):
                    ps = PS("mm")
                    for hp in range(8):
                        mm(ps.ap[:rows, :], og3[:, hp, :rows], wo3[:, hp, ncol * 512:(ncol + 1) * 512], hp == 0, hp == 7, rd_og + [woutb], [ps])
                    tt("dve", x1t[i2].ap[:rows, ncol * 512:(ncol + 1) * 512], ps.ap[:rows, :], xres[i2].ap[:rows, ncol * 512:(ncol + 1) * 512], ALU.add,
                       [ps, xres[i2]], [x1t[i2]])
                dma(x1d[(blk - 16) * 128:(blk - 16) * 128 + rows, :], x1t[i2].ap[:rows], reads=[x1t[i2]])
                if STAGE == 3:
                    if blk < 32:
                        dma(y_p[(blk - 16) * 128:(blk - 15) * 128, :], x1t[i2].ap, reads=[x1t[i2]])
                    else:
                        dma(y_s, x1t[i2].ap[:32], reads=[x1t[i2]])

        if STAGE >= 4:
            S.barrier()
            A.off = hmark
            AX = mybir.AxisListType.X
            roles["aux"] = [4, 5, 6, 7]
            x1b_pre = [A.tile(1024) for _ in range(2)]
            prm = A.tile(8 + 96 + 24 + 32 * 3 + 16 + 4)
            normw2 = prm.ap[:, 0:8]
            cw = prm.ap[:, 8:104]
            cbias = prm.ap[:, 104:128]
            dtb = prm.ap[:, 128:160]
            Aneg = prm.ap[:, 160:192]
            Drep = prm.ap[:, 192:224]
            gnT = prm.ap[:, 224:240]
            one1 = prm.ap[:, 240:241]
            eps5 = prm.ap[:, 241:242]
            for dst, src in ((normw2, s_normw), (cw, s_cw), (cbias, s_cb), (dtb, s_dtb), (Aneg, s_alog), (Drep, s_D), (gnT, s_gn)):
                dma(dst, src, writes=[prm])
            act(Aneg, Aneg, AF.Exp, [prm], [prm])
            ts("dve", Aneg, Aneg, -1.0, ALU.mult, [prm], [prm])
            mset("dve", one1, 1.0, [prm])
            mset("dve", eps5, 1e-5, [prm])
            triM = cview("tri")

            x1b = x1b_pre
            jk = A.tile(2048)
            xn2 = Tile(jk.ap[:, 1024:2048], jk.buf)
            yout = [Tile(jk.ap[:, 0:1024], jk.buf), Tile(jk.ap[:, 1024:2048], jk.buf)]
            sm2 = A.tile(8)
            hT2 = A.tile(8 * 128, BF16)
            hT23 = hT2.ap.rearrange("p (k t) -> p k t", k=8)
            xpre = A.tile(24 * 131)
            xpre3 = xpre.ap.rearrange("p (s t) -> p s t", s=24)
            blockA = A.alloc(3072)
            yacc = Tile(blockA[:, 0:2048])
            Tt = Tile(blockA[:, 2048:3072])
            Ee = Tt
            cacc3 = blockA.rearrange("p (s t) -> p s t", s=24)
            caccT = [yacc, Tt]
            cslab = [Tile(None) for _ in range(24)]
            xbc_w = A.alloc(1536)
            xbcT = Tile(xbc_w.bitcast(BF16))
            xbc3 = xbcT.ap.rearrange("p (s t) -> p s t", s=24)
            ynT = Tile(xbc_w[:, 0:1024].bitcast(BF16), xbcT.buf)
            ynT3 = ynT.ap.rearrange("p (s t) -> p s t", s=16)
            blockB = A.alloc(2048)
            xtok = Tile(blockB[:, 0:1024].bitcast(BF16))
            xdt = Tile(blockB[:, 1024:2048].bitcast(BF16))
            szT = [xtok, xdt]
            sz_ap = blockB
            yb = A.tile(2048, BF16)
            xdte = Tile(yb.ap, yb.buf)
            Btok = A.tile(512, BF16)
            Hf = A.tile(2048)
            Hb = A.tile(2048, BF16)
            dts = A.tile(32 * 8)
            wreg = A.t[:, top0 + 8 * 5152 // 2:A.n]
            xbcT_b = Tile(wreg[:, 0:1536].bitcast(BF16))
            dts_b = Tile(wreg[:, 1536:1792])
            XBC = [xbcT, xbcT_b]
            DTS = [dts, dts_b]
            aT = A.tile(128)
            CBm = A.tile(128)
            MT2 = [Tile(jk.ap[:, 1536:2048].bitcast(BF16)), Tile(jk.ap[:, 0:512].bitcast(BF16))]
            tmpg = Tile(jk.ap[:, 1024:1536])
            CBm2 = [CBm, Tile(jk.ap[:, 512:640])]
            ssd_t = MT2 + [tmpg, CBm2[1]]
            assert A.off <= top0, ("L1 tiles overlap resident weights", A.off, top0)
            mset("dve", xpre.ap, 0.0, [xpre])
            mset("dve", Hf.ap, 0.0, [Hf])
            mset("pool", Hb.ap, 0.0, [Hb])

            def bf_bank(t):
                return t.ap.bitcast(BF16)

            def state_out(dst):
                for q4 in range(4):
                    pT = PS("aux")
                    for j in range(4):
                        t_ = q4 * 4 + j
                        tr(pT.ap[:, j * 128:(j + 1) * 128], Hf.ap[:, t_ * 128:(t_ + 1) * 128], identf, [Hf, cst], [pT])
                    act(jk.ap[:, q4 * 512:(q4 + 1) * 512], pT.ap, AF.Copy, [pT], [jk])
                dma(dst.rearrange("(t p) n -> p t n", p=128), v3(jk.ap, a=16), reads=[jk])

            def state_in(src):
                dma(v3(jk.ap, a=16), src.rearrange("(t p) n -> p t n", p=128), writes=[jk])
                for q4 in range(4):
                    pT = PS("aux")
                    for j in range(4):
                        t_ = q4 * 4 + j
                        tr(pT.ap[:, j * 128:(j + 1) * 128], jk.ap[:, t_ * 128:(t_ + 1) * 128], identf, [jk, cst], [pT])
                    act(Hf.ap[:, q4 * 512:(q4 + 1) * 512], pT.ap, AF.Copy, [pT], [Hf])
                cp("pool", Hb.ap, Hf.ap, [Hf], [Hb])

            def nc_dma(out, in_, reads=(), writes=()):
                return S.op("sp", lambda e: e.dma_start(out=out, in_=in_, allow_slow_non_contiguous=True), reads=reads, writes=writes)

            def l1_block(tok0, L, ydst, bi, pre=False, allproj=False, save=None, reuse=None, par=0):
                xbcT = XBC[par]
                xbc3 = xbcT.ap.rearrange("p (s t) -> p s t", s=24)
                dts = DTS[par]

                def dsl(i):
                    return dts.ap[:, i * 32:(i + 1) * 32]
                NSL = 20 if pre else 24
                NPJ = 24 if (allproj or not pre) else 20
                SL0 = 20 if reuse is not None else 0
                x1 = x1b[bi % 2]
                dma(x1.ap[:L], x1d[tok0:tok0 + L, :], writes=[x1])
                tt("dve", jk.ap[:L, 0:1024], x1.ap[:L], x1.ap[:L], ALU.mult, [x1], [jk])
                S.op("dve", lambda e: e.tensor_reduce(out=sm2.ap[:L, 0:1], in_=jk.ap[:L, 0:1024], axis=AX, op=ALU.add), reads=[jk], writes=[sm2])
                act(sm2.ap[:L, 1:2], sm2.ap[:L, 0:1], AF.Ln, [sm2, eps6], [sm2], bias=eps6.ap[:L], scale=1.0 / 1024)
                act(sm2.ap[:L, 2:3], sm2.ap[:L, 1:2], AF.Exp, [sm2], [sm2], scale=-0.5)
                act(xn2.ap[:L], x1.ap[:L], AF.Copy, [x1, sm2], [xn2], scale=sm2.ap[:L, 2:3])
                for half in range(2):
                    pb = PS("aux")
                    for k4 in range(4):
                        kc = half * 4 + k4
                        tr(pb.ap[:, k4 * 128:k4 * 128 + L], xn2.ap[:L, kc * 128:(kc + 1) * 128], identf[:L, :L], [xn2, cst], [pb])
                    tt("dve", hT23[:, half * 4:half * 4 + 4, :L], v3(pb.ap, a=4)[:, :, :L],
                       normw2[:, half * 4:half * 4 + 4].unsqueeze(2).broadcast_to([128, 4, L]), ALU.mult, [pb, prm], [hT2])
                if reuse is None:
                    ps = PS("aux")
                    for kc in range(8):
                        mm(ps.ap[:L, 0:32], hT23[:, kc, :L], Ws3[:, kc, 5120:5152], kc == 0, kc == 7, [hT2, Wss], [ps])
                    tt("dve", dsl(0)[:L], ps.ap[:L, 0:32], dtb[:L], ALU.add, [ps, prm], [dts])
                for s4 in range((0 if allproj else SL0 // 4), NPJ // 4):
                    ps = PS("mm")
                    for j in range(4):
                        slab = s4 * 4 + j
                        for kc in range(8):
                            mm(ps.ap[:, j * 128:j * 128 + L], Ws3[:, kc, 2048 + slab * 128:2048 + (slab + 1) * 128], hT23[:, kc, :L],
                               kc == 0, kc == 7, [hT2, Wss], [ps])
                    act(xpre3[:, s4 * 4:s4 * 4 + 4, 3:3 + L], v3(ps.ap, a=4)[:, :, :L], AF.Copy, [ps], [xpre])
                if reuse is None:
                    act(dsl(1)[:L], dsl(0)[:L], AF.Abs, [dts], [dts])
                    act(dsl(1)[:L], dsl(1)[:L], AF.Exp, [dts], [dts], scale=-1.0)
                    act(dsl(1)[:L], dsl(1)[:L], AF.Ln, [dts, prm], [dts], bias=one1[:L], scale=1.0)
                    stt(dsl(2)[:L], dsl(0)[:L], 0.0, dsl(1)[:L], ALU.max, ALU.add, [dts], [dts])
                    tt("dve", dsl(3)[:L], dsl(2)[:L], Aneg[:L], ALU.mult, [dts, prm], [dts])
                    ps = PS("aux")
                    mm(ps.ap[:L, 0:32], triM[:L, :L], dsl(3)[:L], True, True, [cst, dts], [ps])
                    cp("dve", dsl(4)[:L], ps.ap[:L, 0:32], [ps], [dts])
                    act(dsl(6)[:L], dsl(4)[:L], AF.Exp, [dts], [dts])
                    ps = PS("aux")
                    mm(ps.ap[:, 0:32], identf[:L, L - 1:L].broadcast_to([L, 128]), dsl(4)[:L], True, True, [cst, dts], [ps])
                    cp("dve", dsl(7), ps.ap[:, 0:32], [ps], [dts])
                    tt("dve", dsl(5)[:L], dsl(7)[:L], dsl(4)[:L], ALU.subtract, [dts], [dts])
                    act(dsl(5)[:L], dsl(5)[:L], AF.Exp, [dts], [dts])
                    act(dsl(7), dsl(7), AF.Exp, [dts], [dts])
                    ps = PS("aux")
                    tr(ps.ap[:32, :L], dsl(4)[:L], identf[:L, :L], [dts, cst], [ps])
                    cp("dve", aT.ap[:32, :L], ps.ap[:32, :L], [ps], [aT])
                else:
                    dma(dts.ap, dtsd[reuse], writes=[dts])
                    dma(aT.ap[:32, :128], aTd[reuse], writes=[aT])
                if save is not None:
                    dma(dtsd[save], dts.ap, reads=[dts])
                    dma(aTd[save], aT.ap[:32, :128], reads=[aT])
                S.touch("dve", caccT)
                S.touch("act", caccT)
                if L <= 32:
                    S.touch("dve", [jk])
                    cw3 = cw.rearrange("p (j s) -> p j s", j=4)
                    accv = cacc3[:, :, :L]
                    tmpv = tmpg.ap[:, 0:24 * L].rearrange("p (s t) -> p s t", s=24)
                    allc = cslab + caccT
                    tt("dve", accv, xpre3[:, :, 0:L], cw3[:, 0, :].unsqueeze(2).broadcast_to([128, 24, L]), ALU.mult, [xpre, prm], allc)
                    tt("dve", accv, accv, cbias.unsqueeze(2).broadcast_to([128, 24, L]), ALU.add, [prm] + allc, allc)
                    for j in range(1, 4):
                        tt("dve", tmpv, xpre3[:, :, j:j + L], cw3[:, j, :].unsqueeze(2).broadcast_to([128, 24, L]), ALU.mult, [xpre, prm], [tmpg])
                        tt("dve", accv, accv, tmpv, ALU.add, [tmpg] + allc, allc)
                else:
                    for j in range(4):
                        for slab in range(SL0, NSL):
                            cs_ = cslab[slab]
                            if j == 0:
                                act(cacc3[:, slab, :L], xpre3[:, slab, 0:L], AF.Identity, [xpre, prm], [cs_], bias=cbias[:, slab:slab + 1], scale=cw[:, slab:slab + 1])
                            else:
                                stt(cacc3[:, slab, :L], xpre3[:, slab, j:j + L], cw[:, j * 24 + slab:j * 24 + slab + 1], cacc3[:, slab, :L],
                                    ALU.mult, ALU.add, [xpre, prm, cs_], [cs_])
                if reuse is not None:
                    dma(xbc3[:, 0:20, :], xbs[reuse].rearrange("p (s t) -> p s t", s=20), writes=[xbcT])
                act(xbc3[:, SL0:NSL, :L], cacc3[:, SL0:NSL, :L], AF.Silu, cslab + caccT, [xbcT])
                if save is not None:
                    dma(xbs[save].rearrange("p (s t) -> p s t", s=20), xbc3[:, 0:20, :], reads=[xbcT])
                cp("dve", xpre3[:, :, 0:3], xpre3[:, :, L:L + 3], [xpre], [xpre])
                if pre:
                    yield
                for half in range(2):
                    pb = PS("aux")
                    pv_ = bf_bank(pb)
                    for j in range(8):
                        slab = half * 8 + j
                        tr(pv_[:L, j * 128:(j + 1) * 128], xbc3[:, slab, :L], identb, [xbcT, cb], [pb])
                    act(xtok.ap[:L, half * 1024:(half + 1) * 1024], pv_[:L, :], AF.Copy, [pb], [xtok])
                pb = PS("aux")
                pv_ = bf_bank(pb)
                for gi in range(4):
                    tr(pv_[:L, gi * 128:(gi + 1) * 128], xbc3[:, 16 + gi, :L], identb, [xbcT, cb], [pb])
                act(Btok.ap[:L], pv_[:L, 0:512], AF.Copy, [pb], [Btok])
                x3 = xtok.ap[:L].rearrange("p (h c) -> p h c", h=32)
                if pre:
                    tt("dve", dsl(1)[:L], dsl(2)[:L], dsl(5)[:L], ALU.mult, [dts], [dts])
                    tt("dve", xdte.ap[:L].rearrange("p (h c) -> p h c", h=32), x3, dsl(1)[:L].unsqueeze(2).broadcast_to([L, 32, 64]), ALU.mult,
                       [xtok, dts], [xdte])
                else:
                    tt("dve", xdt.ap[:L].rearrange("p (h c) -> p h c", h=32), x3, dsl(2)[:L].unsqueeze(2).broadcast_to([L, 32, 64]), ALU.mult,
                       [xtok, dts], [xdt])
                    tt("dve", xdte.ap[:L].rearrange("p (h c) -> p h c", h=32), xdt.ap[:L].rearrange("p (h c) -> p h c", h=32),
                       dsl(5)[:L].unsqueeze(2).broadcast_to([L, 32, 64]), ALU.mult, [xdt, dts], [xdte])
                g3 = lambda ap: ap.rearrange("p (h c) -> p h c", h=8)
                S.touch("dve", [jk])
                S.touch("act", [jk])
                S.touch("pe", [jk])

                def ssd_stage1(gi):
                    BT = xbc3[:, 16 + gi, :L]
                    CT = xbc3[:, 20 + gi, :L]
                    cbm = CBm2[gi % 2]
                    mt = MT2[gi % 2]
                    ps = PS("aux")
                    mm(ps.ap[:L, :L], BT, CT, True, True, [xbcT], [ps])
                    tt("dve", cbm.ap[:L, :L], ps.ap[:L, :L], triM[:L, :L], ALU.mult, [ps, cst], [cbm])
                    for half in range(2):
                        pe_ = PS("mm")
                        for j in range(4):
                            h = gi * 8 + half * 4 + j
                            mm(pe_.ap[:L, j * 128:j * 128 + L], identf[0:32, h:h + 1].broadcast_to([32, L]), aT.ap[:32, :L], True, True, [cst, aT], [pe_])
                        h0 = gi * 8 + half * 4
                        tt("dve", v3(Tt.ap[:L, half * 512:(half + 1) * 512], a=4)[:, :, :L], v3(pe_.ap[:L, :], a=4)[:, :, :L],
                           dsl(4)[:L, h0:h0 + 4].unsqueeze(2).broadcast_to([L, 4, L]), ALU.subtract, [pe_, dts], [Tt])
                    T3 = v3(Tt.ap, a=8)[:L, :, :L]
                    M3 = v3(mt.ap, a=8)[:L, :, :L]
                    tt("dve", T3, T3, triM[:L, :L].unsqueeze(1).broadcast_to([L, 8, L]), ALU.mult, [Tt, cst], [Tt])
                    act(T3, T3, AF.Exp, [Tt], [Tt])
                    tt("dve", M3, T3, cbm.ap[:L, :L].unsqueeze(1).broadcast_to([L, 8, L]), ALU.mult, [Tt, cbm], [mt])

                def ssd_stage2(gi):
                    CT = xbc3[:, 20 + gi, :L]
                    mt = MT2[gi % 2]
                    if not pre:
                        psY = PS("mm")
                        for j in range(8):
                            h = gi * 8 + j
                            mm(psY.ap[:L, j * 64:(j + 1) * 64], v3(mt.ap, a=8)[:L, j, :L], xdt.ap[:L, h * 64:(h + 1) * 64], True, True, [mt, xdt], [psY])
                        psO = PS("mm")
                        mm(psO.ap[:L, :], CT, Hb.ap[:, gi * 512:(gi + 1) * 512], True, True, [xbcT, Hb], [psO])
                        tt("dve", g3(tmpg.ap[:L]), g3(psO.ap[:L, :]), dsl(6)[:L, gi * 8:(gi + 1) * 8].unsqueeze(2).broadcast_to([L, 8, 64]), ALU.mult,
                           [psO, dts], [tmpg])
                        yg = yacc.ap[:L, gi * 512:(gi + 1) * 512]
                        tt("dve", yg, tmpg.ap[:L], psY.ap[:L, :], ALU.add, [tmpg, psY], [yacc])
                        tt("dve", g3(tmpg.ap[:L]), g3(xtok.ap[:L, gi * 512:(gi + 1) * 512]), Drep[:L, gi * 8:(gi + 1) * 8].unsqueeze(2).broadcast_to([L, 8, 64]),
                           ALU.mult, [xtok, prm], [tmpg])
                        tt("dve", yg, yg, tmpg.ap[:L], ALU.add, [tmpg, yacc], [yacc])
                    psS = PS("mm")
                    mm(psS.ap[:, :], Btok.ap[:L, gi * 128:(gi + 1) * 128], xdte.ap[:L, gi * 512:(gi + 1) * 512], True, True, [Btok, xdte], [psS])
                    Hg = Hf.ap[:, gi * 512:(gi + 1) * 512]
                    tt("dve", g3(Hg), g3(Hg), dsl(7)[:, gi * 8:(gi + 1) * 8].unsqueeze(2).broadcast_to([128, 8, 64]), ALU.mult,
                       [Hf, dts] + ([] if pre else [psO]), [Hf])
                    tt("dve", Hg, Hg, psS.ap[:, :], ALU.add, [Hf, psS], [Hf])
                    if not pre:
                        cp("pool", Hb.ap[:, gi * 512:(gi + 1) * 512], Hg, [Hf], [Hb])

                if pre:
                    for gi in range(4):
                        ssd_stage2(gi)
                else:
                    ssd_stage1(0)
                    for gi in range(4):
                        if gi + 1 < 4:
                            ssd_stage1(gi + 1)
                        ssd_stage2(gi)
                if pre:
                    return
                for c in range(4):
                    ps = PS("mm")
                    for kc in range(8):
                        mm(ps.ap[:L, :], hT23[:, kc, :L], Ws3[:, kc, c * 512:(c + 1) * 512], kc == 0, kc == 7, [hT2, Wss], [ps])
                    act(sz_ap[:L, c * 512:(c + 1) * 512], ps.ap[:L, :], AF.Silu, [ps], szT)
                tt("dve", yacc.ap[:L], yacc.ap[:L], sz_ap[:L], ALU.mult, [yacc] + szT, [yacc])
                tt("dve", jk.ap[:L], yacc.ap[:L], yacc.ap[:L], ALU.mult, [yacc], [jk] + ssd_t)
                S.op("dve", lambda e: e.tensor_reduce(out=sm2.ap[:L, 4:8], in_=v3(jk.ap, a=4)[:L], axis=AX, op=ALU.add), reads=[jk], writes=[sm2])
                act(sm2.ap[:L, 4:8], sm2.ap[:L, 4:8], AF.Ln, [sm2, prm], [sm2], bias=eps5[:L], scale=1.0 / 512)
                act(sm2.ap[:L, 4:8], sm2.ap[:L, 4:8], AF.Exp, [sm2], [sm2], scale=-0.5)
                tt("dve", v3(yb.ap, a=4)[:L], v3(yacc.ap, a=4)[:L], sm2.ap[:L, 4:8].unsqueeze(2).broadcast_to([L, 4, 512]), ALU.mult, [yacc, sm2], [yb])
                for half in range(2):
                    pb = PS("aux")
                    pv_ = bf_bank(pb)
                    for j in range(8):
                        slab = half * 8 + j
                        tr(pv_[:, j * 128:j * 128 + L], yb.ap[:L, slab * 128:(slab + 1) * 128], identb[:L, :L], [yb, cb], [pb])
                    tt("dve", ynT3[:, half * 8:half * 8 + 8, :L], v3(pv_, a=8)[:, :, :L], gnT[:, half * 8:half * 8 + 8].unsqueeze(2).broadcast_to([128, 8, L]),
                       ALU.mult, [pb, prm], [ynT])
                yo = yout[bi % 2]
                for ncol in range(2):
                    ps = PS("mm")
                    for slab in range(16):
                        mm(ps.ap[:L, :], ynT3[:, slab, :L], wos3[:, slab, ncol * 512:(ncol + 1) * 512], slab == 0, slab == 15, [ynT, wos], [ps])
                    tt("dve", yo.ap[:L, ncol * 512:(ncol + 1) * 512], ps.ap[:L, :], x1.ap[:L, ncol * 512:(ncol + 1) * 512], ALU.add, [ps, x1], [yo])
                dma(ydst, yo.ap[:L], reads=[yo])

            nblk = 16 if STAGE >= 5 else 2
            def run(gen):
                for _ in gen:
                    pass

            gens = [l1_block(blk * 128, 128, None, blk, pre=True, allproj=(blk == nblk - 1), save=blk, par=blk % 2) for blk in range(nblk)]
            next(gens[0])
            for blk in range(nblk):
                if blk + 1 < nblk:
                    next(gens[blk + 1])
                run(gens[blk])
            for sl_ in range(0, 16, 4):
                pdma(wos3[:, sl_:sl_ + 4, :], s_wout[sl_ * 128:(sl_ + 4) * 128, :].rearrange("(h p) c -> p h c", p=128), writes=[wos, xbcT_b, dts_b])
            ccin_t = Tile(None)
            dma(cc_in[:, 0:2048], Hf.ap, reads=[Hf], writes=[ccin_t])
            dma(cc_in[:, 2048:2120].rearrange("p (s j) -> p s j", s=24), xpre3[:, :, 0:3], reads=[xpre, ccin_t], writes=[ccin_t])
            cct = Tile(None)
            S.op("pool", lambda e: e.collective_compute("AllGather", ALU.bypass, replica_groups=[[0, 1], [2, 3], [4, 5], [6, 7]],
                                                         ins=[cc_in], outs=[cc_out]), reads=[ccin_t], writes=[cct])
            dma(Hf.ap, cc_out[0:128, 0:2048], reads=[cct], writes=[Hf])
            dma(xpre3[:, :, 0:3], cc_out[0:128, 2048:2120].rearrange("p (s j) -> p s j", s=24), reads=[cct], writes=[xpre])
            ts("dve", Hf.ap, Hf.ap, flagt.ap[:, 0:1], ALU.mult, [Hf, flagt], [Hf])
            ts("dve", xpre3[:, :, 0:3], xpre3[:, :, 0:3], flagt.ap[:, 0:1], ALU.mult, [xpre, flagt], [xpre])
            cp("pool", Hb.ap, Hf.ap, [Hf], [Hb])
            for blk in range(nblk):
                run(l1_block(blk * 128, 128, y_p[blk * 128:(blk + 1) * 128, :], blk, reuse=(blk if blk >= 1 else None), allproj=(blk == nblk - 1)))
            for j in range(3):
                nc_dma(nconv_p[j].rearrange("(s p) -> p s", p=128), xpre3[:, :, j], reads=[xpre])
            state_out(nssm_p)
            for b in range(4):
                for j in range(3):
                    nc_dma(xpre3[:, :, j], sconv[b, j].rearrange("(s p) -> p s", p=128), writes=[xpre])
                state_in(sssm[b])
                run(l1_block(2048 + b * 8, 8, y_s[b * 8:(b + 1) * 8, :], b))
                for j in range(3):
                    nc_dma(nconv_s[b, j].rearrange("(s p) -> p s", p=128), xpre3[:, :, 8 + j], reads=[xpre])
                state_out(nssm_s[b])

        S.barrier()
        blockc = es.enter_context(nc.Block())
        S.emit(blockc)
    print("sched: instr counts", {k: len(v) for k, v in S.prog.items()}, "waits", S.nwait, "arena", A.off)
    return nc, cpack, dbg


def prep_inputs(inp, core):
    b = core // 2
    half = core % 2
    m = {}
    if half == 1:
        m["xp"] = np.ascontiguousarray(inp["x_prompt"][b])
    else:
        m["xp"] = np.concatenate([np.zeros((2048, DM), np.float32), inp["x_prompt"][b][:2048]], 0)
    m["flag"] = np.full((128, 1), float(half), np.float32)
    m["xs"] = np.ascontiguousarray(inp["x_sample"][4 * core:4 * core + 4].reshape(32, 1024))
    for g, key in enumerate(["cache_kv_g0", "cache_kv_g1", "cache_kv_g2"]):
        m["kv%d" % g] = np.ascontiguousarray(inp[key][0, 4 * core:4 * core + 4].reshape(4, GROUPS[g][0], 2, 1024))
    m["a_normw"] = np.ascontiguousarray(inp["attn_norm"][0].reshape(8, 128).T)
    m["a_win"] = np.ascontiguousarray(inp["attn_w_in"][0])
    gq = inp["attn_q_gain"][0]
    gk = inp["attn_k_gain"][0]
    gc = np.zeros((128, 6), np.float32)
    for g in range(3):
        gc[:, 2 * g] = np.tile(gq[g], 2)
        gc[:, 2 * g + 1] = np.tile(gk[g], 2)
    m["a_gain"] = gc
    m["a_wout"] = np.ascontiguousarray(inp["attn_w_out"][0])
    m["sconv"] = np.ascontiguousarray(inp["state_conv"][0, 4 * core:4 * core + 4])
    m["sssm"] = np.ascontiguousarray(inp["state_ssm"][0, 4 * core:4 * core + 4].reshape(4, 2048, 128))
    m["s_normw"] = np.ascontiguousarray(inp["ssm_norm"][0].reshape(8, 128).T)
    m["s_win"] = np.ascontiguousarray(inp["ssm_w_in"][0])
    m["s_cw"] = np.ascontiguousarray(inp["ssm_conv_w"][0].reshape(4, 24, 128).transpose(2, 0, 1).reshape(128, 96))
    m["s_cb"] = np.ascontiguousarray(inp["ssm_conv_b"][0].reshape(24, 128).T)
    m["s_dtb"] = np.ascontiguousarray(np.broadcast_to(inp["ssm_dt_bias"][0][None, :], (128, 32)))
    m["s_alog"] = np.ascontiguousarray(np.broadcast_to(inp["ssm_A_log"][0][None, :], (128, 32)))
    m["s_D"] = np.ascontiguousarray(np.broadcast_to(inp["ssm_D"][0][None, :], (128, 32)))
    m["s_gn"] = np.ascontiguousarray(inp["ssm_gate_norm"][0].reshape(16, 128).T)
    m["s_wout"] = np.ascontiguousarray(inp["ssm_w_out"][0])
    return m


_CACHE = {}


def kernel(**inputs):
    inp = {k: np.asarray(v) for k, v in inputs.items()}
    if "nc" not in _CACHE:
        _CACHE["nc"] = build()
    nc, cpack, dbg = _CACHE["nc"]
    in_maps = []
    ncores = int(os.environ.get('K_CORES', '8'))
    for c in range(ncores):
        m = prep_inputs(inp, c)
        m["consts"] = cpack
        in_maps.append(m)
    res = run_bass_kernel_spmd(nc, in_maps, core_ids=list(range(ncores)))
    R = res.results
    if ncores < 8 or STAGE < 5:
        return R
    f = np.float32
    y_prompt = np.stack([np.concatenate([np.asarray(R[2 * b]["y_p"], f), np.asarray(R[2 * b + 1]["y_p"], f)], 0) for b in range(4)])
    y_sample = np.concatenate([np.asarray(R[c]["y_s"], f).reshape(4, 8, DM) for c in range(8)], 0)
    outs = [y_prompt, y_sample]
    for g in range(3):
        W = GROUPS[g][0]
        outs.append(np.stack([np.asarray(R[2 * b + 1]["nkv%d_p" % g], f).reshape(W, 2, 16, 64) for b in range(4)])[None])
    for g in range(3):
        outs.append(np.concatenate([np.asarray(R[c]["nkv%d_s" % g], f).reshape(4, 8, 2, 16, 64) for c in range(8)], 0)[None])
    outs.append(np.stack([np.asarray(R[2 * b + 1]["nconv_p"], f) for b in range(4)])[None])
    outs.append(np.concatenate([np.asarray(R[c]["nconv_s"], f) for c in range(8)], 0)[None])
    outs.append(np.stack([np.asarray(R[2 * b + 1]["nssm_p"], f).reshape(32, 64, 128) for b in range(4)])[None])
    outs.append(np.concatenate([np.asarray(R[c]["nssm_s"], f).reshape(4, 32, 64, 128) for c in range(8)], 0)[None])
    return tuple(outs)
```
